# Optimizing a Trainium2 kernel written in Bass

```python
import math
import jax, jax.numpy as jnp
from jax import lax
import numpy as np

D_MODEL = 1024
BATCH = 8
SEQ = 8192
DEPTH = 2

CTX_LEN = 256
GRID_W = 64
D_MIX = D_MODEL
N_GROUPS = 4
GROUP_W = D_MIX // N_GROUPS
D_FF = 2816
N_MOD = 9
EPS = 1e-6
ROPE_THETA = 10000.0
Q_BLOCK = 128
DA_HEADS = 4
DA_QK = GROUP_W // (2 * DA_HEADS)
DA_V = GROUP_W // DA_HEADS
POOL_WINDOWS = (2, 4, 8, 16)
POOL_GROUP = GROUP_W // len(POOL_WINDOWS)
ML_HEADS = 4
ML_DIM = GROUP_W // ML_HEADS
ML_CHUNK = 64
GQA_HEADS = 4
GQA_KV_HEADS = 2
GQA_DIM = GROUP_W // GQA_HEADS
IN_SPLITS = (GROUP_W, GROUP_W, GROUP_W,
             GROUP_W,
             GROUP_W, GROUP_W, GROUP_W, GROUP_W, 4 * ML_HEADS,
             GROUP_W, GQA_KV_HEADS * GQA_DIM, GQA_KV_HEADS * GQA_DIM)
IN_WIDTH = sum(IN_SPLITS)

kernel_name = 'hybrid_parallel_group_dit_block'


def layer_norm(x, g=None, b=None):
    xf = x.astype(jnp.float32)
    mu = xf.mean(-1, keepdims=True)
    var = jnp.square(xf - mu).mean(-1, keepdims=True)
    y = (xf - mu) * lax.rsqrt(var + EPS)
    if g is not None:
        y = y * g.astype(jnp.float32) + b.astype(jnp.float32)
    return y.astype(x.dtype)


def rms_norm(x, g):
    xf = x.astype(jnp.float32)
    y = xf * lax.rsqrt(jnp.mean(jnp.square(xf), -1, keepdims=True) + EPS) * g.astype(jnp.float32)
    return y.astype(x.dtype)


def rope_tables(row, col, dim):
    axis_dim = dim // 2
    inv = ROPE_THETA ** (-jnp.arange(0, axis_dim, 2, dtype=jnp.float32) / axis_dim)
    ang = jnp.concatenate([row.astype(jnp.float32)[:, None] * inv,
                           col.astype(jnp.float32)[:, None] * inv], axis=-1)
    return jnp.cos(ang), jnp.sin(ang)


def apply_rope(x, cos, sin):
    half = x.shape[-1] // 2
    shape = (1, cos.shape[0]) + (1,) * (x.ndim - 3) + (half,)
    cos = cos.reshape(shape).astype(x.dtype)
    sin = sin.reshape(shape).astype(x.dtype)
    x1, x2 = x[..., :half], x[..., half:]
    return jnp.concatenate([x1 * cos - x2 * sin, x2 * cos + x1 * sin], axis=-1)


def sweep_query_blocks(fn, q):
    B, L = q.shape[0], q.shape[1]
    nb = L // Q_BLOCK
    qb = jnp.moveaxis(q.reshape((B, nb, Q_BLOCK) + q.shape[2:]), 1, 0)
    out = jnp.moveaxis(lax.map(fn, qb), 0, 1)
    return out.reshape((B, L) + out.shape[3:])


def diff_attention_core(q, k, v, lam):
    s = jnp.einsum('bqhmd,bkhmd->bhmqk', q, k).astype(jnp.float32) * DA_QK ** -0.5
    p = jax.nn.softmax(s, axis=-1)
    a = p[:, :, 0] - lam * p[:, :, 1]
    return jnp.einsum('bhqk,bkhe->bqhe', a.astype(v.dtype), v)


def gqa_core(q, k, v):
    s = jnp.einsum('bqhgd,bkhd->bhgqk', q, k).astype(jnp.float32) * GQA_DIM ** -0.5
    p = jax.nn.softmax(s, axis=-1)
    return jnp.einsum('bhgqk,bkhd->bqhgd', p.astype(v.dtype), v)


def multiscale_pool(u, pool_w, pool_scale):
    B, L, _ = u.shape
    uf = u.astype(jnp.float32)
    csum = jnp.concatenate([jnp.zeros((B, 1, GROUP_W), jnp.float32), jnp.cumsum(uf, axis=1)], axis=1)
    t = jnp.arange(L)
    outs = []
    for g, w in enumerate(POOL_WINDOWS):
        lo = jnp.clip(t - w // 2, 0, L)
        hi = jnp.clip(t + w // 2, 0, L)
        sl = slice(g * POOL_GROUP, (g + 1) * POOL_GROUP)
        mean = (csum[:, hi, sl] - csum[:, lo, sl]) / (hi - lo).astype(jnp.float32)[None, :, None]
        outs.append(mean - uf[:, :, sl])
    d = jnp.stack(outs, axis=2)
    y = jnp.einsum('blgc,gce->blge', d, pool_w.astype(jnp.float32)).reshape(B, L, GROUP_W)
    return (y * pool_scale.astype(jnp.float32)).astype(u.dtype)


def mlstm_gates(cg, gate_b):
    B, T, _ = cg.shape
    g = (cg.astype(jnp.float32) + gate_b.astype(jnp.float32).reshape(-1)).reshape(B, T, 4, ML_HEADS)
    g = jnp.transpose(g, (2, 0, 3, 1))
    return (g[0], jax.nn.log_sigmoid(g[1]), g[2], jax.nn.log_sigmoid(g[3]))


def mlstm_zero_state(B):
    return (jnp.zeros((B, ML_HEADS, ML_DIM, ML_DIM), jnp.float32),
            jnp.zeros((B, ML_HEADS, ML_DIM), jnp.float32),
            jnp.zeros((B, ML_HEADS), jnp.float32))


def mlstm_state_update(k, v, li, lf, state):
    C, n, m = state
    b = jnp.cumsum(lf, axis=-1)
    b_end = b[..., -1]
    g = b_end[..., None] - b + li
    m_new = jnp.maximum(b_end + m, g.max(-1))
    wk = jnp.exp(g - m_new[..., None])
    decay = jnp.exp(b_end + m - m_new)
    kf, vf = k.astype(jnp.float32), v.astype(jnp.float32)
    C_new = decay[..., None, None] * C + jnp.einsum('bht,bthv,bthe->bhve', wk, vf, kf)
    n_new = decay[..., None] * n + jnp.einsum('bht,bthe->bhe', wk, kf)
    return (C_new, n_new, m_new)


def mlstm_chunk_output(q, k, v, li, lf, state):
    C, n, m = state
    T = q.shape[1]
    b = jnp.cumsum(lf, axis=-1)
    dmat = b[..., :, None] - b[..., None, :] + li[..., None, :]
    dmat = jnp.where(jnp.tril(jnp.ones((T, T), dtype=bool)), dmat, -jnp.inf)
    inter = b + m[..., None]
    m_t = jnp.maximum(inter, dmat.max(axis=-1))
    w_intra = jnp.exp(dmat - m_t[..., None])
    w_inter = jnp.exp(inter - m_t)
    qf = q.astype(jnp.float32) * ML_DIM ** -0.5
    kf, vf = k.astype(jnp.float32), v.astype(jnp.float32)
    a = jnp.einsum('bjhe,bkhe->bhjk', qf, kf) * w_intra
    num = jnp.einsum('bhjk,bkhv->bjhv', a, vf) + jnp.einsum('bhj,bhve,bjhe->bjhv', w_inter, C, qf)
    den = a.sum(-1) + w_inter * jnp.einsum('bhe,bjhe->bhj', n, qf)
    den = jnp.maximum(jnp.abs(den), jnp.exp(-m_t))
    return num / jnp.swapaxes(den, 1, 2)[..., None]


def mlstm_scan(q, k, v, li, lf, state):
    B, T, H, d = q.shape
    nc = T // ML_CHUNK
    def tok_chunks(a):
        return jnp.moveaxis(a.reshape((B, nc, ML_CHUNK) + a.shape[2:]), 1, 0)
    def gate_chunks(a):
        return jnp.moveaxis(a.reshape(B, H, nc, ML_CHUNK), 2, 0)
    def body(st, xs):
        qc, kc, vc, lic, lfc = xs
        h = mlstm_chunk_output(qc, kc, vc, lic, lfc, st)
        return mlstm_state_update(kc, vc, lic, lfc, st), h
    st, hs = lax.scan(body, state, (tok_chunks(q), tok_chunks(k), tok_chunks(v), gate_chunks(li), gate_chunks(lf)))
    h = jnp.moveaxis(hs, 0, 1).reshape(B, T, H, d)
    return h.astype(q.dtype), st


def mlstm_bidirectional(q, k, v, gates, st_f, st_b):
    li_f, lf_f, li_b, lf_b = gates
    hf, end_f = mlstm_scan(q, k, v, li_f, lf_f, st_f)
    hb, end_b = mlstm_scan(q[:, ::-1], k[:, ::-1], v[:, ::-1], li_b[..., ::-1], lf_b[..., ::-1], st_b)
    return hf + hb[:, ::-1], end_f, end_b


def token_mixing(h_lat, h_ctx, layer, rope_a, rope_d, w_in, w_out, diff_lambda, diff_norm_g,
                 pool_w, pool_scale, ml_gate_b, ml_norm_g, q_norm_g, k_norm_g, ctx_out):
    B, L, _ = h_lat.shape
    Lc = h_ctx.shape[1]
    G = GQA_HEADS // GQA_KV_HEADS
    offsets = np.cumsum(IN_SPLITS)[:-1].tolist()
    aq, ak, av, bu, cq, ck, cv, co, cg, dq, dk, dv = jnp.split(h_lat @ w_in, offsets, axis=-1)
    aq_c, ak_c, av_c, bu_c, cq_c, ck_c, cv_c, co_c, cg_c, dq_c, dk_c, dv_c = jnp.split(h_ctx @ w_in, offsets, axis=-1)

    lam_init = 0.8 - 0.6 * math.exp(-0.3 * layer)
    dl = diff_lambda.astype(jnp.float32)
    lam = jnp.exp(jnp.sum(dl[0] * dl[1])) - jnp.exp(jnp.sum(dl[2] * dl[3])) + lam_init
    def diff_heads(t, n):
        return t.reshape(B, n, DA_HEADS, 2, DA_QK)
    def diff_finish(o):
        n = o.shape[1]
        return (rms_norm(o, diff_norm_g.reshape(DA_HEADS, DA_V)) * (1.0 - lam_init)).reshape(B, n, GROUP_W)
    k_a_c = diff_heads(ak_c, Lc)
    v_a_c = av_c.reshape(B, Lc, DA_HEADS, DA_V)
    k_a = jnp.concatenate([k_a_c, apply_rope(diff_heads(ak, L), *rope_a)], axis=1)
    v_a = jnp.concatenate([v_a_c, av.reshape(B, L, DA_HEADS, DA_V)], axis=1)
    q_a = apply_rope(diff_heads(aq, L), *rope_a)
    a_lat = diff_finish(sweep_query_blocks(lambda qb: diff_attention_core(qb, k_a, v_a, lam), q_a))

    b_lat = multiscale_pool(bu, pool_w, pool_scale)

    def ml_heads(t, n):
        return t.reshape(B, n, ML_HEADS, ML_DIM)
    def ml_finish(h, o):
        n = h.shape[1]
        return rms_norm(h, ml_norm_g.reshape(ML_HEADS, ML_DIM)).reshape(B, n, GROUP_W) * jax.nn.sigmoid(o)
    zero = mlstm_zero_state(B)
    gates_c = mlstm_gates(cg_c, ml_gate_b)
    q_m_c, k_m_c, v_m_c = ml_heads(cq_c, Lc), ml_heads(ck_c, Lc), ml_heads(cv_c, Lc)
    if ctx_out:
        h_m_c, st_f, st_b = mlstm_bidirectional(q_m_c, k_m_c, v_m_c, gates_c, zero, zero)
    else:
        st_f = mlstm_state_update(k_m_c, v_m_c, gates_c[0], gates_c[1], zero)
        st_b = mlstm_state_update(k_m_c[:, ::-1], v_m_c[:, ::-1], gates_c[2][..., ::-1], gates_c[3][..., ::-1], zero)
    h_m, _, _ = mlstm_bidirectional(ml_heads(cq, L), ml_heads(ck, L), ml_heads(cv, L),
                                    mlstm_gates(cg, ml_gate_b), st_f, st_b)
    c_lat = ml_finish(h_m, co)

    def gq_heads(t, n):
        return rms_norm(t.reshape(B, n, GQA_KV_HEADS, G, GQA_DIM), q_norm_g)
    def gk_heads(t, n):
        return rms_norm(t.reshape(B, n, GQA_KV_HEADS, GQA_DIM), k_norm_g)
    k_d_c = gk_heads(dk_c, Lc)
    v_d_c = dv_c.reshape(B, Lc, GQA_KV_HEADS, GQA_DIM)
    k_d = jnp.concatenate([k_d_c, apply_rope(gk_heads(dk, L), *rope_d)], axis=1)
    v_d = jnp.concatenate([v_d_c, dv.reshape(B, L, GQA_KV_HEADS, GQA_DIM)], axis=1)
    q_d = apply_rope(gq_heads(dq, L), *rope_d)
    d_lat = sweep_query_blocks(lambda qb: gqa_core(qb, k_d, v_d), q_d).reshape(B, L, GROUP_W)

    out_lat = jnp.concatenate([a_lat, b_lat, c_lat, d_lat], axis=-1) @ w_out
    if not ctx_out:
        return out_lat, None
    a_c = diff_finish(diff_attention_core(diff_heads(aq_c, Lc), k_a_c, v_a_c, lam))
    b_c = multiscale_pool(bu_c, pool_w, pool_scale)
    c_c = ml_finish(h_m_c, co_c)
    d_c = gqa_core(gq_heads(dq_c, Lc), k_d_c, v_d_c).reshape(B, Lc, GROUP_W)
    out_ctx = jnp.concatenate([a_c, b_c, c_c, d_c], axis=-1) @ w_out
    return out_lat, out_ctx


def modulate(x, mod, s):
    return layer_norm(x) * (1.0 + mod[:, 3 * s + 1]) + mod[:, 3 * s]


def swiglu(h, wi, wo):
    gate, up = jnp.split(h @ wi, 2, axis=-1)
    return (jax.nn.silu(gate) * up) @ wo


def macaron_ffn(x, mod, s, wi, wo, g, b, alpha):
    h = modulate(x, mod, s)
    return layer_norm(alpha * x + 0.5 * mod[:, 3 * s + 2] * swiglu(h, wi, wo), g, b)


def setup_inputs(seed: int = 0) -> dict:
    key = jax.random.key(seed)
    ks = jax.random.split(key, 22)
    def nrm(k, shape, s):
        return jax.random.normal(k, shape, jnp.float32) * s
    beta = (8.0 * DEPTH) ** -0.25
    gate_offset = jnp.array([0.0, 3.0, 0.0, 3.0], jnp.float32)[None, :, None]
    return {
        'x': nrm(ks[0], (BATCH, SEQ, D_MODEL), 1.0),
        'c': nrm(ks[1], (BATCH, D_MODEL), 1.0),
        'ctx': nrm(ks[2], (BATCH, CTX_LEN, D_MODEL), 1.0),
        'c_ctx': nrm(ks[3], (D_MODEL,), 1.0),
        'w_ada': nrm(ks[4], (DEPTH, D_MODEL, N_MOD * D_MODEL), 0.5 * D_MODEL ** -0.5),
        'b_ada': nrm(ks[5], (DEPTH, N_MOD * D_MODEL), 0.02),
        'ln_g': 1.0 + nrm(ks[6], (DEPTH, 3, D_MODEL), 0.02),
        'ln_b': nrm(ks[7], (DEPTH, 3, D_MODEL), 0.02),
        'ffn1_wi': nrm(ks[8], (DEPTH, D_MODEL, 2 * D_FF), D_MODEL ** -0.5),
        'ffn1_wo': nrm(ks[9], (DEPTH, D_FF, D_MODEL), beta * D_FF ** -0.5),
        'ffn2_wi': nrm(ks[10], (DEPTH, D_MODEL, 2 * D_FF), D_MODEL ** -0.5),
        'ffn2_wo': nrm(ks[11], (DEPTH, D_FF, D_MODEL), beta * D_FF ** -0.5),
        'w_in': nrm(ks[12], (DEPTH, D_MODEL, IN_WIDTH), D_MODEL ** -0.5),
        'w_out': nrm(ks[13], (DEPTH, D_MIX, D_MODEL), beta * D_MIX ** -0.5),
        'diff_lambda': nrm(ks[14], (DEPTH, 4, DA_QK), 0.1),
        'diff_norm_g': 1.0 + nrm(ks[15], (DEPTH, GROUP_W), 0.02),
        'pool_w': nrm(ks[16], (DEPTH, len(POOL_WINDOWS), POOL_GROUP, POOL_GROUP), POOL_GROUP ** -0.5),
        'pool_scale': 1.0 + nrm(ks[17], (DEPTH, GROUP_W), 0.1),
        'ml_gate_b': gate_offset + nrm(ks[18], (DEPTH, 4, ML_HEADS), 0.1),
        'ml_norm_g': 1.0 + nrm(ks[19], (DEPTH, GROUP_W), 0.02),
        'gqa_qnorm_g': 1.0 + nrm(ks[20], (DEPTH, GQA_DIM), 0.02),
        'gqa_knorm_g': 1.0 + nrm(ks[21], (DEPTH, GQA_DIM), 0.02),
    }


def reference(x, c, ctx, c_ctx, w_ada, b_ada, ln_g, ln_b, ffn1_wi, ffn1_wo, ffn2_wi, ffn2_wo,
              w_in, w_out, diff_lambda, diff_norm_g, pool_w, pool_scale, ml_gate_b, ml_norm_g,
              gqa_qnorm_g, gqa_knorm_g):
    B, L, _ = x.shape
    rows = L // GRID_W
    row = jnp.repeat(jnp.arange(rows), GRID_W)
    col = jnp.tile(jnp.arange(GRID_W), rows)
    rope_a = rope_tables(row, col, DA_QK)
    rope_d = rope_tables(row, col, GQA_DIM)
    alpha = (2.0 * DEPTH) ** 0.25
    x_ctx = ctx
    for l in range(DEPTH):
        last = l == DEPTH - 1
        mod_l = (jax.nn.silu(c) @ w_ada[l] + b_ada[l]).reshape(B, N_MOD, 1, D_MODEL)
        mod_c = (jax.nn.silu(c_ctx)[None] @ w_ada[l] + b_ada[l]).reshape(1, N_MOD, 1, D_MODEL)
        x = macaron_ffn(x, mod_l, 0, ffn1_wi[l], ffn1_wo[l], ln_g[l, 0], ln_b[l, 0], alpha)
        x_ctx = macaron_ffn(x_ctx, mod_c, 0, ffn1_wi[l], ffn1_wo[l], ln_g[l, 0], ln_b[l, 0], alpha)
        o_lat, o_ctx = token_mixing(modulate(x, mod_l, 1), modulate(x_ctx, mod_c, 1), l, rope_a, rope_d,
                                    w_in[l], w_out[l], diff_lambda[l], diff_norm_g[l], pool_w[l],
                                    pool_scale[l], ml_gate_b[l], ml_norm_g[l], gqa_qnorm_g[l],
                                    gqa_knorm_g[l], not last)
        x = layer_norm(alpha * x + mod_l[:, 5] * o_lat, ln_g[l, 1], ln_b[l, 1])
        if not last:
            x_ctx = layer_norm(alpha * x_ctx + mod_c[:, 5] * o_ctx, ln_g[l, 1], ln_b[l, 1])
            x_ctx = macaron_ffn(x_ctx, mod_c, 2, ffn2_wi[l], ffn2_wo[l], ln_g[l, 2], ln_b[l, 2], alpha)
        x = macaron_ffn(x, mod_l, 2, ffn2_wi[l], ffn2_wo[l], ln_g[l, 2], ln_b[l, 2], alpha)
    return x
```

```python
import numpy as np
import ml_dtypes
from contextlib import ExitStack
import concourse.bass as bass
import concourse.mybir as mybir
from concourse.bass_utils import run_bass_kernel_spmd

F32 = mybir.dt.float32
BF16 = mybir.dt.bfloat16
AF = mybir.ActivationFunctionType
ALU = mybir.AluOpType
AX = mybir.AxisListType

D = 1024
DFF = 2816
NFC = DFF // 128
NMOD = 9
EPS = 1e-6
DEPTH = 2
ALPHA = (2.0 * DEPTH) ** 0.25
INW = 2576
PIPE_FFN = False


class Sem:
    def __init__(self, h):
        self.h = h
        self.count = 0


class Buf:
    def __init__(self, t, name):
        self.t = t
        self.name = name
        self.w = {}
        self.r = {}
        self.dsem = None
        self.psum = False

    def __getitem__(self, k):
        return self.t[k]


class KB:
    def __init__(self, nc):
        self.nc = nc
        self.es = ExitStack()
        self.eng = {}
        for name, h in [("pe", nc.tensor), ("act", nc.scalar), ("dve", nc.vector),
                        ("pool", nc.gpsimd), ("sp", nc.sync)]:
            sem = Sem(self.es.enter_context(nc.semaphore("s_" + name)))
            self.eng[name] = (h, sem)
        self.known = {name: {} for name in self.eng}
        self.bar = Sem(self.es.enter_context(nc.semaphore("s_bar")))
        self.dfree = [Sem(self.es.enter_context(nc.semaphore("s_d%d" % i))) for i in range(72)]
        self.dall = list(self.dfree)
        self.dused = []
        self.n_ins = 0
        self.uid = 0

    def sb(self, ctx, name, shape, dt):
        self.uid += 1
        name = "%s_%d" % (name, self.uid)
        t = ctx.enter_context(self.nc.sbuf_tensor(name, list(shape), dt))
        return Buf(t, name)

    def ps(self, ctx, name, shape, dt):
        self.uid += 1
        name = "%s_%d" % (name, self.uid)
        t = ctx.enter_context(self.nc.psum_tensor(name, list(shape), dt))
        b = Buf(t, name)
        b.psum = True
        return b

    def _deps(self, reads, writes):
        d = {}
        for b in reads:
            for sm, v in b.w.items():
                if d.get(sm, 0) < v:
                    d[sm] = v
        for b in writes:
            for sm, v in b.w.items():
                if d.get(sm, 0) < v:
                    d[sm] = v
            for sm, v in b.r.items():
                if d.get(sm, 0) < v:
                    d[sm] = v
        return d

    def _wait(self, ename, d):
        h, _ = self.eng[ename]
        kn = self.known[ename]
        for sm, v in d.items():
            if kn.get(sm, 0) >= v:
                continue
            assert v <= sm.count, "dependency on a not-yet-issued increment"
            h.wait_ge(sm.h, v)
            kn[sm] = v
            self.n_ins += 1

    def op(self, ename, fn, reads=(), writes=(), inc=True):
        h, sem = self.eng[ename]
        d = self._deps(reads, writes)
        for b in reads:
            if b.psum:
                for sm, v in b.r.items():
                    if sm is not sem and d.get(sm, 0) < v:
                        d[sm] = v
        if ename == "pe":
            d.pop(sem, None)
        self._wait(ename, d)
        ins = fn(h)
        self.n_ins += 1
        if inc:
            ins.then_inc(sem.h, 1)
            sem.count += 1
            tick = sem.count
        else:
            tick = sem.count + 1
        for b in writes:
            b.w = {sem: tick}
            b.r = {}
        for b in reads:
            if b.r.get(sem, 0) < tick:
                b.r[sem] = tick
        return ins

    def dma(self, q, out, in_, sb, load, **kw):
        h, _ = self.eng[q]
        if sb.dsem is None:
            sb.dsem = self.dfree.pop()
            self.dused.append(sb)
        ds = sb.dsem
        if load:
            d = self._deps((), (sb,))
            if not sb.r and set(sb.w.keys()) == {ds}:
                d.pop(ds, None)
        else:
            d = self._deps((sb,), ())
        self._wait(q, d)
        ins = h.dma_start(out=out, in_=in_, **kw)
        self.n_ins += 1
        ins.then_inc(ds.h, 16)
        ds.count += 16
        tick = ds.count
        if load:
            sb.w = {ds: tick}
            sb.r = {}
        else:
            sb.r[ds] = tick
        return ins

    def barrier(self):
        h, _ = self.eng["sp"]
        allsems = [s for (_, s) in self.eng.values()] + self.dall
        kn = self.known["sp"]
        for sm in allsems:
            if kn.get(sm, 0) < sm.count:
                h.wait_ge(sm.h, sm.count)
                kn[sm] = sm.count
        h.sem_inc(self.bar.h, 1)
        self.bar.count += 1
        for name, (eh, _) in self.eng.items():
            if name != "sp":
                eh.wait_ge(self.bar.h, self.bar.count)
            self.known[name] = {sm: sm.count for sm in allsems}
        for b in self.dused:
            self.dfree.append(b.dsem)
            b.dsem = None
        self.dused = []


def _bc(ap, n=128):
    return ap.partition_broadcast(n)


class Prog:
    def __init__(self, L, Lc, NL, dbg=()):
        self.L, self.Lc, self.NL = L, Lc, NL
        self.LT = L + Lc
        self.dbg = set(dbg)
        nc = bass.Bass("TRN2", target_bir_lowering=False)
        self.nc = nc
        self.kb = KB(nc)
        LT = self.LT

        def din(name, shape, dt=F32):
            return nc.dram_tensor(name, list(shape), dt, kind="ExternalInput").ap()

        def dsc(name, shape, dt=F32):
            kind = "ExternalOutput" if name in self.dbg else "Internal"
            return nc.dram_tensor(name, list(shape), dt, kind=kind).ap()

        self.i = dict(
            x=din("x", [L, D]), ctx=din("ctx", [Lc, D]), cvec=din("cvec", [128, 2, 8]),
            w_ada=din("w_ada", [NL, D, NMOD * D]), b_ada=din("b_ada", [NL, NMOD * D]),
            ln_g=din("ln_g", [NL, 3, D]), ln_b=din("ln_b", [NL, 3, D]),
            ffn1_wi=din("ffn1_wi", [NL, D, 2 * DFF]), ffn1_wo=din("ffn1_wo", [NL, DFF, D]),
            ffn2_wi=din("ffn2_wi", [NL, D, 2 * DFF]), ffn2_wo=din("ffn2_wo", [NL, DFF, D]),
            ident=din("ident", [128, 128], BF16),
        )
        self.i.update(
            w_in=din("w_in", [NL, D, INW]), w_out=din("w_out", [NL, D, D]),
            diff_lambda=din("diff_lambda", [NL, 128]), diff_norm_g=din("diff_norm_g", [NL, 256]),
            pool_w=din("pool_w", [NL, 4, 64, 64]), pool_scale=din("pool_scale", [NL, 256]),
            ml_gate_b=din("ml_gate_b", [NL, 16]), ml_norm_g=din("ml_norm_g", [NL, 256]),
            gqa_qnorm_g=din("gqa_qnorm_g", [NL, 64]), gqa_knorm_g=din("gqa_knorm_g", [NL, 64]),
            ropeA_c=din("ropeA_c", [128, LT]), ropeA_s=din("ropeA_s", [128, LT]),
            ropeD_c=din("ropeD_c", [128, LT]), ropeD_s=din("ropeD_s", [128, LT]),
            blk64=din("blk64", [128, 128]), poolM=din("poolM", [128, 20, 128]),
            cmat=din("cmat", [128, 9, 128]), blkmask=din("blkmask", [128, 130]),
        )
        self.out = nc.dram_tensor("out", [L, D], F32, kind="ExternalOutput").ap()
        self.xs = dsc("xs", [LT, D])
        self.modd = dsc("modd", [NL, 2, NMOD * D])
        self.AqT = dsc("AqT", [256, LT], BF16); self.AkT = dsc("AkT", [256, LT], BF16)
        self.Av = dsc("Av", [LT, 4, 65], BF16)
        self.DqT = dsc("DqT", [256, LT], BF16); self.DkT = dsc("DkT", [128, LT], BF16)
        self.Dv = dsc("Dv", [LT, 2, 65], BF16)
        self.CqT = dsc("CqT", [256, LT], BF16); self.CkT = dsc("CkT", [256, LT], BF16)
        self.Cv = dsc("Cv", [LT, 4, 65], BF16)
        self.Ck = dsc("Ck", [LT, 256]); self.Co = dsc("Co", [LT, 256]); self.Cg = dsc("Cg", [LT, 16])
        self.Bu = dsc("Bu", [LT, 256])
        self.mix = dsc("mix", [LT, D], BF16)

    def phase_mod(self):
        kb, nc, I = self.kb, self.nc, self.i
        NL = self.NL
        CG = 3072
        with ExitStack() as cx:
            cv = kb.sb(cx, "cv", [128, 2, 8], F32)
            sv = kb.sb(cx, "sv", [128, 2, 8], F32)
            wbuf = [kb.sb(cx, "wada%d" % i, [128, CG], F32) for i in range(3)]
            bb = kb.sb(cx, "bada", [2, CG], F32)
            ob = [kb.sb(cx, "modo%d" % i, [2, CG], F32) for i in range(2)]
            pss = [kb.ps(cx, "modps%d" % i, [128, 512], F32) for i in range(6)]
            kb.dma("sp", cv[:], I["cvec"][:, :, :], cv, True)
            kb.op("act", lambda e: e.activation(out=sv[:], in_=cv[:], func=AF.Sigmoid), [cv], [sv])
            kb.op("dve", lambda e: e.tensor_tensor(out=sv[:], in0=sv[:], in1=cv[:], op=ALU.mult), [sv, cv], [sv])
            it = 0
            for l in range(NL):
                for cg in range(NMOD * D // CG):
                    for k in range(8):
                        wb = wbuf[it % 3]
                        it += 1
                        kb.dma("sp", wb[:], I["w_ada"][l, k * 128:(k + 1) * 128, cg * CG:(cg + 1) * CG], wb, True)
                        for j in range(6):
                            kb.op("pe", lambda e, j=j, k=k, wb=wb: e.matmul(
                                pss[j][0:2, :], sv[:, :, k], wb[:, j * 512:(j + 1) * 512],
                                start=(k == 0), stop=(k == 7)), [sv, wb], [pss[j]], inc=(j == 5))
                    o = ob[(l * 3 + cg) % 2]
                    kb.dma("sp", bb[:], _bc(I["b_ada"][l, cg * CG:(cg + 1) * CG], 2), bb, True)
                    for j in range(6):
                        kb.op("dve", lambda e, j=j, o=o: e.tensor_tensor(
                            out=o[:, j * 512:(j + 1) * 512], in0=pss[j][0:2, :], in1=bb[:, j * 512:(j + 1) * 512],
                            op=ALU.add), [pss[j], bb], [o])
                    kb.dma("pool", self.modd[l, :, cg * CG:(cg + 1) * CG], o[:], o, False)
        kb.barrier()

    def load_ffn_consts(self, cx, l, s, si):
        kb, I = self.kb, self.i
        c = {}
        c["sc1"] = kb.sb(cx, "sc1", [128, 2, 8], F32)
        c["sh"] = kb.sb(cx, "sh", [128, 2, 8], F32)
        for r in range(2):
            src1 = self.modd[l, r, (3 * s + 1) * D:(3 * s + 2) * D].rearrange("(k p) -> p k", p=128)
            src0 = self.modd[l, r, (3 * s) * D:(3 * s + 1) * D].rearrange("(k p) -> p k", p=128)
            kb.dma("sp", c["sc1"][:, r, :], src1, c["sc1"], True, allow_slow_non_contiguous=True)
            kb.dma("sp", c["sh"][:, r, :], src0, c["sh"], True, allow_slow_non_contiguous=True)
        kb.op("dve", lambda e: e.tensor_scalar_add(out=c["sc1"][:], in0=c["sc1"][:], scalar1=1.0), [c["sc1"]], [c["sc1"]])
        return c

    def ln_part(self, xt, xn, tmp):
        kb = self.kb
        st, mv, rstd = tmp["st"], tmp["mv"], tmp["rstd"]
        for hh in range(2):
            kb.op("dve", lambda e, hh=hh: e.bn_stats(out=st[:, hh, :], in_=xt[:, hh * 512:(hh + 1) * 512]), [xt], [st])
        kb.op("dve", lambda e: e.bn_aggr(out=mv[:], in_=st[:]), [st], [mv])
        kb.op("act", lambda e: e.activation(out=rstd[:], in_=mv[:, 1:2], func=AF.Sqrt, bias=tmp["epsc"][:, 0:1], scale=1.0),
              [mv, tmp["epsc"]], [rstd])
        kb.op("dve", lambda e: e.reciprocal(out=rstd[:], in_=rstd[:]), [rstd], [rstd])
        kb.op("dve", lambda e: e.tensor_scalar(out=xn[:], in0=xt[:], scalar1=mv[:, 0:1], scalar2=rstd[:, 0:1],
                                              op0=ALU.subtract, op1=ALU.mult), [xt, mv, rstd], [xn])

    def tr_part(self, xn, hT, col0, r, c, tp, ident):
        kb = self.kb
        for k in range(8):
            kb.op("pe", lambda e, k=k: e.transpose(tp[:, k, :], xn[:, k * 128:(k + 1) * 128], ident[:]),
                  [xn, ident], [tp], inc=(k == 7))
        for k in range(8):
            kb.op("act", lambda e, k=k: e.activation(out=hT[:, k, col0:col0 + 128], in_=tp[:, k, :], func=AF.Identity,
                                                    scale=c["sc1"][:, r, k:k + 1], bias=c["sh"][:, r, k:k + 1]),
                  [tp, c["sc1"], c["sh"]], [hT])

    def ln_to_hT(self, xt, hT, col0, r, c, tmp):
        self.ln_part(xt, tmp["xn"], tmp)
        self.tr_part(tmp["xn"], hT, col0, r, c, tmp["tp"], tmp["ident"])

    def ln_affine_out(self, z, dst, g, b, tmp):
        kb = self.kb
        st, mv, rstd = tmp["st2"], tmp["mv2"], tmp["rstd2"]
        nmr = tmp["nmr"]
        for hh in range(2):
            kb.op("dve", lambda e, hh=hh: e.bn_stats(out=st[:, hh, :], in_=z[:, hh * 512:(hh + 1) * 512]), [z], [st])
        kb.op("dve", lambda e: e.bn_aggr(out=mv[:], in_=st[:]), [st], [mv])
        kb.op("act", lambda e: e.activation(out=rstd[:], in_=mv[:, 1:2], func=AF.Sqrt, bias=tmp["epsc"][:, 0:1], scale=1.0),
              [mv, tmp["epsc"]], [rstd])
        kb.op("dve", lambda e: e.reciprocal(out=rstd[:], in_=rstd[:]), [rstd], [rstd])
        kb.op("dve", lambda e: e.scalar_tensor_tensor(out=nmr[:], in0=mv[:, 0:1], scalar=-1.0, in1=rstd[:], op0=ALU.mult, op1=ALU.mult),
              [mv, rstd], [nmr])
        kb.op("act", lambda e: e.activation(out=z[:], in_=z[:], func=AF.Identity, scale=rstd[:, 0:1], bias=nmr[:, 0:1]),
              [z, rstd, nmr], [z])
        kb.op("pool", lambda e: e.tensor_tensor(out=z[:], in0=z[:], in1=g[:], op=ALU.mult), [z, g], [z])
        kb.op("pool", lambda e: e.tensor_tensor(out=dst[:], in0=z[:], in1=b[:], op=ALU.add), [z, b], [dst])

    def tiles(self, with_ctx):
        t = [(i * 128, 0) for i in range(self.L // 128)]
        if with_ctx:
            t += [(self.L + i * 128, 1) for i in range(self.Lc // 128)]
        return t

    def phase_ffn(self, l, s, src_lat, src_ctx, dst_lat, dst_ctx, with_ctx):
        kb, nc, I = self.kb, self.nc, self.i
        L = self.L
        si = 0 if s == 0 else 2
        wi_d = I["ffn1_wi" if s == 0 else "ffn2_wi"]
        wo_d = I["ffn1_wo" if s == 0 else "ffn2_wo"]
        NT = 256

        def srcap(row0):
            return src_lat[row0:row0 + 128, :] if row0 < L else src_ctx[row0 - L:row0 - L + 128, :]

        def dstap(row0):
            return dst_lat[row0:row0 + 128, :] if row0 < L else dst_ctx[row0 - L:row0 - L + 128, :]

        with ExitStack() as cx:
            wi = kb.sb(cx, "wi", [128, 8, 2 * DFF], BF16)
            wo = kb.sb(cx, "wo", [128, NFC, D], BF16)
            c = self.load_ffn_consts(cx, l, s, si)
            gate = kb.sb(cx, "gate", [128, D], F32)
            gln = kb.sb(cx, "gln", [128, D], F32)
            bln = kb.sb(cx, "bln", [128, D], F32)
            xsl = [kb.sb(cx, "xsl%d" % i, [128, D], F32) for i in range(4)]
            zsl = [kb.sb(cx, "zsl%d" % i, [128, D], F32) for i in range(2)]
            hTs = [kb.sb(cx, "hT%d" % i, [128, 8, NT], BF16) for i in range(2)]
            xnb = [kb.sb(cx, "xnb%d" % i, [128, D], BF16) for i in range(NT // 128)]
            aT = kb.sb(cx, "aT", [128, NFC, NT], BF16)
            sg = [kb.sb(cx, "sg%d" % i, [128, NT], F32) for i in range(2)]
            tmp = dict(
                st=kb.sb(cx, "st", [128, 2, 6], F32), mv=kb.sb(cx, "mv", [128, 2], F32),
                rstd=kb.sb(cx, "rstd", [128, 1], F32),
                st2=kb.sb(cx, "st2", [128, 2, 6], F32), mv2=kb.sb(cx, "mv2", [128, 2], F32),
                rstd2=kb.sb(cx, "rstd2", [128, 1], F32), nmr=kb.sb(cx, "nmr", [128, 1], F32),
                ident=kb.sb(cx, "ident", [128, 128], BF16),
                epsc=kb.sb(cx, "epsc", [128, 1], F32),
            )
            tps = [kb.ps(cx, "tp%d" % i, [128, 8, 128], BF16) for i in range(2)]
            pg = [kb.ps(cx, "pg%d" % i, [128, 512], F32) for i in range(2)]
            pu = [kb.ps(cx, "pu%d" % i, [128, 512], F32) for i in range(2)]
            py = [kb.ps(cx, "py%d" % i, [128, 512], F32) for i in range(2)]
            kb.dma("sp", tmp["ident"][:], I["ident"][:, :], tmp["ident"], True)
            kb.op("dve", lambda e: e.memset(tmp["epsc"][:], EPS), [], [tmp["epsc"]])
            kb.dma("sp", gln[:], _bc(I["ln_g"][l, si, :]), gln, True)
            kb.dma("sp", bln[:], _bc(I["ln_b"][l, si, :]), bln, True)
            stage = xsl + zsl
            pieces = []
            for k in range(8):
                for c0 in range(0, 2 * DFF, 1024):
                    w = min(1024, 2 * DFF - c0)
                    pieces.append((wi_d[l, k * 128:(k + 1) * 128, c0:c0 + w], wi, (k, c0, w)))
            for fc in range(NFC):
                pieces.append((wo_d[l, fc * 128:(fc + 1) * 128, :], wo, (fc, 0, 1024)))
            cast_eng = ["dve", "act", "pool"]
            for n, (src, dstb, (a, c0, w)) in enumerate(pieces):
                sbuf = stage[n % len(stage)]
                kb.dma("sp", sbuf[:, 0:w], src, sbuf, True)
                en = cast_eng[n % 3]
                if en == "act":
                    kb.op("act", lambda e, sbuf=sbuf, dstb=dstb, a=a, c0=c0, w=w: e.copy(
                        out=dstb[:, a, c0:c0 + w], in_=sbuf[:, 0:w]), [sbuf], [dstb])
                else:
                    kb.op(en, lambda e, sbuf=sbuf, dstb=dstb, a=a, c0=c0, w=w: e.tensor_copy(
                        out=dstb[:, a, c0:c0 + w], in_=sbuf[:, 0:w]), [sbuf], [dstb])
            tl = self.tiles(with_ctx)
            groups = [tl[i:i + NT // 128] for i in range(0, len(tl), NT // 128)]
            cur_r = None
            xi = 0
            xts_of = {}

            def prep_ln(gi):
                nonlocal xi
                xts = []
                for j, (row0, _) in enumerate(groups[gi]):
                    xt = xsl[xi % 4]
                    xi += 1
                    xts.append(xt)
                    kb.dma("sp", xt[:], srcap(row0), xt, True)
                    self.ln_part(xt, xnb[j], tmp)
                xts_of[gi] = xts

            def prep_tr(gi):
                r_ = groups[gi][0][1]
                for j in range(len(groups[gi])):
                    self.tr_part(xnb[j], hTs[gi % 2], j * 128, r_, c, tps[j % 2], tmp["ident"])

            PIPE = PIPE_FFN
            if PIPE:
                prep_ln(0)
                prep_tr(0)
            for gi, grp in enumerate(groups):
                r = grp[0][1]
                assert all(t[1] == r for t in grp)
                hT = hTs[gi % 2]
                nt = len(grp) * 128
                if not PIPE:
                    prep_ln(gi)
                    prep_tr(gi)
                xts = xts_of.pop(gi)
                if PIPE and gi + 1 < len(groups):
                    prep_ln(gi + 1)
                for fc in range(NFC):
                    g_ps, u_ps = pg[fc % 2], pu[fc % 2]
                    for k in range(8):
                        kb.op("pe", lambda e, k=k, fc=fc, g_ps=g_ps: e.matmul(
                            g_ps[:, 0:nt], wi[:, k, fc * 128:(fc + 1) * 128], hT[:, k, 0:nt],
                            start=(k == 0), stop=(k == 7)), [wi, hT], [g_ps], inc=False)
                    for k in range(8):
                        kb.op("pe", lambda e, k=k, fc=fc, u_ps=u_ps: e.matmul(
                            u_ps[:, 0:nt], wi[:, k, DFF + fc * 128:DFF + (fc + 1) * 128], hT[:, k, 0:nt],
                            start=(k == 0), stop=(k == 7)), [wi, hT], [u_ps], inc=(k == 7))
                    sgb = sg[fc % 2]
                    kb.op("act", lambda e, g_ps=g_ps, sgb=sgb: e.activation(out=sgb[:, 0:nt], in_=g_ps[:, 0:nt], func=AF.Silu),
                          [g_ps], [sgb])
                    kb.op("dve", lambda e, u_ps=u_ps, sgb=sgb, fc=fc: e.tensor_tensor(
                        out=aT[:, fc, 0:nt], in0=u_ps[:, 0:nt], in1=sgb[:, 0:nt], op=ALU.mult), [u_ps, sgb], [aT])
                if PIPE and gi + 1 < len(groups):
                    prep_tr(gi + 1)
                if r != cur_r:
                    kb.dma("sp", gate[:], _bc(self.modd[l, r, (3 * s + 2) * D:(3 * s + 3) * D]), gate, True)
                    kb.op("dve", lambda e: e.tensor_scalar_mul(out=gate[:], in0=gate[:], scalar1=0.5), [gate], [gate])
                    cur_r = r
                for j, (row0, _) in enumerate(grp):
                    xt = xts[j]
                    z = zsl[j % 2]
                    for hh in range(2):
                        for fc in range(NFC):
                            kb.op("pe", lambda e, fc=fc, hh=hh, j=j: e.matmul(
                                py[hh][:, :], aT[:, fc, j * 128:(j + 1) * 128], wo[:, fc, hh * 512:(hh + 1) * 512],
                                start=(fc == 0), stop=(fc == NFC - 1)), [aT, wo], [py[hh]], inc=(fc == NFC - 1))
                    for hh in range(2):
                        kb.op("dve", lambda e, hh=hh, z=z: e.tensor_tensor(
                            out=z[:, hh * 512:(hh + 1) * 512], in0=py[hh][:, :], in1=gate[:, hh * 512:(hh + 1) * 512],
                            op=ALU.mult), [py[hh], gate], [z])
                    kb.op("dve", lambda e, z=z, xt=xt: e.scalar_tensor_tensor(
                        out=z[:], in0=xt[:], scalar=ALPHA, in1=z[:], op0=ALU.mult, op1=ALU.add), [xt, z], [z])
                    self.ln_affine_out(z, xt, gln, bln, tmp)
                    kb.dma("pool", dstap(row0), xt[:], xt, False)
        kb.barrier()


    def phase_proj(self, l):
        kb, nc, I = self.kb, self.nc, self.i
        L, LT = self.L, self.LT
        NF = 2304
        NTK = 1424
        with ExitStack() as cx:
            WF = kb.sb(cx, "WF", [128, 8, NF], BF16)
            WT = kb.sb(cx, "WT", [128, 8, NTK], BF16)
            c = self.load_ffn_consts(cx, l, 1, 1)
            stg = [kb.sb(cx, "wstg%d" % i, [128, INW], F32) for i in range(2)]
            xsl = [kb.sb(cx, "xsl%d" % i, [128, D], F32) for i in range(3)]
            hT = kb.sb(cx, "hT", [128, 8, 512], BF16)
            tmp = dict(
                st=kb.sb(cx, "st", [128, 2, 6], F32), mv=kb.sb(cx, "mv", [128, 2], F32),
                rstd=kb.sb(cx, "rstd", [128, 1], F32), xn=kb.sb(cx, "xn", [128, D], BF16),
                ident=kb.sb(cx, "ident", [128, 128], BF16), epsc=kb.sb(cx, "epsc", [128, 1], F32),
                tp=kb.ps(cx, "tp", [128, 8, 128], BF16),
            )
            rope = [[kb.sb(cx, "rope%d_%d" % (i, j), [128, 512], F32) for j in range(4)] for i in range(2)]
            blk = kb.sb(cx, "blk64", [128, 128], F32)
            gq = kb.sb(cx, "gq", [128, 4], F32)
            gb = kb.sb(cx, "gateb", [128, 16], F32)
            t1 = [kb.sb(cx, "t1_%d" % i, [128, 512], F32) for i in range(2)]
            t2 = [kb.sb(cx, "t2_%d" % i, [128, 512], F32) for i in range(2)]
            sq = [kb.sb(cx, "sq%d" % i, [128, 512], F32) for i in range(2)]
            rs = [kb.sb(cx, "rs%d" % i, [128, 512], F32) for i in range(2)]
            ob = [kb.sb(cx, "ob%d" % i, [128, 512], BF16) for i in range(4)]
            avo = [kb.sb(cx, "avo%d" % i, [128, 4, 65], BF16) for i in range(2)]
            cvo = [kb.sb(cx, "cvo%d" % i, [128, 4, 65], BF16) for i in range(2)]
            dvo = [kb.sb(cx, "dvo%d" % i, [128, 2, 65], BF16) for i in range(2)]
            cko = [kb.sb(cx, "cko%d" % i, [128, 512], F32) for i in range(2)]
            buo = [kb.sb(cx, "buo%d" % i, [128, 256], F32) for i in range(2)]
            cgo = [kb.sb(cx, "cgo%d" % i, [128, 16], F32) for i in range(2)]
            cgt = [kb.sb(cx, "cgt%d" % i, [128, 16], F32) for i in range(2)]
            pf = [kb.ps(cx, "pf%d" % i, [128, 512], F32) for i in range(4)]
            pss = kb.ps(cx, "pss", [128, 512], F32)
            pt = [kb.ps(cx, "pt%d" % i, [128, 512], F32) for i in range(2)]
            kb.dma("sp", tmp["ident"][:], I["ident"][:, :], tmp["ident"], True)
            kb.op("dve", lambda e: e.memset(tmp["epsc"][:], EPS), [], [tmp["epsc"]])
            kb.dma("sp", blk[:], I["blk64"][:, :], blk, True)
            kb.dma("sp", gb[:], _bc(I["ml_gate_b"][l, :]), gb, True)
            for hf in range(2):
                for ci, (nm, sw) in enumerate([("gqa_qnorm_g", 0), ("gqa_qnorm_g", 1), ("gqa_knorm_g", 0), ("gqa_knorm_g", 1)]):
                    for q in range(2):
                        so = (q * 32 + 32 * sw) % 64
                        kb.dma("sp", gq[hf * 64 + q * 32:hf * 64 + q * 32 + 32, ci:ci + 1],
                               I[nm][l, so:so + 32].rearrange("(p o) -> p o", o=1), gq, True)
            for b_ in avo + cvo + dvo:
                kb.op("dve", lambda e, b_=b_: e.memset(b_[:], 1.0), [], [b_])
            ci = 0
            for k in range(8):
                sg_ = stg[k % 2]
                kb.dma("sp", sg_[:], I["w_in"][l, k * 128:(k + 1) * 128, :], sg_, True)
                ops = []
                ops.append((WF[:, k, 0:512], sg_[:, 0:512]))
                for a in range(2):
                    ops.append((WF[:, k, 512:1024].rearrange("p (b h d) -> p b h d", h=2, d=16)[:, :, a, :],
                                sg_[:, 0:512].rearrange("p (b h d) -> p b h d", h=2, d=16)[:, :, 1 - a, :]))
                ops.append((WF[:, k, 1024:1280].rearrange("p (c s d) -> p c s d", c=2, s=2),
                            sg_[:, 2064:2320].rearrange("p (s c d) -> p c s d", c=2, s=2)))
                ops.append((WF[:, k, 1280:1408], sg_[:, 2320:2448]))
                for a in range(2):
                    ops.append((WF[:, k, 1408:1664].rearrange("p (c s h d) -> p c s h d", c=2, s=2, h=2)[:, :, :, a, :],
                                sg_[:, 2064:2320].rearrange("p (s c h d) -> p c s h d", c=2, s=2, h=2)[:, :, :, 1 - a, :]))
                    ops.append((WF[:, k, 1664:1792].rearrange("p (c h d) -> p c h d", c=2, h=2)[:, :, a, :],
                                sg_[:, 2320:2448].rearrange("p (c h d) -> p c h d", c=2, h=2)[:, :, 1 - a, :]))
                ops.append((WF[:, k, 1792:2304], sg_[:, 1024:1536]))
                ops.append((WT[:, k, 0:256], sg_[:, 512:768]))
                ops.append((WT[:, k, 256:512], sg_[:, 1536:1792]))
                ops.append((WT[:, k, 512:768], sg_[:, 1280:1536]))
                ops.append((WT[:, k, 768:1024], sg_[:, 1792:2048]))
                ops.append((WT[:, k, 1024:1280], sg_[:, 768:1024]))
                ops.append((WT[:, k, 1280:1408], sg_[:, 2448:2576]))
                ops.append((WT[:, k, 1408:1424], sg_[:, 2048:2064]))
                for oi, (o_, i_) in enumerate(ops):
                    en = ["dve", "pool"][ci % 2]
                    ci += 1
                    dstb = WF if oi < len(ops) - 7 else WT
                    kb.op(en, lambda e, o_=o_, i_=i_: e.tensor_copy(out=o_, in_=i_), [sg_], [dstb])
            import os
            STOP = int(os.environ.get("PROJ_STOP", "99"))
            tl = self.tiles(True)
            groups = [tl[i:i + 4] for i in range(0, len(tl), 4)]
            if STOP < 1:
                groups = []
            xi = 0
            tix = 0
            pfi = 0
            obi = 0
            for gi, grp in enumerate(groups):
                r = grp[0][1]
                row0g = grp[0][0]
                nt = len(grp) * 128
                rp = rope[gi % 2]
                for j_, nm in enumerate(["ropeA_c", "ropeA_s", "ropeD_c", "ropeD_s"]):
                    kb.dma("sp", rp[j_][:, 0:nt], I[nm][:, row0g:row0g + nt], rp[j_], True)
                xts = []
                for j, (row0, _) in enumerate(grp):
                    xt = xsl[xi % 3]
                    xi += 1
                    kb.dma("sp", xt[:], self.xs[row0:row0 + 128, :], xt, True)
                    self.ln_to_hT(xt, hT, j * 128, r, c, tmp)

                def fmm(cc):
                    nonlocal pfi
                    p_ = pf[pfi % 4]
                    pfi += 1
                    for k in range(8):
                        kb.op("pe", lambda e, k=k, p_=p_, cc=cc: e.matmul(
                            p_[:, 0:nt], WF[:, k, cc * 128:(cc + 1) * 128], hT[:, k, 0:nt],
                            start=(k == 0), stop=(k == 7)), [WF, hT], [p_], inc=(k == 7))
                    return p_

                def outbuf():
                    nonlocal obi
                    o_ = ob[obi % 4]
                    obi += 1
                    return o_

                for (c1, c2, dstT) in [(0, 4, self.AqT), (1, 5, self.AqT), (2, 6, self.AkT), (3, 7, self.AkT)][:(4 if STOP >= 2 else 0)]:
                    p1 = fmm(c1)
                    p2 = fmm(c2)
                    a_, b_ = t1[c1 % 2], t2[c1 % 2]
                    o_ = outbuf()
                    kb.op("dve", lambda e, p1=p1, a_=a_: e.tensor_tensor(out=a_[:, 0:nt], in0=p1[:, 0:nt], in1=rp[0][:, 0:nt], op=ALU.mult),
                          [p1, rp[0]], [a_])
                    kb.op("dve", lambda e, p2=p2, b_=b_: e.tensor_tensor(out=b_[:, 0:nt], in0=p2[:, 0:nt], in1=rp[1][:, 0:nt], op=ALU.mult),
                          [p2, rp[1]], [b_])
                    kb.op("dve", lambda e, a_=a_, b_=b_, o_=o_: e.tensor_tensor(out=o_[:, 0:nt], in0=a_[:, 0:nt], in1=b_[:, 0:nt], op=ALU.add),
                          [a_, b_], [o_])
                    rr = (c1 % 2) * 128
                    kb.dma("pool", dstT[rr:rr + 128, row0g:row0g + nt], o_[:, 0:nt], o_, False)
                for (c1, c2, dstT, rr, gc) in [(8, 11, self.DqT, 0, 0), (9, 12, self.DqT, 128, 0), (10, 13, self.DkT, 0, 2)][:(3 if STOP >= 3 else 0)]:
                    p1 = fmm(c1)
                    p2 = fmm(c2)
                    a_, b_ = t1[c1 % 2], t2[c1 % 2]
                    q_, r_ = sq[c1 % 2], rs[c1 % 2]
                    o_ = outbuf()
                    kb.op("act", lambda e, p1=p1, q_=q_: e.activation(out=q_[:, 0:nt], in_=p1[:, 0:nt], func=AF.Square), [p1], [q_])
                    kb.op("pe", lambda e, q_=q_: e.matmul(pss[:, 0:nt], blk[:, :], q_[:, 0:nt], start=True, stop=True), [blk, q_], [pss])
                    kb.op("act", lambda e, r_=r_: e.activation(out=r_[:, 0:nt], in_=pss[:, 0:nt], func=AF.Sqrt,
                                                              bias=tmp["epsc"][:, 0:1], scale=1.0 / 64), [pss, tmp["epsc"]], [r_])
                    kb.op("dve", lambda e, r_=r_: e.reciprocal(out=r_[:, 0:nt], in_=r_[:, 0:nt]), [r_], [r_])
                    kb.op("dve", lambda e, p1=p1, a_=a_, gc=gc: e.scalar_tensor_tensor(
                        out=a_[:, 0:nt], in0=p1[:, 0:nt], scalar=gq[:, gc:gc + 1], in1=rp[2][:, 0:nt], op0=ALU.mult, op1=ALU.mult),
                        [p1, gq, rp[2]], [a_])
                    kb.op("dve", lambda e, p2=p2, b_=b_, gc=gc: e.scalar_tensor_tensor(
                        out=b_[:, 0:nt], in0=p2[:, 0:nt], scalar=gq[:, gc + 1:gc + 2], in1=rp[3][:, 0:nt], op0=ALU.mult, op1=ALU.mult),
                        [p2, gq, rp[3]], [b_])
                    kb.op("dve", lambda e, a_=a_, b_=b_: e.tensor_tensor(out=a_[:, 0:nt], in0=a_[:, 0:nt], in1=b_[:, 0:nt], op=ALU.add),
                          [a_, b_], [a_])
                    kb.op("dve", lambda e, a_=a_, r_=r_, o_=o_: e.tensor_tensor(out=o_[:, 0:nt], in0=a_[:, 0:nt], in1=r_[:, 0:nt], op=ALU.mult),
                          [a_, r_], [o_])
                    kb.dma("pool", dstT[rr:rr + 128, row0g:row0g + nt], o_[:, 0:nt], o_, False)
                for (c1, dstT, rr) in [(14, self.CqT, 0), (15, self.CqT, 128), (16, self.CkT, 0), (17, self.CkT, 128)][:(4 if STOP >= 4 else 0)]:
                    p1 = fmm(c1)
                    o_ = outbuf()
                    kb.op("act", lambda e, p1=p1, o_=o_: e.copy(out=o_[:, 0:nt], in_=p1[:, 0:nt]), [p1], [o_])
                    kb.dma("pool", dstT[rr:rr + 128, row0g:row0g + nt], o_[:, 0:nt], o_, False)
                for j, (row0, _) in enumerate(grp):
                    ti = tix % 2
                    tix += 1
                    for tc, (c0, w) in enumerate([(0, 512), (512, 512), (1024, 400)][:max(0, STOP - 4)]):
                        p_ = pt[tc % 2] if tc < 2 else pss
                        for k in range(8):
                            kb.op("pe", lambda e, k=k, p_=p_, c0=c0, w=w, j=j: e.matmul(
                                p_[:, 0:w], hT[:, k, j * 128:(j + 1) * 128], WT[:, k, c0:c0 + w],
                                start=(k == 0), stop=(k == 7)), [hT, WT], [p_], inc=(k == 7))
                        SUB = int(os.environ.get("PROJ_SUB", "9"))
                        if tc == 0:
                            if SUB >= 1:
                                kb.op("act", lambda e, p_=p_, ti=ti: e.copy(out=avo[ti][:, :, 0:64], in_=p_[:, 0:256].rearrange("p (h d) -> p h d", h=4)),
                                      [p_], [avo[ti]])
                            if SUB >= 2:
                                kb.op("dve", lambda e, p_=p_, ti=ti: e.tensor_copy(out=cvo[ti][:, :, 0:64], in_=p_[:, 256:512].rearrange("p (h d) -> p h d", h=4)),
                                      [p_], [cvo[ti]])
                            if SUB >= 3:
                                kb.dma("pool", self.Av[row0:row0 + 128, :, :], avo[ti][:], avo[ti], False)
                            if SUB >= 4:
                                kb.dma("pool", self.Cv[row0:row0 + 128, :, :], cvo[ti][:], cvo[ti], False)
                        elif tc == 1:
                            kb.op("dve", lambda e, p_=p_, ti=ti: e.tensor_copy(out=cko[ti][:, 0:256], in_=p_[:, 0:256]), [p_], [cko[ti]])
                            kb.op("act", lambda e, p_=p_, ti=ti: e.activation(out=cko[ti][:, 256:512], in_=p_[:, 256:512], func=AF.Sigmoid),
                                  [p_], [cko[ti]])
                            kb.dma("pool", self.Ck[row0:row0 + 128, :], cko[ti][:, 0:256], cko[ti], False)
                            kb.dma("pool", self.Co[row0:row0 + 128, :], cko[ti][:, 256:512], cko[ti], False)
                        else:
                            kb.op("dve", lambda e, p_=p_, ti=ti: e.tensor_copy(out=buo[ti][:], in_=p_[:, 0:256]), [p_], [buo[ti]])
                            kb.op("act", lambda e, p_=p_, ti=ti: e.copy(out=dvo[ti][:, :, 0:64], in_=p_[:, 256:384].rearrange("p (h d) -> p h d", h=2)),
                                  [p_], [dvo[ti]])
                            kb.op("dve", lambda e, p_=p_, ti=ti: e.tensor_tensor(out=cgo[ti][:], in0=p_[:, 384:400], in1=gb[:], op=ALU.add),
                                  [p_, gb], [cgo[ti]])
                            kb.op("act", lambda e, ti=ti: e.activation(out=cgt[ti][:], in_=cgo[ti][:], func=AF.Exp, scale=-1.0), [cgo[ti]], [cgt[ti]])
                            kb.op("dve", lambda e, ti=ti: e.tensor_scalar_add(out=cgt[ti][:], in0=cgt[ti][:], scalar1=1.0), [cgt[ti]], [cgt[ti]])
                            kb.op("act", lambda e, ti=ti: e.activation(out=cgt[ti][:], in_=cgt[ti][:], func=AF.Ln), [cgt[ti]], [cgt[ti]])
                            for q in (4, 12):
                                kb.op("dve", lambda e, ti=ti, q=q: e.tensor_scalar_mul(out=cgo[ti][:, q:q + 4], in0=cgt[ti][:, q:q + 4], scalar1=-1.0),
                                      [cgt[ti], cgo[ti]], [cgo[ti]])
                            kb.dma("pool", self.Bu[row0:row0 + 128, :], buo[ti][:], buo[ti], False)
                            kb.dma("pool", self.Dv[row0:row0 + 128, :, :], dvo[ti][:], dvo[ti], False)
                            kb.dma("pool", self.Cg[row0:row0 + 128, :], cgo[ti][:], cgo[ti], False)
        kb.barrier()


    def phase_attn(self, l, with_ctx_q):
        import math
        kb, nc, I = self.kb, self.nc, self.i
        L, LT, Lc = self.L, self.LT, self.Lc
        NCH = LT // 128
        lam_init = 0.8 - 0.6 * math.exp(-0.3 * l)
        with ExitStack() as cx:
            AkT = kb.sb(cx, "AkTs", [128, 2, LT], BF16)
            Avs = kb.sb(cx, "Avs", [128, NCH, 260], BF16)
            DkT = kb.sb(cx, "DkTs", [128, LT], BF16)
            Dvs = kb.sb(cx, "Dvs", [128, NCH, 130], BF16)
            Aq = [kb.sb(cx, "Aqg%d" % i, [128, 2, 512], BF16) for i in range(2)]
            Dq = [kb.sb(cx, "Dqg%d" % i, [128, 2, 512], BF16) for i in range(2)]
            pT2 = [kb.sb(cx, "pT2_%d" % i, [128, 2, 512], BF16) for i in range(2)]
            dl = kb.sb(cx, "dl", [128, 128], F32)
            dlp = kb.sb(cx, "dlp", [128, 2, 32], F32)
            lamt = kb.sb(cx, "lamt", [128, 2], F32)
            neglam = kb.sb(cx, "neglam", [128, 1], F32)
            gA = kb.sb(cx, "gA", [128, 256], F32)
            epsc = kb.sb(cx, "epsc", [128, 1], F32)
            rec = [kb.sb(cx, "rec%d" % i, [128, 4], F32) for i in range(2)]
            am = [kb.sb(cx, "am%d" % i, [128, 4, 64], F32) for i in range(2)]
            dsq = kb.sb(cx, "dsq", [128, 4, 64], F32)
            ssq = kb.sb(cx, "ssq", [128, 4], F32)
            amix = [kb.sb(cx, "amix%d" % i, [128, 4, 256], BF16) for i in range(2)]
            dmix = [kb.sb(cx, "dmix%d" % i, [128, 4, 256], BF16) for i in range(2)]
            psS2 = [kb.ps(cx, "psS2_%d" % i, [128, 2, 512], F32) for i in range(2)]
            po = [kb.ps(cx, "po%d" % i, [128, 512], F32) for i in range(4)]
            for c_ in range(2):
                kb.dma("sp", AkT[:, c_, :], self.AkT[c_ * 128:(c_ + 1) * 128, :], AkT, True)
            kb.dma("sp", DkT[:], self.DkT[:, :], DkT, True)
            avv = self.Av.rearrange("(c p) h e -> p c (h e)", p=128)
            dvv = self.Dv.rearrange("(c p) h e -> p c (h e)", p=128)
            for c0 in range(0, NCH, 8):
                c1 = min(NCH, c0 + 8)
                kb.dma("sp", Avs[:, c0:c1, :], avv[:, c0:c1, :], Avs, True)
                kb.dma("sp", Dvs[:, c0:c1, :], dvv[:, c0:c1, :], Dvs, True)
            kb.dma("sp", dl[:], _bc(I["diff_lambda"][l, :]), dl, True)
            kb.dma("sp", gA[:], _bc(I["diff_norm_g"][l, :]), gA, True)
            kb.op("dve", lambda e: e.memset(epsc[:], EPS), [], [epsc])
            kb.op("dve", lambda e: e.tensor_scalar_mul(out=gA[:], in0=gA[:], scalar1=1.0 - lam_init), [gA], [gA])
            dlv = dl[:].rearrange("p (a b) -> p a b", a=4)
            kb.op("dve", lambda e: e.tensor_tensor(out=dlp[:, 0, :], in0=dlv[:, 0, :], in1=dlv[:, 1, :], op=ALU.mult), [dl], [dlp])
            kb.op("dve", lambda e: e.tensor_tensor(out=dlp[:, 1, :], in0=dlv[:, 2, :], in1=dlv[:, 3, :], op=ALU.mult), [dl], [dlp])
            kb.op("dve", lambda e: e.reduce_sum(out=lamt[:], in_=dlp[:], axis=AX.X), [dlp], [lamt])
            kb.op("act", lambda e: e.activation(out=lamt[:], in_=lamt[:], func=AF.Exp), [lamt], [lamt])
            kb.op("dve", lambda e: e.tensor_tensor(out=neglam[:], in0=lamt[:, 1:2], in1=lamt[:, 0:1], op=ALU.subtract), [lamt], [neglam])
            kb.op("dve", lambda e: e.tensor_scalar_add(out=neglam[:], in0=neglam[:], scalar1=-lam_init), [neglam], [neglam])

            cnt = dict(s=0, o=0)

            def pair_sweep(specs, nq, kcs, scale):
                Os = [po[(cnt["o"] % 2) * 2], po[(cnt["o"] % 2) * 2 + 1]]
                cnt["o"] += 1
                n = len(kcs)
                Ss = [None] * n

                def emitS(idx):
                    sl = cnt["s"] % 2
                    cnt["s"] += 1
                    S2, P2 = psS2[sl], pT2[sl]
                    for a in range(2):
                        kT_fn, q_ap, v_fn, kTb, qb, vb = specs[a]
                        lhsT, kw = kT_fn(kcs[idx])
                        kb.op("pe", lambda e: e.matmul(S2[:, a, 0:nq * 128], lhsT, q_ap, start=True, stop=True, **kw), [kTb, qb], [S2])
                    Ss[idx] = (S2, P2)
                emitS(0)
                for idx in range(n):
                    if idx + 1 < n:
                        emitS(idx + 1)
                    S2, P2 = Ss[idx]
                    kb.op("act", lambda e: e.activation(out=P2[:, :, 0:nq * 128], in_=S2[:, :, 0:nq * 128], func=AF.Exp, scale=scale), [S2], [P2])
                    for a in range(2):
                        v_fn, vb = specs[a][2], specs[a][5]
                        for j in range(nq):
                            kb.op("pe", lambda e, j=j: e.matmul(
                                Os[a][:, j * 65:(j + 1) * 65], P2[:, a, j * 128:(j + 1) * 128], v_fn(kcs[idx]),
                                start=(idx == 0 and j == 0), stop=(idx == n - 1), skip_group_check=True),
                                [P2, vb], [Os[a]], inc=(j == nq - 1))
                return Os

            qgroups = [(q0, 4, list(range(NCH))) for q0 in range(0, L, 512)]
            if with_ctx_q:
                qgroups.append((L, Lc // 128, list(range(L // 128, NCH))))
            for gi, (q0, nq, kcs) in enumerate(qgroups):
                aq, dq = Aq[gi % 2], Dq[gi % 2]
                nqt = nq * 128
                for c_ in range(2):
                    kb.dma("sp", aq[:, c_, 0:nqt], self.AqT[c_ * 128:(c_ + 1) * 128, q0:q0 + nqt], aq, True)
                    kb.dma("sp", dq[:, c_, 0:nqt], self.DqT[c_ * 128:(c_ + 1) * 128, q0:q0 + nqt], dq, True)
                amx, dmx = amix[gi % 2], dmix[gi % 2]
                for h in range(4):
                    c_ = h // 2
                    specs = []
                    for m in range(2):
                        base = (h % 2) * 64 + m * 32
                        kw = dict(tile_position=(96, 0)) if base == 96 else {}
                        specs.append((lambda kc, base=base, kw=kw: (AkT[base:base + 32, c_, kc * 128:(kc + 1) * 128], kw),
                                      aq[base:base + 32, c_, 0:nqt], lambda kc: Avs[:, kc, h * 65:(h + 1) * 65], AkT, aq, Avs))
                    Os = pair_sweep(specs, nq, kcs, 32 ** -0.5)
                    for m in range(2):
                        O = Os[m]
                        Ov = O[:, 0:260].rearrange("p (j e) -> p j e", e=65)
                        r_, a_ = rec[m], am[m]
                        kb.op("dve", lambda e: e.reciprocal(out=r_[:, 0:nq], in_=Ov[:, 0:nq, 64]), [O], [r_])
                        kb.op("dve", lambda e: e.tensor_tensor(out=a_[:, 0:nq, :], in0=Ov[:, 0:nq, 0:64],
                                                              in1=r_[:, 0:nq].unsqueeze(2).to_broadcast([128, nq, 64]), op=ALU.mult),
                              [O, r_], [a_])
                    kb.op("dve", lambda e: e.scalar_tensor_tensor(out=am[0][:, 0:nq, :], in0=am[1][:, 0:nq, :], scalar=neglam[:, 0:1],
                                                                 in1=am[0][:, 0:nq, :], op0=ALU.mult, op1=ALU.add),
                          [am[0], am[1], neglam], [am[0]])
                    kb.op("dve", lambda e: e.tensor_tensor(out=dsq[:, 0:nq, :], in0=am[0][:, 0:nq, :], in1=am[0][:, 0:nq, :], op=ALU.mult),
                          [am[0]], [dsq])
                    kb.op("dve", lambda e: e.reduce_sum(out=ssq[:, 0:nq], in_=dsq[:, 0:nq, :], axis=AX.X), [dsq], [ssq])
                    kb.op("act", lambda e: e.activation(out=ssq[:, 0:nq], in_=ssq[:, 0:nq], func=AF.Sqrt, bias=epsc[:, 0:1], scale=1.0 / 64),
                          [ssq, epsc], [ssq])
                    kb.op("dve", lambda e: e.reciprocal(out=ssq[:, 0:nq], in_=ssq[:, 0:nq]), [ssq], [ssq])
                    kb.op("dve", lambda e: e.tensor_tensor(out=am[0][:, 0:nq, :], in0=am[0][:, 0:nq, :],
                                                          in1=ssq[:, 0:nq].unsqueeze(2).to_broadcast([128, nq, 64]), op=ALU.mult),
                          [am[0], ssq], [am[0]])
                    kb.op("dve", lambda e, h=h: e.tensor_tensor(out=amx[:, 0:nq, h * 64:(h + 1) * 64], in0=am[0][:, 0:nq, :],
                                                               in1=gA[:, h * 64:(h + 1) * 64].unsqueeze(1).to_broadcast([128, nq, 64]), op=ALU.mult),
                          [am[0], gA], [amx])
                for c_ in range(2):
                    heads = [c_, c_ + 2]
                    specs = []
                    for h in heads:
                        kv = h // 2
                        specs.append((lambda kc, kv=kv: (DkT[kv * 64:(kv + 1) * 64, kc * 128:(kc + 1) * 128], {}),
                                      dq[kv * 64:(kv + 1) * 64, c_, 0:nqt], lambda kc, kv=kv: Dvs[:, kc, kv * 65:(kv + 1) * 65], DkT, dq, Dvs))
                    Os = pair_sweep(specs, nq, kcs, 64 ** -0.5)
                    for a, h in enumerate(heads):
                        O = Os[a]
                        Ov = O[:, 0:260].rearrange("p (j e) -> p j e", e=65)
                        r_ = rec[a]
                        kb.op("dve", lambda e: e.reciprocal(out=r_[:, 0:nq], in_=Ov[:, 0:nq, 64]), [O], [r_])
                        kb.op("dve", lambda e, h=h: e.tensor_tensor(out=dmx[:, 0:nq, h * 64:(h + 1) * 64], in0=Ov[:, 0:nq, 0:64],
                                                                   in1=r_[:, 0:nq].unsqueeze(2).to_broadcast([128, nq, 64]), op=ALU.mult),
                              [O, r_], [dmx])
                for j in range(nq):
                    kb.dma("pool", self.mix[q0 + j * 128:q0 + (j + 1) * 128, 0:256], amx[:, j, :], amx, False)
                    kb.dma("pool", self.mix[q0 + j * 128:q0 + (j + 1) * 128, 768:1024], dmx[:, j, :], dmx, False)
        kb.barrier()


    def phase_pool(self, l, with_ctx):
        kb, nc, I = self.kb, self.nc, self.i
        L, LT = self.L, self.LT
        with ExitStack() as cx:
            PM = kb.sb(cx, "PM", [128, 20, 128], F32)
            pwf = kb.sb(cx, "pwf", [64, 4, 64], F32)
            pw = kb.sb(cx, "pw", [64, 4, 64], BF16)
            psc = kb.sb(cx, "psc", [128, 256], F32)
            ut = [kb.sb(cx, "ut%d" % i, [128, 256], F32) for i in range(4)]
            dT = [kb.sb(cx, "dT%d" % i, [64, 512], BF16) for i in range(2)]
            yo = [kb.sb(cx, "yo%d" % i, [128, 256], BF16) for i in range(2)]
            pd = [kb.ps(cx, "pd%d" % i, [128, 512], F32) for i in range(2)]
            py = [kb.ps(cx, "pyb%d" % i, [128, 512], F32) for i in range(2)]
            kb.dma("sp", PM[:], I["poolM"][:, :, :], PM, True)
            kb.dma("sp", pwf[:], I["pool_w"][l].rearrange("g c e -> c g e"), pwf, True)
            kb.dma("sp", psc[:], _bc(I["pool_scale"][l, :]), psc, True)
            kb.op("dve", lambda e: e.tensor_copy(out=pw[:], in_=pwf[:]), [pwf], [pw])
            seqs = [(0, L // 128)]
            if with_ctx:
                seqs.append((L, self.Lc // 128))
            ui = 0
            ti = 0
            for (r0, n) in seqs:
                tiles = {}

                def get(t):
                    nonlocal ui
                    if t not in tiles:
                        u = ut[ui % 4]
                        ui += 1
                        kb.dma("sp", u[:], self.Bu[r0 + t * 128:r0 + (t + 1) * 128, :], u, True)
                        tiles[t] = u
                    return tiles[t]
                for t in range(n):
                    srcs = []
                    if t > 0:
                        srcs.append((get(t - 1), 0))
                    srcs.append((get(t), 3 if t == 0 else (4 if t == n - 1 else 1)))
                    if t < n - 1:
                        srcs.append((get(t + 1), 2))
                    tiles.pop(t - 2, None)
                    p_d, p_y = pd[ti % 2], py[ti % 2]
                    d_, y_ = dT[ti % 2], yo[ti % 2]
                    ti += 1
                    first = True
                    for g in range(4):
                        for si, (u, v) in enumerate(srcs):
                            kb.op("pe", lambda e, g=g, u=u, v=v, first=first: e.matmul(
                                p_d[0:64, g * 128:(g + 1) * 128], u[:, g * 64:(g + 1) * 64], PM[:, v * 4 + g, :],
                                start=first, stop=(si == len(srcs) - 1), skip_group_check=True),
                                [u, PM], [p_d], inc=(g == 3 and si == len(srcs) - 1))
                            first = False
                    kb.op("act", lambda e: e.copy(out=d_[:, :], in_=p_d[0:64, :]), [p_d], [d_])
                    for g in range(4):
                        kb.op("pe", lambda e, g=g: e.matmul(p_y[:, g * 64:(g + 1) * 64], d_[:, g * 128:(g + 1) * 128], pw[:, g, :],
                                                            start=True, stop=True, skip_group_check=True), [d_, pw], [p_y], inc=(g == 3))
                    kb.op("dve", lambda e: e.tensor_tensor(out=y_[:], in0=p_y[:, 0:256], in1=psc[:], op=ALU.mult), [p_y, psc], [y_])
                    kb.dma("pool", self.mix[r0 + t * 128:r0 + (t + 1) * 128, 256:512], y_[:], y_, False)
        kb.barrier()

    def phase_wout(self, l, with_ctx):
        kb, nc, I = self.kb, self.nc, self.i
        with ExitStack() as cx:
            wo = kb.sb(cx, "wout", [128, 8, D], BF16)
            stg = [kb.sb(cx, "wostg%d" % i, [128, D], F32) for i in range(2)]
            gate = kb.sb(cx, "gate5", [128, D], F32)
            gln = kb.sb(cx, "gln", [128, D], F32)
            bln = kb.sb(cx, "bln", [128, D], F32)
            ident = kb.sb(cx, "ident", [128, 128], BF16)
            epsc = kb.sb(cx, "epsc", [128, 1], F32)
            mt = [kb.sb(cx, "mt%d" % i, [128, D], BF16) for i in range(2)]
            mT = [kb.sb(cx, "mT%d" % i, [128, 8, 128], BF16) for i in range(2)]
            xsl = [kb.sb(cx, "xsl%d" % i, [128, D], F32) for i in range(3)]
            zsl = [kb.sb(cx, "zsl%d" % i, [128, D], F32) for i in range(2)]
            tmp = dict(st2=kb.sb(cx, "st2", [128, 2, 6], F32), mv2=kb.sb(cx, "mv2", [128, 2], F32),
                       rstd2=kb.sb(cx, "rstd2", [128, 1], F32), nmr=kb.sb(cx, "nmr", [128, 1], F32), epsc=epsc)
            tp = [kb.ps(cx, "tpw%d" % i, [128, 8, 128], BF16) for i in range(2)]
            py = [kb.ps(cx, "pyw%d" % i, [128, 512], F32) for i in range(4)]
            kb.dma("sp", ident[:], I["ident"][:, :], ident, True)
            kb.op("dve", lambda e: e.memset(epsc[:], EPS), [], [epsc])
            kb.dma("sp", gln[:], _bc(I["ln_g"][l, 1, :]), gln, True)
            kb.dma("sp", bln[:], _bc(I["ln_b"][l, 1, :]), bln, True)
            for k in range(8):
                sg_ = stg[k % 2]
                kb.dma("sp", sg_[:], I["w_out"][l, k * 128:(k + 1) * 128, :], sg_, True)
                kb.op(["dve", "pool"][k % 2], lambda e, k=k, sg_=sg_: e.tensor_copy(out=wo[:, k, :], in_=sg_[:]), [sg_], [wo])
            cur_r = None
            for ti, (row0, r) in enumerate(self.tiles(with_ctx)):
                if r != cur_r:
                    kb.dma("sp", gate[:], _bc(self.modd[l, r, 5 * D:6 * D]), gate, True)
                    cur_r = r
                m_, mT_, xt, z = mt[ti % 2], mT[ti % 2], xsl[ti % 3], zsl[ti % 2]
                tp_ = tp[ti % 2]
                kb.dma("sp", m_[:], self.mix[row0:row0 + 128, :], m_, True)
                kb.dma("sp", xt[:], self.xs[row0:row0 + 128, :], xt, True)
                for k in range(8):
                    kb.op("pe", lambda e, k=k: e.transpose(tp_[:, k, :], m_[:, k * 128:(k + 1) * 128], ident[:]),
                          [m_, ident], [tp_], inc=(k == 7))
                kb.op("act", lambda e: e.copy(out=mT_[:], in_=tp_[:]), [tp_], [mT_])
                for hh in range(2):
                    p_ = py[(ti % 2) * 2 + hh]
                    for k in range(8):
                        kb.op("pe", lambda e, k=k, hh=hh, p_=p_: e.matmul(p_[:, :], mT_[:, k, :], wo[:, k, hh * 512:(hh + 1) * 512],
                                                                     start=(k == 0), stop=(k == 7)), [mT_, wo], [p_], inc=(k == 7))
                    kb.op("dve", lambda e, hh=hh, p_=p_: e.tensor_tensor(out=z[:, hh * 512:(hh + 1) * 512], in0=p_[:, :],
                                                                        in1=gate[:, hh * 512:(hh + 1) * 512], op=ALU.mult), [p_, gate], [z])
                kb.op("dve", lambda e: e.scalar_tensor_tensor(out=z[:], in0=xt[:], scalar=ALPHA, in1=z[:], op0=ALU.mult, op1=ALU.add),
                      [xt, z], [z])
                self.ln_affine_out(z, xt, gln, bln, tmp)
                kb.dma("pool", self.xs[row0:row0 + 128, :], xt[:], xt, False)
        kb.barrier()

    def phase_mlstm(self, l, with_ctx_out):
        kb, nc, I = self.kb, self.nc, self.i
        L, LT = self.L, self.LT
        NCH = LT // 128
        nlat = L // 128
        nctx = self.Lc // 128
        with ExitStack() as cx:
            Hb = kb.sb(cx, "Hb", [128, NCH, 256], F32)
            CM = kb.sb(cx, "CM", [128, 9, 128], F32)
            bmask = kb.sb(cx, "bmask", [128, 130], F32)
            ones = kb.sb(cx, "ones", [128, 1], F32)
            epsc = kb.sb(cx, "epsc", [128, 1], F32)
            gC = kb.sb(cx, "gC", [128, 256], F32)
            Cst = [kb.sb(cx, "Cst%d" % i, [128, 130], F32) for i in range(2)]
            Cbf = [kb.sb(cx, "Cbf%d" % i, [128, 130], BF16) for i in range(2)]
            qT = [kb.sb(cx, "qT%d" % i, [128, 2, 128], BF16) for i in range(2)]
            kT = [kb.sb(cx, "kT%d" % i, [128, 2, 128], BF16) for i in range(2)]
            va = [kb.sb(cx, "va%d" % i, [128, 260], BF16) for i in range(2)]
            kt = [kb.sb(cx, "kt%d" % i, [128, 256], F32) for i in range(2)]
            cg = [kb.sb(cx, "cg%d" % i, [128, 16], F32) for i in range(2)]
            so = [kb.sb(cx, "so%d" % i, [128, 256], F32) for i in range(2)]
            lfrep = kb.sb(cx, "lfrep", [128, 4, 128], F32)
            linm = kb.sb(cx, "linm", [128, 4, 128], F32)
            lfrep2 = kb.sb(cx, "lfrep2", [128, 256], F32)
            DT = kb.sb(cx, "DT", [128, 512], F32)
            AT = kb.sb(cx, "AT", [128, 512], BF16)
            ew = kb.sb(cx, "ew", [128, 8], F32)
            INs = kb.sb(cx, "INs", [128, 4, 65], F32)
            tot = kb.sb(cx, "tot", [128, 4, 65], F32)
            den = kb.sb(cx, "den", [128, 4], F32)
            K2 = kb.sb(cx, "K2", [128, 256], BF16)
            dec = kb.sb(cx, "dec", [128, 2], F32)
            tmpu = kb.sb(cx, "tmpu", [128, 130], F32)
            hs = kb.sb(cx, "hs", [128, 256], F32)
            sq = kb.sb(cx, "sqc", [128, 256], F32)
            ssq = kb.sb(cx, "ssqc", [128, 4], F32)
            outb = [kb.sb(cx, "outb%d" % i, [128, 256], BF16) for i in range(2)]
            GT = kb.ps(cx, "GT", [128, 512], F32)
            STa = kb.ps(cx, "STa", [128, 512], F32)
            STb = kb.ps(cx, "STb", [128, 512], F32)
            ND = kb.ps(cx, "ND", [128, 512], F32)
            IN = kb.ps(cx, "IN", [128, 512], F32)
            BW = kb.ps(cx, "BW", [128, 512], F32)
            UPD = [kb.ps(cx, "UPD%d" % i, [128, 512], F32) for i in range(2)]
            kb.dma("sp", CM[:], I["cmat"][:, :, :], CM, True)
            kb.dma("sp", bmask[:], I["blkmask"][:, :], bmask, True)
            kb.dma("sp", gC[:], _bc(I["ml_norm_g"][l, :]), gC, True)
            kb.op("dve", lambda e: e.memset(ones[:], 1.0), [], [ones])
            kb.op("dve", lambda e: e.memset(epsc[:], EPS), [], [epsc])
            import os
            MS = int(os.environ.get("MS", "99"))
            fwd_order = [nlat + i for i in range(nctx)] + list(range(nlat))
            bwd_order = [nlat + i for i in reversed(range(nctx))] + list(reversed(range(nlat)))
            it = 0
            for dr, order in enumerate([fwd_order, bwd_order]):
                for pr in range(2):
                    kb.op("dve", lambda e, pr=pr: e.memset(Cst[pr][:], 0.0), [], [Cst[pr]])
                    kb.op("dve", lambda e, pr=pr: e.memset(Cbf[pr][:], 0.0), [], [Cbf[pr]])
                Tm, nTm, Ts, nm = (0, 1, 2, 3) if dr == 0 else (4, 5, 6, 7)
                li0, lf0 = (0, 4) if dr == 0 else (8, 12)
                for ch in order:
                    b_ = it % 2
                    it += 1
                    r0 = ch * 128
                    q_, k_, v_, kt_, cg_, so_ = qT[b_], kT[b_], va[b_], kt[b_], cg[b_], so[b_]
                    for c_ in range(2):
                        kb.dma("sp", q_[:, c_, :], self.CqT[c_ * 128:(c_ + 1) * 128, r0:r0 + 128], q_, True)
                        kb.dma("sp", k_[:, c_, :], self.CkT[c_ * 128:(c_ + 1) * 128, r0:r0 + 128], k_, True)
                    kb.dma("sp", v_[:], self.Cv[r0:r0 + 128, :, :].rearrange("p h e -> p (h e)"), v_, True)
                    kb.dma("sp", kt_[:], self.Ck[r0:r0 + 128, :], kt_, True)
                    kb.dma("sp", cg_[:], self.Cg[r0:r0 + 128, :], cg_, True)
                    is_out = dr == 1 and (ch < nlat or with_ctx_out)
                    if is_out:
                        kb.dma("sp", so_[:], self.Co[r0:r0 + 128, :], so_, True)
                    lf = cg_[:, lf0:lf0 + 4]
                    li = cg_[:, li0:li0 + 4]
                    kb.op("dve", lambda e: e.tensor_copy(out=lfrep[:], in_=lf.unsqueeze(2).to_broadcast([128, 4, 128])), [cg_], [lfrep])
                    kb.op("dve", lambda e: e.tensor_copy(out=lfrep2[:].rearrange("p (h e) -> p h e", e=64),
                                                        in_=lf.unsqueeze(2).to_broadcast([128, 4, 64])), [cg_], [lfrep2])
                    kb.op("dve", lambda e: e.tensor_tensor(out=linm[:], in0=CM[:, nm, :].unsqueeze(1).to_broadcast([128, 4, 128]),
                                                          in1=li.unsqueeze(2).to_broadcast([128, 4, 128]), op=ALU.add), [CM, cg_], [linm])
                    if MS < 2:
                        continue
                    HM = [0, 2, 1, 3]
                    for bi, h in enumerate(HM):
                        o_ = GT[:, bi * 128:(bi + 1) * 128]
                        kb.op("pe", lambda e, h=h, o_=o_: e.matmul(o_, lfrep[:, h, :], CM[:, Tm, :], start=(bi == 0), stop=False, skip_group_check=True),
                              [lfrep, CM], [GT], inc=False)
                        kb.op("pe", lambda e, h=h, o_=o_: e.matmul(o_, CM[:, nTm, :], lfrep[:, h, :], start=False, stop=False, skip_group_check=True),
                              [lfrep, CM], [GT], inc=False)
                        kb.op("pe", lambda e, h=h, o_=o_: e.matmul(o_, CM[:, 8, :], linm[:, h, :], start=False, stop=True, skip_group_check=True),
                              [linm, CM], [GT], inc=(bi == 3))
                    kb.op("act", lambda e: e.activation(out=DT[:], in_=GT[:, :], func=AF.Exp), [GT], [DT])
                    if MS < 4:
                        continue
                    for bi, h in enumerate(HM):
                        pb = (h % 2) * 64
                        STx = STa if pb == 0 else STb
                        kb.op("pe", lambda e, h=h, pb=pb, STx=STx, bi=bi: e.matmul(
                            STx[:, (bi % 2) * 128:(bi % 2 + 1) * 128], k_[pb:pb + 64, h // 2, :], q_[pb:pb + 64, h // 2, :],
                            start=True, stop=True, skip_group_check=True), [k_, q_], [STx], inc=(bi % 2 == 1))
                    for half, STx in enumerate([STa, STb]):
                        kb.op("dve", lambda e, half=half, STx=STx: e.scalar_tensor_tensor(
                            out=AT[:, half * 256:(half + 1) * 256], in0=STx[:, 0:256], scalar=0.125, in1=DT[:, half * 256:(half + 1) * 256],
                            op0=ALU.mult, op1=ALU.mult), [STx, DT], [AT])
                    if MS < 6:
                        continue
                    for bi, h in enumerate(HM):
                        kb.op("pe", lambda e, h=h, bi=bi: e.matmul(ND[:, h * 65:(h + 1) * 65], AT[:, bi * 128:(bi + 1) * 128], v_[:, h * 65:(h + 1) * 65],
                                                                   start=True, stop=True, skip_group_check=True), [AT, v_], [ND], inc=(bi == 3))
                    if MS < 7:
                        continue
                    for pr in range(2):
                        kb.op("pe", lambda e, pr=pr: e.matmul(IN[:, pr * 130:(pr + 1) * 130], q_[:, pr, :], Cbf[pr][:, :],
                                                              start=True, stop=True, skip_group_check=True), [q_, Cbf[pr]], [IN], inc=(pr == 1))
                    if MS < 8:
                        continue
                    kb.op("pe", lambda e: e.matmul(BW[:, 0:4], CM[:, Tm, :], lf, start=True, stop=True, skip_group_check=True), [CM, cg_], [BW], inc=False)
                    kb.op("pe", lambda e: e.matmul(BW[:, 4:8], CM[:, Ts, :], lf, start=False, stop=False, skip_group_check=True), [CM, cg_], [BW], inc=False)
                    kb.op("pe", lambda e: e.matmul(BW[:, 4:8], CM[:, 8, :], li, start=False, stop=True, skip_group_check=True), [CM, cg_], [BW])
                    kb.op("act", lambda e: e.activation(out=ew[:], in_=BW[:, 0:8], func=AF.Exp), [BW], [ew])
                    if MS < 9:
                        continue
                    INv = IN[:, 0:260].rearrange("p (h e) -> p h e", e=65)
                    NDv = ND[:, 0:260].rearrange("p (h e) -> p h e", e=65)
                    kb.op("dve", lambda e: e.scalar_tensor_tensor(out=INs[:], in0=INv, scalar=0.125,
                                                                 in1=ew[:, 0:4].unsqueeze(2).to_broadcast([128, 4, 65]), op0=ALU.mult, op1=ALU.mult),
                          [IN, ew], [INs])
                    kb.op("dve", lambda e: e.tensor_tensor(out=tot[:], in0=NDv, in1=INs[:], op=ALU.add), [ND, INs], [tot])
                    kb.op("dve", lambda e: e.tensor_scalar_mul(out=den[:], in0=tot[:, :, 64], scalar1=-1.0), [tot], [den])
                    kb.op("dve", lambda e: e.scalar_tensor_tensor(out=den[:], in0=tot[:, :, 64], scalar=1.0, in1=den[:], op0=ALU.max, op1=ALU.max),
                          [tot, den], [den])
                    kb.op("dve", lambda e: e.reciprocal(out=den[:], in_=den[:]), [den], [den])
                    hv = Hb[:, ch, :].rearrange("p (h e) -> p h e", e=64)
                    dbc = den[:].unsqueeze(2).to_broadcast([128, 4, 64])
                    if dr == 0:
                        kb.op("dve", lambda e: e.tensor_tensor(out=hv, in0=tot[:, :, 0:64], in1=dbc, op=ALU.mult), [tot, den], [Hb])
                    else:
                        hsv = hs[:].rearrange("p (h e) -> p h e", e=64)
                        kb.op("dve", lambda e: e.tensor_tensor(out=hsv, in0=tot[:, :, 0:64], in1=dbc, op=ALU.mult), [tot, den], [hs])
                        if is_out:
                            kb.op("dve", lambda e: e.tensor_tensor(out=hs[:], in0=hs[:], in1=Hb[:, ch, :], op=ALU.add), [hs, Hb], [hs])
                            kb.op("dve", lambda e: e.tensor_tensor(out=sq[:], in0=hs[:], in1=hs[:], op=ALU.mult), [hs], [sq])
                            kb.op("dve", lambda e: e.reduce_sum(out=ssq[:], in_=sq[:].rearrange("p (h e) -> p h e", e=64), axis=AX.X), [sq], [ssq])
                            kb.op("act", lambda e: e.activation(out=ssq[:], in_=ssq[:], func=AF.Sqrt, bias=epsc[:, 0:1], scale=1.0 / 64),
                                  [ssq, epsc], [ssq])
                            kb.op("dve", lambda e: e.reciprocal(out=ssq[:], in_=ssq[:]), [ssq], [ssq])
                            kb.op("dve", lambda e: e.tensor_tensor(out=hsv, in0=hsv, in1=ssq[:].unsqueeze(2).to_broadcast([128, 4, 64]), op=ALU.mult),
                                  [hs, ssq], [hs])
                            kb.op("dve", lambda e: e.tensor_tensor(out=hs[:], in0=hs[:], in1=gC[:], op=ALU.mult), [hs, gC], [hs])
                            ob_ = outb[b_]
                            kb.op("dve", lambda e: e.tensor_tensor(out=ob_[:], in0=hs[:], in1=so_[:], op=ALU.mult), [hs, so_], [ob_])
                            kb.dma("pool", self.mix[r0:r0 + 128, 512:768], ob_[:], ob_, False)
                    if MS < 11:
                        continue
                    kb.op("dve", lambda e: e.tensor_tensor(out=K2[:].rearrange("p (h e) -> p h e", e=64),
                                                          in0=kt_[:].rearrange("p (h e) -> p h e", e=64),
                                                          in1=ew[:, 4:8].unsqueeze(2).to_broadcast([128, 4, 64]), op=ALU.mult), [kt_, ew], [K2])
                    for pr in range(2):
                        kb.op("pe", lambda e, pr=pr: e.matmul(UPD[pr][:, 0:130], K2[:, pr * 128:(pr + 1) * 128], v_[:, pr * 130:(pr + 1) * 130],
                                                              start=True, stop=True), [K2, v_], [UPD[pr]])
                    for pr in range(2):
                        kb.op("pe", lambda e, pr=pr: e.matmul(BW[:, 8 + pr:9 + pr], lfrep2[:, pr * 128:(pr + 1) * 128], ones[:, 0:1],
                                                              start=True, stop=True, skip_group_check=True), [lfrep2, ones], [BW], inc=(pr == 1))
                    kb.op("act", lambda e: e.activation(out=dec[:], in_=BW[:, 8:10], func=AF.Exp), [BW], [dec])
                    for pr in range(2):
                        kb.op("dve", lambda e, pr=pr: e.tensor_tensor(out=tmpu[:], in0=UPD[pr][:, 0:130], in1=bmask[:], op=ALU.mult),
                              [UPD[pr], bmask], [tmpu])
                        kb.op("dve", lambda e, pr=pr: e.scalar_tensor_tensor(out=Cst[pr][:], in0=Cst[pr][:], scalar=dec[:, pr:pr + 1], in1=tmpu[:],
                                                                            op0=ALU.mult, op1=ALU.add), [Cst[pr], dec, tmpu], [Cst[pr]])
                        kb.op("act", lambda e, pr=pr: e.copy(out=Cbf[pr][:], in_=Cst[pr][:]), [Cst[pr]], [Cbf[pr]])
        kb.barrier()

    def finish(self):
        self.kb.es.close()
        return self.nc


def host_consts(L=8192, Lc=256):
    LT = L + Lc
    c = {}
    c["ident"] = np.eye(128, dtype=np.float32).astype(ml_dtypes.bfloat16)
    t = np.arange(L)
    row = (t // 64).astype(np.float32)
    col = (t % 64).astype(np.float32)
    def tables(dim):
        axis_dim = dim // 2
        inv = (10000.0 ** (-np.arange(0, axis_dim, 2, dtype=np.float32) / axis_dim)).astype(np.float32)
        ang = np.concatenate([row[:, None] * inv, col[:, None] * inv], axis=-1).astype(np.float32)
        cos = np.cos(ang).astype(np.float32)
        sin = np.sin(ang).astype(np.float32)
        half = dim // 2
        C = np.ones((128, LT), np.float32)
        S = np.zeros((128, LT), np.float32)
        for p in range(128):
            d = p % dim
            C[p, :L] = cos[:, d % half]
            S[p, :L] = (-1.0 if d < half else 1.0) * sin[:, d % half]
        return C, S
    c["ropeA_c"], c["ropeA_s"] = tables(32)
    c["ropeD_c"], c["ropeD_s"] = tables(64)
    p = np.arange(128)
    c["blk64"] = (p[:, None] // 64 == p[None, :] // 64).astype(np.float32)
    PM = np.zeros((128, 20, 128), np.float32)
    for g, w in enumerate((2, 4, 8, 16)):
        h = w // 2
        for v in range(5):
            M = np.zeros((128, 128), np.float32)
            for t_ in range(128):
                lo, hi = t_ - h, t_ + h
                cnt = float(w)
                if v == 3:
                    lo = max(lo, 0); cnt = float(hi - lo)
                if v == 4:
                    hi = min(hi, 128); cnt = float(hi - lo)
                for tp_ in range(lo, hi):
                    if v in (1, 3, 4):
                        src = tp_
                    elif v == 0:
                        src = tp_ + 128
                    else:
                        src = tp_ - 128
                    if v == 3 and tp_ >= 128:
                        continue
                    if v == 4 and tp_ < 0:
                        continue
                    if 0 <= src < 128:
                        M[src, t_] += 1.0 / cnt
                if v in (1, 3, 4):
                    M[t_, t_] -= 1.0
            PM[:, v * 4 + g, :] = M
    c["poolM"] = PM
    s_ = np.arange(128)[:, None]
    j_ = np.arange(128)[None, :]
    CM = np.zeros((128, 9, 128), np.float32)
    CM[:, 0] = (s_ <= j_); CM[:, 1] = -(s_ <= j_).astype(np.float32); CM[:, 2] = (s_ > j_)
    CM[:, 3] = np.where(s_ <= j_, 0.0, -30000.0)
    CM[:, 4] = (s_ >= j_); CM[:, 5] = -(s_ >= j_).astype(np.float32); CM[:, 6] = (s_ < j_)
    CM[:, 7] = np.where(s_ >= j_, 0.0, -30000.0)
    CM[:, 8] = np.eye(128)
    c["cmat"] = CM
    bm = np.zeros((128, 130), np.float32)
    bm[0:64, 0:65] = 1.0
    bm[64:128, 65:130] = 1.0
    c["blkmask"] = bm
    return c


def build_program(L=8192, Lc=256, NL=DEPTH, dbg=()):
    P = Prog(L, Lc, NL, dbg=dbg)
    P.phase_mod()
    for l in range(NL):
        last = l == NL - 1
        if l == 0:
            src_lat, src_ctx = P.i["x"], P.i["ctx"]
        else:
            src_lat, src_ctx = P.xs[0:L, :], P.xs[L:L + Lc, :]
        P.phase_ffn(l, 0, src_lat, src_ctx, P.xs[0:L, :], P.xs[L:L + Lc, :], True)
        P.phase_proj(l)
        P.phase_attn(l, not last)
        P.phase_pool(l, not last)
        P.phase_mlstm(l, not last)
        P.phase_wout(l, not last)
        P.phase_ffn(l, 2, P.xs[0:L, :], P.xs[L:L + Lc, :], P.out if last else P.xs[0:L, :], P.xs[L:L + Lc, :], not last)
    P.finish()
    return P


def make_in_maps(inputs, L, Lc, ncores):
    hc = host_consts(L, Lc)
    shared = {}
    for k in ["w_ada", "b_ada", "ln_g", "ln_b", "ffn1_wi", "ffn1_wo", "ffn2_wi", "ffn2_wo", "w_in", "w_out",
              "diff_norm_g", "pool_w", "pool_scale", "ml_norm_g", "gqa_qnorm_g", "gqa_knorm_g"]:
        shared[k] = np.ascontiguousarray(np.asarray(inputs[k], dtype=np.float32))
    shared["diff_lambda"] = np.ascontiguousarray(np.asarray(inputs["diff_lambda"], dtype=np.float32).reshape(DEPTH, 128))
    shared["ml_gate_b"] = np.ascontiguousarray(np.asarray(inputs["ml_gate_b"], dtype=np.float32).reshape(DEPTH, 16))
    shared.update(hc)
    x = np.asarray(inputs["x"], dtype=np.float32)
    ctx = np.asarray(inputs["ctx"], dtype=np.float32)
    c = np.asarray(inputs["c"], dtype=np.float32)
    c_ctx = np.asarray(inputs["c_ctx"], dtype=np.float32)
    maps = []
    for b in range(ncores):
        cv = np.zeros((128, 2, 8), np.float32)
        cv[:, 0, :] = c[b].reshape(8, 128).T
        cv[:, 1, :] = c_ctx.reshape(8, 128).T
        m = dict(shared)
        m["x"] = np.ascontiguousarray(x[b, :L])
        m["ctx"] = np.ascontiguousarray(ctx[b, :Lc])
        m["cvec"] = cv
        maps.append(m)
    return maps


def kernel(**inputs):
    L, Lc, B = 8192, 256, 8
    P = build_program(L, Lc)
    maps = make_in_maps(inputs, L, Lc, B)
    res = run_bass_kernel_spmd(P.nc, maps, core_ids=list(range(B)))
    out = np.stack([np.asarray(res.results[b]["out"]) for b in range(B)], axis=0)
    return out.astype(np.float32)
```

```python
import numpy as np
import ml_dtypes
from contextlib import ExitStack
import concourse.bass as bass
import concourse.mybir as mybir
from concourse.bass_utils import run_bass_kernel_spmd

F32 = mybir.dt.float32
BF16 = mybir.dt.bfloat16
AF = mybir.ActivationFunctionType
ALU = mybir.AluOpType
AX = mybir.AxisListType

D = 1024
DFF = 2816
NFC = DFF // 128
NMOD = 9
EPS = 1e-6
DEPTH = 2
ALPHA = (2.0 * DEPTH) ** 0.25
INW = 2576
PIPE_FFN = False


class Sem:
    def __init__(self, h):
        self.h = h
        self.count = 0


class Buf:
    def __init__(self, t, name):
        self.t = t
        self.name = name
        self.w = {}
        self.r = {}
        self.dsem = None
        self.psum = False

    def __getitem__(self, k):
        return self.t[k]


class KB:
    def __init__(self, nc):
        self.nc = nc
        self.es = ExitStack()
        self.eng = {}
        for name, h in [("pe", nc.tensor), ("act", nc.scalar), ("dve", nc.vector),
                        ("pool", nc.gpsimd), ("sp", nc.sync)]:
            sem = Sem(self.es.enter_context(nc.semaphore("s_" + name)))
            self.eng[name] = (h, sem)
        self.known = {name: {} for name in self.eng}
        self.bar = Sem(self.es.enter_context(nc.semaphore("s_bar")))
        self.dfree = [Sem(self.es.enter_context(nc.semaphore("s_d%d" % i))) for i in range(72)]
        self.dall = list(self.dfree)
        self.dused = []
        self.n_ins = 0
        self.uid = 0

    def sb(self, ctx, name, shape, dt):
        self.uid += 1
        name = "%s_%d" % (name, self.uid)
        t = ctx.enter_context(self.nc.sbuf_tensor(name, list(shape), dt))
        return Buf(t, name)

    def ps(self, ctx, name, shape, dt):
        self.uid += 1
        name = "%s_%d" % (name, self.uid)
        t = ctx.enter_context(self.nc.psum_tensor(name, list(shape), dt))
        b = Buf(t, name)
        b.psum = True
        return b

    def _deps(self, reads, writes):
        d = {}
        for b in reads:
            for sm, v in b.w.items():
                if d.get(sm, 0) < v:
                    d[sm] = v
        for b in writes:
            for sm, v in b.w.items():
                if d.get(sm, 0) < v:
                    d[sm] = v
            for sm, v in b.r.items():
                if d.get(sm, 0) < v:
                    d[sm] = v
        return d

    def _wait(self, ename, d):
        h, _ = self.eng[ename]
        kn = self.known[ename]
        for sm, v in d.items():
            if kn.get(sm, 0) >= v:
                continue
            assert v <= sm.count, "dependency on a not-yet-issued increment"
            h.wait_ge(sm.h, v)
            kn[sm] = v
            self.n_ins += 1

    def op(self, ename, fn, reads=(), writes=(), inc=True):
        h, sem = self.eng[ename]
        d = self._deps(reads, writes)
        for b in reads:
            if b.psum:
                for sm, v in b.r.items():
                    if sm is not sem and d.get(sm, 0) < v:
                        d[sm] = v
        if ename == "pe":
            d.pop(sem, None)
        self._wait(ename, d)
        ins = fn(h)
        self.n_ins += 1
        if inc:
            ins.then_inc(sem.h, 1)
            sem.count += 1
            tick = sem.count
        else:
            tick = sem.count + 1
        for b in writes:
            b.w = {sem: tick}
            b.r = {}
        for b in reads:
            if b.r.get(sem, 0) < tick:
                b.r[sem] = tick
        return ins

    def dma(self, q, out, in_, sb, load, **kw):
        h, _ = self.eng[q]
        if sb.dsem is None:
            sb.dsem = self.dfree.pop()
            self.dused.append(sb)
        ds = sb.dsem
        if load:
            d = self._deps((), (sb,))
            if not sb.r and set(sb.w.keys()) == {ds}:
                d.pop(ds, None)
        else:
            d = self._deps((sb,), ())
        self._wait(q, d)
        ins = h.dma_start(out=out, in_=in_, **kw)
        self.n_ins += 1
        ins.then_inc(ds.h, 16)
        ds.count += 16
        tick = ds.count
        if load:
            sb.w = {ds: tick}
            sb.r = {}
        else:
            sb.r[ds] = tick
        return ins

    def barrier(self):
        h, _ = self.eng["sp"]
        allsems = [s for (_, s) in self.eng.values()] + self.dall
        kn = self.known["sp"]
        for sm in allsems:
            if kn.get(sm, 0) < sm.count:
                h.wait_ge(sm.h, sm.count)
                kn[sm] = sm.count
        h.sem_inc(self.bar.h, 1)
        self.bar.count += 1
        for name, (eh, _) in self.eng.items():
            if name != "sp":
                eh.wait_ge(self.bar.h, self.bar.count)
            self.known[name] = {sm: sm.count for sm in allsems}
        for b in self.dused:
            self.dfree.append(b.dsem)
            b.dsem = None
        self.dused = []


def _bc(ap, n=128):
    return ap.partition_broadcast(n)


class Prog:
    def __init__(self, L, Lc, NL, dbg=()):
        self.L, self.Lc, self.NL = L, Lc, NL
        self.LT = L + Lc
        self.dbg = set(dbg)
        nc = bass.Bass("TRN2", target_bir_lowering=False)
        self.nc = nc
        self.kb = KB(nc)
        LT = self.LT

        def din(name, shape, dt=F32):
            return nc.dram_tensor(name, list(shape), dt, kind="ExternalInput").ap()

        def dsc(name, shape, dt=F32):
            kind = "ExternalOutput" if name in self.dbg else "Internal"
            return nc.dram_tensor(name, list(shape), dt, kind=kind).ap()

        self.i = dict(
            x=din("x", [L, D]), ctx=din("ctx", [Lc, D]), cvec=din("cvec", [128, 2, 8]),
            w_ada=din("w_ada", [NL, D, NMOD * D]), b_ada=din("b_ada", [NL, NMOD * D]),
            ln_g=din("ln_g", [NL, 3, D]), ln_b=din("ln_b", [NL, 3, D]),
            ffn1_wi=din("ffn1_wi", [NL, D, 2 * DFF]), ffn1_wo=din("ffn1_wo", [NL, DFF, D]),
            ffn2_wi=din("ffn2_wi", [NL, D, 2 * DFF]), ffn2_wo=din("ffn2_wo", [NL, DFF, D]),
            ident=din("ident", [128, 128], BF16),
        )
        self.i.update(
            w_in=din("w_in", [NL, D, INW]), w_out=din("w_out", [NL, D, D]),
            diff_lambda=din("diff_lambda", [NL, 128]), diff_norm_g=din("diff_norm_g", [NL, 256]),
            pool_w=din("pool_w", [NL, 4, 64, 64]), pool_scale=din("pool_scale", [NL, 256]),
            ml_gate_b=din("ml_gate_b", [NL, 16]), ml_norm_g=din("ml_norm_g", [NL, 256]),
            gqa_qnorm_g=din("gqa_qnorm_g", [NL, 64]), gqa_knorm_g=din("gqa_knorm_g", [NL, 64]),
            ropeA_c=din("ropeA_c", [128, LT]), ropeA_s=din("ropeA_s", [128, LT]),
            ropeD_c=din("ropeD_c", [128, LT]), ropeD_s=din("ropeD_s", [128, LT]),
            blk64=din("blk64", [128, 128]), poolM=din("poolM", [128, 20, 128]),
            cmat=din("cmat", [128, 9, 128]), blkmask=din("blkmask", [128, 130]),
        )
        self.out = nc.dram_tensor("out", [L, D], F32, kind="ExternalOutput").ap()
        self.xs = dsc("xs", [LT, D])
        self.modd = dsc("modd", [NL, 2, NMOD * D])
        self.AqT = dsc("AqT", [256, LT], BF16); self.AkT = dsc("AkT", [256, LT], BF16)
        self.Av = dsc("Av", [LT, 4, 65], BF16)
        self.DqT = dsc("DqT", [256, LT], BF16); self.DkT = dsc("DkT", [128, LT], BF16)
        self.Dv = dsc("Dv", [LT, 2, 65], BF16)
        self.CqT = dsc("CqT", [256, LT], BF16); self.CkT = dsc("CkT", [256, LT], BF16)
        self.Cv = dsc("Cv", [LT, 4, 65], BF16)
        self.Ck = dsc("Ck", [LT, 256]); self.Co = dsc("Co", [LT, 256]); self.Cg = dsc("Cg", [LT, 16])
        self.Bu = dsc("Bu", [LT, 256])
        self.mix = dsc("mix", [LT, D], BF16)

    def phase_mod(self):
        kb, nc, I = self.kb, self.nc, self.i
        NL = self.NL
        CG = 3072
        with ExitStack() as cx:
            cv = kb.sb(cx, "cv", [128, 2, 8], F32)
            sv = kb.sb(cx, "sv", [128, 2, 8], F32)
            wbuf = [kb.sb(cx, "wada%d" % i, [128, CG], F32) for i in range(3)]
            bb = kb.sb(cx, "bada", [2, CG], F32)
            ob = [kb.sb(cx, "modo%d" % i, [2, CG], F32) for i in range(2)]
            pss = [kb.ps(cx, "modps%d" % i, [128, 512], F32) for i in range(6)]
            kb.dma("sp", cv[:], I["cvec"][:, :, :], cv, True)
            kb.op("act", lambda e: e.activation(out=sv[:], in_=cv[:], func=AF.Sigmoid), [cv], [sv])
            kb.op("dve", lambda e: e.tensor_tensor(out=sv[:], in0=sv[:], in1=cv[:], op=ALU.mult), [sv, cv], [sv])
            it = 0
            for l in range(NL):
                for cg in range(NMOD * D // CG):
                    for k in range(8):
                        wb = wbuf[it % 3]
                        it += 1
                        kb.dma("sp", wb[:], I["w_ada"][l, k * 128:(k + 1) * 128, cg * CG:(cg + 1) * CG], wb, True)
                        for j in range(6):
                            kb.op("pe", lambda e, j=j, k=k, wb=wb: e.matmul(
                                pss[j][0:2, :], sv[:, :, k], wb[:, j * 512:(j + 1) * 512],
                                start=(k == 0), stop=(k == 7)), [sv, wb], [pss[j]], inc=(j == 5))
                    o = ob[(l * 3 + cg) % 2]
                    kb.dma("sp", bb[:], _bc(I["b_ada"][l, cg * CG:(cg + 1) * CG], 2), bb, True)
                    for j in range(6):
                        kb.op("dve", lambda e, j=j, o=o: e.tensor_tensor(
                            out=o[:, j * 512:(j + 1) * 512], in0=pss[j][0:2, :], in1=bb[:, j * 512:(j + 1) * 512],
                            op=ALU.add), [pss[j], bb], [o])
                    kb.dma("pool", self.modd[l, :, cg * CG:(cg + 1) * CG], o[:], o, False)
        kb.barrier()

    def load_ffn_consts(self, cx, l, s, si):
        kb, I = self.kb, self.i
        c = {}
        c["sc1"] = kb.sb(cx, "sc1", [128, 2, 8], F32)
        c["sh"] = kb.sb(cx, "sh", [128, 2, 8], F32)
        for r in range(2):
            src1 = self.modd[l, r, (3 * s + 1) * D:(3 * s + 2) * D].rearrange("(k p) -> p k", p=128)
            src0 = self.modd[l, r, (3 * s) * D:(3 * s + 1) * D].rearrange("(k p) -> p k", p=128)
            kb.dma("sp", c["sc1"][:, r, :], src1, c["sc1"], True, allow_slow_non_contiguous=True)
            kb.dma("sp", c["sh"][:, r, :], src0, c["sh"], True, allow_slow_non_contiguous=True)
        kb.op("dve", lambda e: e.tensor_scalar_add(out=c["sc1"][:], in0=c["sc1"][:], scalar1=1.0), [c["sc1"]], [c["sc1"]])
        return c

    def ln_part(self, xt, xn, tmp):
        kb = self.kb
        st, mv, rstd = tmp["st"], tmp["mv"], tmp["rstd"]
        for hh in range(2):
            kb.op("dve", lambda e, hh=hh: e.bn_stats(out=st[:, hh, :], in_=xt[:, hh * 512:(hh + 1) * 512]), [xt], [st])
        kb.op("dve", lambda e: e.bn_aggr(out=mv[:], in_=st[:]), [st], [mv])
        kb.op("act", lambda e: e.activation(out=rstd[:], in_=mv[:, 1:2], func=AF.Sqrt, bias=tmp["epsc"][:, 0:1], scale=1.0),
              [mv, tmp["epsc"]], [rstd])
        kb.op("dve", lambda e: e.reciprocal(out=rstd[:], in_=rstd[:]), [rstd], [rstd])
        kb.op("dve", lambda e: e.tensor_scalar(out=xn[:], in0=xt[:], scalar1=mv[:, 0:1], scalar2=rstd[:, 0:1],
                                              op0=ALU.subtract, op1=ALU.mult), [xt, mv, rstd], [xn])

    def tr_part(self, xn, hT, col0, r, c, tp, ident):
        kb = self.kb
        for k in range(8):
            kb.op("pe", lambda e, k=k: e.transpose(tp[:, k, :], xn[:, k * 128:(k + 1) * 128], ident[:]),
                  [xn, ident], [tp], inc=(k == 7))
        for k in range(8):
            kb.op("act", lambda e, k=k: e.activation(out=hT[:, k, col0:col0 + 128], in_=tp[:, k, :], func=AF.Identity,
                                                    scale=c["sc1"][:, r, k:k + 1], bias=c["sh"][:, r, k:k + 1]),
                  [tp, c["sc1"], c["sh"]], [hT])

    def ln_to_hT(self, xt, hT, col0, r, c, tmp):
        self.ln_part(xt, tmp["xn"], tmp)
        self.tr_part(tmp["xn"], hT, col0, r, c, tmp["tp"], tmp["ident"])

    def ln_affine_out(self, z, dst, g, b, tmp):
        kb = self.kb
        st, mv, rstd = tmp["st2"], tmp["mv2"], tmp["rstd2"]
        for hh in range(2):
            kb.op("dve", lambda e, hh=hh: e.bn_stats(out=st[:, hh, :], in_=z[:, hh * 512:(hh + 1) * 512]), [z], [st])
        kb.op("dve", lambda e: e.bn_aggr(out=mv[:], in_=st[:]), [st], [mv])
        kb.op("act", lambda e: e.activation(out=rstd[:], in_=mv[:, 1:2], func=AF.Sqrt, bias=tmp["epsc"][:, 0:1], scale=1.0),
              [mv, tmp["epsc"]], [rstd])
        kb.op("dve", lambda e: e.reciprocal(out=rstd[:], in_=rstd[:]), [rstd], [rstd])
        kb.op("dve", lambda e: e.tensor_scalar(out=z[:], in0=z[:], scalar1=mv[:, 0:1], scalar2=rstd[:, 0:1],
                                              op0=ALU.subtract, op1=ALU.mult), [z, mv, rstd], [z])
        kb.op("dve", lambda e: e.tensor_tensor(out=z[:], in0=z[:], in1=g[:], op=ALU.mult), [z, g], [z])
        kb.op("dve", lambda e: e.tensor_tensor(out=dst[:], in0=z[:], in1=b[:], op=ALU.add), [z, b], [dst])

    def tiles(self, with_ctx):
        t = [(i * 128, 0) for i in range(self.L // 128)]
        if with_ctx:
            t += [(self.L + i * 128, 1) for i in range(self.Lc // 128)]
        return t

    def phase_ffn(self, l, s, src_lat, src_ctx, dst_lat, dst_ctx, with_ctx):
        kb, nc, I = self.kb, self.nc, self.i
        L = self.L
        si = 0 if s == 0 else 2
        wi_d = I["ffn1_wi" if s == 0 else "ffn2_wi"]
        wo_d = I["ffn1_wo" if s == 0 else "ffn2_wo"]
        NT = 256

        def srcap(row0):
            return src_lat[row0:row0 + 128, :] if row0 < L else src_ctx[row0 - L:row0 - L + 128, :]

        def dstap(row0):
            return dst_lat[row0:row0 + 128, :] if row0 < L else dst_ctx[row0 - L:row0 - L + 128, :]

        with ExitStack() as cx:
            wi = kb.sb(cx, "wi", [128, 8, 2 * DFF], BF16)
            wo = kb.sb(cx, "wo", [128, NFC, D], BF16)
            c = self.load_ffn_consts(cx, l, s, si)
            gate = kb.sb(cx, "gate", [128, D], F32)
            gln = kb.sb(cx, "gln", [128, D], F32)
            bln = kb.sb(cx, "bln", [128, D], F32)
            xsl = [kb.sb(cx, "xsl%d" % i, [128, D], F32) for i in range(4)]
            zsl = [kb.sb(cx, "zsl%d" % i, [128, D], F32) for i in range(2)]
            hTs = [kb.sb(cx, "hT%d" % i, [128, 8, NT], BF16) for i in range(2)]
            xnb = [kb.sb(cx, "xnb%d" % i, [128, D], BF16) for i in range(NT // 128)]
            aT = kb.sb(cx, "aT", [128, NFC, NT], BF16)
            sg = [kb.sb(cx, "sg%d" % i, [128, NT], F32) for i in range(2)]
            tmp = dict(
                st=kb.sb(cx, "st", [128, 2, 6], F32), mv=kb.sb(cx, "mv", [128, 2], F32),
                rstd=kb.sb(cx, "rstd", [128, 1], F32),
                st2=kb.sb(cx, "st2", [128, 2, 6], F32), mv2=kb.sb(cx, "mv2", [128, 2], F32),
                rstd2=kb.sb(cx, "rstd2", [128, 1], F32),
                ident=kb.sb(cx, "ident", [128, 128], BF16),
                epsc=kb.sb(cx, "epsc", [128, 1], F32),
            )
            tps = [kb.ps(cx, "tp%d" % i, [128, 8, 128], BF16) for i in range(2)]
            pg = [kb.ps(cx, "pg%d" % i, [128, 512], F32) for i in range(2)]
            pu = [kb.ps(cx, "pu%d" % i, [128, 512], F32) for i in range(2)]
            py = [kb.ps(cx, "py%d" % i, [128, 512], F32) for i in range(2)]
            kb.dma("sp", tmp["ident"][:], I["ident"][:, :], tmp["ident"], True)
            kb.op("dve", lambda e: e.memset(tmp["epsc"][:], EPS), [], [tmp["epsc"]])
            kb.dma("sp", gln[:], _bc(I["ln_g"][l, si, :]), gln, True)
            kb.dma("sp", bln[:], _bc(I["ln_b"][l, si, :]), bln, True)
            stage = xsl + zsl
            pieces = []
            for k in range(8):
                for c0 in range(0, 2 * DFF, 1024):
                    w = min(1024, 2 * DFF - c0)
                    pieces.append((wi_d[l, k * 128:(k + 1) * 128, c0:c0 + w], wi, (k, c0, w)))
            for fc in range(NFC):
                pieces.append((wo_d[l, fc * 128:(fc + 1) * 128, :], wo, (fc, 0, 1024)))
            cast_eng = ["dve", "act", "pool"]
            for n, (src, dstb, (a, c0, w)) in enumerate(pieces):
                sbuf = stage[n % len(stage)]
                kb.dma("sp", sbuf[:, 0:w], src, sbuf, True)
                en = cast_eng[n % 3]
                if en == "act":
                    kb.op("act", lambda e, sbuf=sbuf, dstb=dstb, a=a, c0=c0, w=w: e.copy(
                        out=dstb[:, a, c0:c0 + w], in_=sbuf[:, 0:w]), [sbuf], [dstb])
                else:
                    kb.op(en, lambda e, sbuf=sbuf, dstb=dstb, a=a, c0=c0, w=w: e.tensor_copy(
                        out=dstb[:, a, c0:c0 + w], in_=sbuf[:, 0:w]), [sbuf], [dstb])
            tl = self.tiles(with_ctx)
            groups = [tl[i:i + NT // 128] for i in range(0, len(tl), NT // 128)]
            cur_r = None
            xi = 0
            xts_of = {}

            def prep_ln(gi):
                nonlocal xi
                xts = []
                for j, (row0, _) in enumerate(groups[gi]):
                    xt = xsl[xi % 4]
                    xi += 1
                    xts.append(xt)
                    kb.dma("sp", xt[:], srcap(row0), xt, True)
                    self.ln_part(xt, xnb[j], tmp)
                xts_of[gi] = xts

            def prep_tr(gi):
                r_ = groups[gi][0][1]
                for j in range(len(groups[gi])):
                    self.tr_part(xnb[j], hTs[gi % 2], j * 128, r_, c, tps[j % 2], tmp["ident"])

            PIPE = PIPE_FFN
            if PIPE:
                prep_ln(0)
                prep_tr(0)
            for gi, grp in enumerate(groups):
                r = grp[0][1]
                assert all(t[1] == r for t in grp)
                hT = hTs[gi % 2]
                nt = len(grp) * 128
                if not PIPE:
                    prep_ln(gi)
                    prep_tr(gi)
                xts = xts_of.pop(gi)
                if PIPE and gi + 1 < len(groups):
                    prep_ln(gi + 1)
                for fc in range(NFC):
                    g_ps, u_ps = pg[fc % 2], pu[fc % 2]
                    for k in range(8):
                        kb.op("pe", lambda e, k=k, fc=fc, g_ps=g_ps: e.matmul(
                            g_ps[:, 0:nt], wi[:, k, fc * 128:(fc + 1) * 128], hT[:, k, 0:nt],
                            start=(k == 0), stop=(k == 7)), [wi, hT], [g_ps], inc=False)
                    for k in range(8):
                        kb.op("pe", lambda e, k=k, fc=fc, u_ps=u_ps: e.matmul(
                            u_ps[:, 0:nt], wi[:, k, DFF + fc * 128:DFF + (fc + 1) * 128], hT[:, k, 0:nt],
                            start=(k == 0), stop=(k == 7)), [wi, hT], [u_ps], inc=(k == 7))
                    sgb = sg[fc % 2]
                    kb.op("act", lambda e, g_ps=g_ps, sgb=sgb: e.activation(out=sgb[:, 0:nt], in_=g_ps[:, 0:nt], func=AF.Silu),
                          [g_ps], [sgb])
                    kb.op("dve", lambda e, u_ps=u_ps, sgb=sgb, fc=fc: e.tensor_tensor(
                        out=aT[:, fc, 0:nt], in0=u_ps[:, 0:nt], in1=sgb[:, 0:nt], op=ALU.mult), [u_ps, sgb], [aT])
                if PIPE and gi + 1 < len(groups):
                    prep_tr(gi + 1)
                if r != cur_r:
                    kb.dma("sp", gate[:], _bc(self.modd[l, r, (3 * s + 2) * D:(3 * s + 3) * D]), gate, True)
                    kb.op("dve", lambda e: e.tensor_scalar_mul(out=gate[:], in0=gate[:], scalar1=0.5), [gate], [gate])
                    cur_r = r
                for j, (row0, _) in enumerate(grp):
                    xt = xts[j]
                    z = zsl[j % 2]
                    for hh in range(2):
                        for fc in range(NFC):
                            kb.op("pe", lambda e, fc=fc, hh=hh, j=j: e.matmul(
                                py[hh][:, :], aT[:, fc, j * 128:(j + 1) * 128], wo[:, fc, hh * 512:(hh + 1) * 512],
                                start=(fc == 0), stop=(fc == NFC - 1)), [aT, wo], [py[hh]], inc=(fc == NFC - 1))
                    for hh in range(2):
                        kb.op("dve", lambda e, hh=hh, z=z: e.tensor_tensor(
                            out=z[:, hh * 512:(hh + 1) * 512], in0=py[hh][:, :], in1=gate[:, hh * 512:(hh + 1) * 512],
                            op=ALU.mult), [py[hh], gate], [z])
                    kb.op("dve", lambda e, z=z, xt=xt: e.scalar_tensor_tensor(
                        out=z[:], in0=xt[:], scalar=ALPHA, in1=z[:], op0=ALU.mult, op1=ALU.add), [xt, z], [z])
                    self.ln_affine_out(z, xt, gln, bln, tmp)
                    kb.dma("pool", dstap(row0), xt[:], xt, False)
        kb.barrier()


    def phase_proj(self, l):
        kb, nc, I = self.kb, self.nc, self.i
        L, LT = self.L, self.LT
        NF = 2304
        NTK = 1424
        with ExitStack() as cx:
            WF = kb.sb(cx, "WF", [128, 8, NF], BF16)
            WT = kb.sb(cx, "WT", [128, 8, NTK], BF16)
            c = self.load_ffn_consts(cx, l, 1, 1)
            stg = [kb.sb(cx, "wstg%d" % i, [128, INW], F32) for i in range(2)]
            ident = kb.sb(cx, "ident", [128, 128], BF16)
            epsc = kb.sb(cx, "epsc", [128, 1], F32)
            blk = kb.sb(cx, "blk64", [128, 128], F32)
            gq = kb.sb(cx, "gq", [128, 4], F32)
            gb = kb.sb(cx, "gateb", [128, 16], F32)

            def mk(n):
                R = dict(
                    xsl=[kb.sb(cx, "xsl%s%d" % (n, i), [128, D], F32) for i in range(2)],
                    hT=kb.sb(cx, "hT" + n, [128, 8, 512], BF16),
                    tmp=dict(st=kb.sb(cx, "st" + n, [128, 2, 6], F32), mv=kb.sb(cx, "mv" + n, [128, 2], F32),
                             rstd=kb.sb(cx, "rstd" + n, [128, 1], F32), xn=kb.sb(cx, "xn" + n, [128, D], BF16),
                             ident=ident, epsc=epsc, tp=kb.ps(cx, "tp" + n, [128, 8, 128], BF16)),
                    rope=[kb.sb(cx, "rope%s%d" % (n, j), [128, 512], F32) for j in range(4)],
                    t1=kb.sb(cx, "t1" + n, [128, 512], F32), t2=kb.sb(cx, "t2" + n, [128, 512], F32),
                    sq=kb.sb(cx, "sq" + n, [128, 512], F32), rs=kb.sb(cx, "rs" + n, [128, 512], F32),
                    ob=[kb.sb(cx, "ob%s%d" % (n, i), [128, 512], BF16) for i in range(2)],
                    avo=kb.sb(cx, "avo" + n, [128, 4, 65], BF16), cvo=kb.sb(cx, "cvo" + n, [128, 4, 65], BF16),
                    dvo=kb.sb(cx, "dvo" + n, [128, 2, 65], BF16), cko=kb.sb(cx, "cko" + n, [128, 512], F32),
                    buo=kb.sb(cx, "buo" + n, [128, 256], F32), cgo=kb.sb(cx, "cgo" + n, [128, 16], F32),
                    cgt=kb.sb(cx, "cgt" + n, [128, 16], F32),
                    pf=[kb.ps(cx, "pf%s%d" % (n, i), [128, 512], F32) for i in range(2)],
                    px=kb.ps(cx, "px" + n, [128, 512], F32),
                )
                return R
            RS = [mk("a"), mk("b")]
            tmp = RS[0]["tmp"]
            kb.dma("sp", tmp["ident"][:], I["ident"][:, :], tmp["ident"], True)
            kb.op("dve", lambda e: e.memset(tmp["epsc"][:], EPS), [], [tmp["epsc"]])
            kb.dma("sp", blk[:], I["blk64"][:, :], blk, True)
            kb.dma("sp", gb[:], _bc(I["ml_gate_b"][l, :]), gb, True)
            for hf in range(2):
                for ci, (nm, sw) in enumerate([("gqa_qnorm_g", 0), ("gqa_qnorm_g", 1), ("gqa_knorm_g", 0), ("gqa_knorm_g", 1)]):
                    for q in range(2):
                        so = (q * 32 + 32 * sw) % 64
                        kb.dma("sp", gq[hf * 64 + q * 32:hf * 64 + q * 32 + 32, ci:ci + 1],
                               I[nm][l, so:so + 32].rearrange("(p o) -> p o", o=1), gq, True)
            for R_ in RS:
                for b_ in (R_["avo"], R_["cvo"], R_["dvo"]):
                    kb.op("dve", lambda e, b_=b_: e.memset(b_[:], 1.0), [], [b_])
            ci = 0
            for k in range(8):
                sg_ = stg[k % 2]
                kb.dma("sp", sg_[:], I["w_in"][l, k * 128:(k + 1) * 128, :], sg_, True)
                ops = []
                ops.append((WF[:, k, 0:512], sg_[:, 0:512]))
                for a in range(2):
                    ops.append((WF[:, k, 512:1024].rearrange("p (b h d) -> p b h d", h=2, d=16)[:, :, a, :],
                                sg_[:, 0:512].rearrange("p (b h d) -> p b h d", h=2, d=16)[:, :, 1 - a, :]))
                ops.append((WF[:, k, 1024:1280].rearrange("p (c s d) -> p c s d", c=2, s=2),
                            sg_[:, 2064:2320].rearrange("p (s c d) -> p c s d", c=2, s=2)))
                ops.append((WF[:, k, 1280:1408], sg_[:, 2320:2448]))
                for a in range(2):
                    ops.append((WF[:, k, 1408:1664].rearrange("p (c s h d) -> p c s h d", c=2, s=2, h=2)[:, :, :, a, :],
                                sg_[:, 2064:2320].rearrange("p (s c h d) -> p c s h d", c=2, s=2, h=2)[:, :, :, 1 - a, :]))
                    ops.append((WF[:, k, 1664:1792].rearrange("p (c h d) -> p c h d", c=2, h=2)[:, :, a, :],
                                sg_[:, 2320:2448].rearrange("p (c h d) -> p c h d", c=2, h=2)[:, :, 1 - a, :]))
                ops.append((WF[:, k, 1792:2304], sg_[:, 1024:1536]))
                ops.append((WT[:, k, 0:256], sg_[:, 512:768]))
                ops.append((WT[:, k, 256:512], sg_[:, 1536:1792]))
                ops.append((WT[:, k, 512:768], sg_[:, 1280:1536]))
                ops.append((WT[:, k, 768:1024], sg_[:, 1792:2048]))
                ops.append((WT[:, k, 1024:1280], sg_[:, 768:1024]))
                ops.append((WT[:, k, 1280:1408], sg_[:, 2448:2576]))
                ops.append((WT[:, k, 1408:1424], sg_[:, 2048:2064]))
                for oi, (o_, i_) in enumerate(ops):
                    en = ["dve", "pool"][ci % 2]
                    ci += 1
                    dstb = WF if oi < len(ops) - 7 else WT
                    kb.op(en, lambda e, o_=o_, i_=i_: e.tensor_copy(out=o_, in_=i_), [sg_], [dstb])
            tl = self.tiles(True)
            groups = [tl[i:i + 4] for i in range(0, len(tl), 4)]

            def run_chain(cid, R):
                hT, tmpc, rp = R["hT"], R["tmp"], R["rope"]
                t1_, t2_, sq_, rs_ = R["t1"], R["t2"], R["sq"], R["rs"]
                pf, px = R["pf"], R["px"]
                avo, cvo, dvo, cko, buo, cgo, cgt = R["avo"], R["cvo"], R["dvo"], R["cko"], R["buo"], R["cgo"], R["cgt"]
                xi = 0
                obi = 0
                for gi in range(cid, len(groups), 2):
                    grp = groups[gi]
                    r = grp[0][1]
                    row0g = grp[0][0]
                    nt = len(grp) * 128
                    for j_, nm in enumerate(["ropeA_c", "ropeA_s", "ropeD_c", "ropeD_s"]):
                        kb.dma("sp", rp[j_][:, 0:nt], I[nm][:, row0g:row0g + nt], rp[j_], True)
                    for j, (row0, _) in enumerate(grp):
                        xt = R["xsl"][xi % 2]
                        xi += 1
                        kb.dma("sp", xt[:], self.xs[row0:row0 + 128, :], xt, True)
                        self.ln_part(xt, tmpc["xn"], tmpc)
                        yield
                        self.tr_part(tmpc["xn"], hT, j * 128, r, c, tmpc["tp"], ident)
                        yield

                    def fmm(cc, p_):
                        for k in range(8):
                            kb.op("pe", lambda e, k=k: e.matmul(
                                p_[:, 0:nt], WF[:, k, cc * 128:(cc + 1) * 128], hT[:, k, 0:nt],
                                start=(k == 0), stop=(k == 7)), [WF, hT], [p_], inc=(k == 7))
                        return p_

                    for (c1, c2, dstT) in [(0, 4, self.AqT), (1, 5, self.AqT), (2, 6, self.AkT), (3, 7, self.AkT)]:
                        p1 = fmm(c1, pf[0])
                        p2 = fmm(c2, pf[1])
                        yield
                        o_ = R["ob"][obi % 2]
                        obi += 1
                        kb.op("dve", lambda e: e.tensor_tensor(out=t1_[:, 0:nt], in0=p1[:, 0:nt], in1=rp[0][:, 0:nt], op=ALU.mult),
                              [p1, rp[0]], [t1_])
                        kb.op("dve", lambda e: e.tensor_tensor(out=t2_[:, 0:nt], in0=p2[:, 0:nt], in1=rp[1][:, 0:nt], op=ALU.mult),
                              [p2, rp[1]], [t2_])
                        kb.op("dve", lambda e: e.tensor_tensor(out=o_[:, 0:nt], in0=t1_[:, 0:nt], in1=t2_[:, 0:nt], op=ALU.add),
                              [t1_, t2_], [o_])
                        rr = (c1 % 2) * 128
                        kb.dma("pool", dstT[rr:rr + 128, row0g:row0g + nt], o_[:, 0:nt], o_, False)
                    for (c1, c2, dstT, rr, gc) in [(8, 11, self.DqT, 0, 0), (9, 12, self.DqT, 128, 0), (10, 13, self.DkT, 0, 2)]:
                        p1 = fmm(c1, pf[0])
                        p2 = fmm(c2, pf[1])
                        yield
                        o_ = R["ob"][obi % 2]
                        obi += 1
                        kb.op("act", lambda e: e.activation(out=sq_[:, 0:nt], in_=p1[:, 0:nt], func=AF.Square), [p1], [sq_])
                        yield
                        kb.op("pe", lambda e: e.matmul(px[:, 0:nt], blk[:, :], sq_[:, 0:nt], start=True, stop=True), [blk, sq_], [px])
                        kb.op("dve", lambda e: e.scalar_tensor_tensor(
                            out=t1_[:, 0:nt], in0=p1[:, 0:nt], scalar=gq[:, gc:gc + 1], in1=rp[2][:, 0:nt], op0=ALU.mult, op1=ALU.mult),
                            [p1, gq, rp[2]], [t1_])
                        kb.op("dve", lambda e: e.scalar_tensor_tensor(
                            out=t2_[:, 0:nt], in0=p2[:, 0:nt], scalar=gq[:, gc + 1:gc + 2], in1=rp[3][:, 0:nt], op0=ALU.mult, op1=ALU.mult),
                            [p2, gq, rp[3]], [t2_])
                        yield
                        kb.op("act", lambda e: e.activation(out=rs_[:, 0:nt], in_=px[:, 0:nt], func=AF.Sqrt,
                                                            bias=epsc[:, 0:1], scale=1.0 / 64), [px, epsc], [rs_])
                        kb.op("dve", lambda e: e.tensor_tensor(out=t1_[:, 0:nt], in0=t1_[:, 0:nt], in1=t2_[:, 0:nt], op=ALU.add),
                              [t1_, t2_], [t1_])
                        yield
                        kb.op("dve", lambda e: e.reciprocal(out=rs_[:, 0:nt], in_=rs_[:, 0:nt]), [rs_], [rs_])
                        kb.op("dve", lambda e: e.tensor_tensor(out=o_[:, 0:nt], in0=t1_[:, 0:nt], in1=rs_[:, 0:nt], op=ALU.mult),
                              [t1_, rs_], [o_])
                        kb.dma("pool", dstT[rr:rr + 128, row0g:row0g + nt], o_[:, 0:nt], o_, False)
                    for ci_, (c1, dstT, rr) in enumerate([(14, self.CqT, 0), (15, self.CqT, 128), (16, self.CkT, 0), (17, self.CkT, 128)]):
                        p1 = fmm(c1, pf[ci_ % 2])
                        yield
                        o_ = R["ob"][obi % 2]
                        obi += 1
                        kb.op("act", lambda e: e.copy(out=o_[:, 0:nt], in_=p1[:, 0:nt]), [p1], [o_])
                        kb.dma("pool", dstT[rr:rr + 128, row0g:row0g + nt], o_[:, 0:nt], o_, False)
                    for j, (row0, _) in enumerate(grp):
                        for tc, (c0, w) in enumerate([(0, 512), (512, 512), (1024, 400)]):
                            p_ = [pf[0], pf[1], px][tc]
                            for k in range(8):
                                kb.op("pe", lambda e, k=k: e.matmul(
                                    p_[:, 0:w], hT[:, k, j * 128:(j + 1) * 128], WT[:, k, c0:c0 + w],
                                    start=(k == 0), stop=(k == 7)), [hT, WT], [p_], inc=(k == 7))
                            yield
                            if tc == 0:
                                kb.op("act", lambda e: e.copy(out=avo[:, :, 0:64], in_=p_[:, 0:256].rearrange("p (h d) -> p h d", h=4)),
                                      [p_], [avo])
                                kb.op("dve", lambda e: e.tensor_copy(out=cvo[:, :, 0:64], in_=p_[:, 256:512].rearrange("p (h d) -> p h d", h=4)),
                                      [p_], [cvo])
                                kb.dma("pool", self.Av[row0:row0 + 128, :, :], avo[:], avo, False)
                                kb.dma("pool", self.Cv[row0:row0 + 128, :, :], cvo[:], cvo, False)
                            elif tc == 1:
                                kb.op("dve", lambda e: e.tensor_copy(out=cko[:, 0:256], in_=p_[:, 0:256]), [p_], [cko])
                                kb.op("act", lambda e: e.activation(out=cko[:, 256:512], in_=p_[:, 256:512], func=AF.Sigmoid), [p_], [cko])
                                kb.dma("pool", self.Ck[row0:row0 + 128, :], cko[:, 0:256], cko, False)
                                kb.dma("pool", self.Co[row0:row0 + 128, :], cko[:, 256:512], cko, False)
                            else:
                                kb.op("dve", lambda e: e.tensor_copy(out=buo[:], in_=p_[:, 0:256]), [p_], [buo])
                                kb.op("act", lambda e: e.copy(out=dvo[:, :, 0:64], in_=p_[:, 256:384].rearrange("p (h d) -> p h d", h=2)),
                                      [p_], [dvo])
                                kb.op("dve", lambda e: e.tensor_tensor(out=cgo[:], in0=p_[:, 384:400], in1=gb[:], op=ALU.add), [p_, gb], [cgo])
                                kb.op("act", lambda e: e.activation(out=cgt[:], in_=cgo[:], func=AF.Exp, scale=-1.0), [cgo], [cgt])
                                kb.op("dve", lambda e: e.tensor_scalar_add(out=cgt[:], in0=cgt[:], scalar1=1.0), [cgt], [cgt])
                                kb.op("act", lambda e: e.activation(out=cgt[:], in_=cgt[:], func=AF.Ln), [cgt], [cgt])
                                for q in (4, 12):
                                    kb.op("dve", lambda e, q=q: e.tensor_scalar_mul(out=cgo[:, q:q + 4], in0=cgt[:, q:q + 4], scalar1=-1.0),
                                          [cgt, cgo], [cgo])
                                kb.dma("pool", self.Bu[row0:row0 + 128, :], buo[:], buo, False)
                                kb.dma("pool", self.Dv[row0:row0 + 128, :, :], dvo[:], dvo, False)
                                kb.dma("pool", self.Cg[row0:row0 + 128, :], cgo[:], cgo, False)

            alive = [run_chain(0, RS[0]), run_chain(1, RS[1])]
            while alive:
                for g_ in list(alive):
                    try:
                        next(g_)
                    except StopIteration:
                        alive.remove(g_)
        kb.barrier()

    def phase_attn(self, l, with_ctx_q):
        import math
        kb, nc, I = self.kb, self.nc, self.i
        L, LT, Lc = self.L, self.LT, self.Lc
        NCH = LT // 128
        lam_init = 0.8 - 0.6 * math.exp(-0.3 * l)
        with ExitStack() as cx:
            AkT = kb.sb(cx, "AkTs", [128, 2, LT], BF16)
            Avs = kb.sb(cx, "Avs", [128, NCH, 260], BF16)
            DkT = kb.sb(cx, "DkTs", [128, LT], BF16)
            Dvs = kb.sb(cx, "Dvs", [128, NCH, 130], BF16)
            Aq = [kb.sb(cx, "Aqg%d" % i, [128, 2, 512], BF16) for i in range(2)]
            Dq = [kb.sb(cx, "Dqg%d" % i, [128, 2, 512], BF16) for i in range(2)]
            pT = [kb.sb(cx, "pT%d" % i, [128, 512], BF16) for i in range(4)]
            dl = kb.sb(cx, "dl", [128, 128], F32)
            dlp = kb.sb(cx, "dlp", [128, 2, 32], F32)
            lamt = kb.sb(cx, "lamt", [128, 2], F32)
            neglam = kb.sb(cx, "neglam", [128, 1], F32)
            gA = kb.sb(cx, "gA", [128, 256], F32)
            epsc = kb.sb(cx, "epsc", [128, 1], F32)
            rec = [kb.sb(cx, "rec%d" % i, [128, 4], F32) for i in range(2)]
            am = [kb.sb(cx, "am%d" % i, [128, 4, 64], F32) for i in range(2)]
            dsq = kb.sb(cx, "dsq", [128, 4, 64], F32)
            ssq = kb.sb(cx, "ssq", [128, 4], F32)
            amix = [kb.sb(cx, "amix%d" % i, [128, 4, 256], BF16) for i in range(2)]
            dmix = [kb.sb(cx, "dmix%d" % i, [128, 4, 256], BF16) for i in range(2)]
            psS = [kb.ps(cx, "psS%d" % i, [128, 512], F32) for i in range(4)]
            po = [kb.ps(cx, "po%d" % i, [128, 512], F32) for i in range(4)]
            for c_ in range(2):
                kb.dma("sp", AkT[:, c_, :], self.AkT[c_ * 128:(c_ + 1) * 128, :], AkT, True)
            kb.dma("sp", DkT[:], self.DkT[:, :], DkT, True)
            avv = self.Av.rearrange("(c p) h e -> p c (h e)", p=128)
            dvv = self.Dv.rearrange("(c p) h e -> p c (h e)", p=128)
            for c0 in range(0, NCH, 8):
                c1 = min(NCH, c0 + 8)
                kb.dma("sp", Avs[:, c0:c1, :], avv[:, c0:c1, :], Avs, True)
                kb.dma("sp", Dvs[:, c0:c1, :], dvv[:, c0:c1, :], Dvs, True)
            kb.dma("sp", dl[:], _bc(I["diff_lambda"][l, :]), dl, True)
            kb.dma("sp", gA[:], _bc(I["diff_norm_g"][l, :]), gA, True)
            kb.op("dve", lambda e: e.memset(epsc[:], EPS), [], [epsc])
            kb.op("dve", lambda e: e.tensor_scalar_mul(out=gA[:], in0=gA[:], scalar1=1.0 - lam_init), [gA], [gA])
            dlv = dl[:].rearrange("p (a b) -> p a b", a=4)
            kb.op("dve", lambda e: e.tensor_tensor(out=dlp[:, 0, :], in0=dlv[:, 0, :], in1=dlv[:, 1, :], op=ALU.mult), [dl], [dlp])
            kb.op("dve", lambda e: e.tensor_tensor(out=dlp[:, 1, :], in0=dlv[:, 2, :], in1=dlv[:, 3, :], op=ALU.mult), [dl], [dlp])
            kb.op("dve", lambda e: e.reduce_sum(out=lamt[:], in_=dlp[:], axis=AX.X), [dlp], [lamt])
            kb.op("act", lambda e: e.activation(out=lamt[:], in_=lamt[:], func=AF.Exp), [lamt], [lamt])
            kb.op("dve", lambda e: e.tensor_tensor(out=neglam[:], in0=lamt[:, 1:2], in1=lamt[:, 0:1], op=ALU.subtract), [lamt], [neglam])
            kb.op("dve", lambda e: e.tensor_scalar_add(out=neglam[:], in0=neglam[:], scalar1=-lam_init), [neglam], [neglam])

            cnt = dict(s=0, o=0)

            def pair_sweep(specs, nq, kcs, scale):
                Os = [po[(cnt["o"] % 2) * 2], po[(cnt["o"] % 2) * 2 + 1]]
                cnt["o"] += 1
                n = len(kcs)
                Ss = [None] * n

                def emitS(idx):
                    sl = cnt["s"] % 2
                    cnt["s"] += 1
                    Ss[idx] = []
                    for a in range(2):
                        S, P = psS[sl * 2 + a], pT[sl * 2 + a]
                        kT_fn, q_ap, v_fn, kTb, qb, vb = specs[a]
                        lhsT, kw = kT_fn(kcs[idx])
                        kb.op("pe", lambda e: e.matmul(S[:, 0:nq * 128], lhsT, q_ap, start=True, stop=True, **kw), [kTb, qb], [S])
                        Ss[idx].append((S, P))
                emitS(0)
                for idx in range(n):
                    if idx + 1 < n:
                        emitS(idx + 1)
                    for a in range(2):
                        S, P = Ss[idx][a]
                        kb.op("act", lambda e: e.activation(out=P[:, 0:nq * 128], in_=S[:, 0:nq * 128], func=AF.Exp, scale=scale), [S], [P])
                    for a in range(2):
                        S, P = Ss[idx][a]
                        v_fn, vb = specs[a][2], specs[a][5]
                        for j in range(nq):
                            kb.op("pe", lambda e, j=j: e.matmul(
                                Os[a][:, j * 65:(j + 1) * 65], P[:, j * 128:(j + 1) * 128], v_fn(kcs[idx]),
                                start=(idx == 0 and j == 0), stop=(idx == n - 1), skip_group_check=True),
                                [P, vb], [Os[a]], inc=(j == nq - 1))
                return Os

            qgroups = [(q0, 4, list(range(NCH))) for q0 in range(0, L, 512)]
            if with_ctx_q:
                qgroups.append((L, Lc // 128, list(range(L // 128, NCH))))
            for gi, (q0, nq, kcs) in enumerate(qgroups):
                aq, dq = Aq[gi % 2], Dq[gi % 2]
                nqt = nq * 128
                for c_ in range(2):
                    kb.dma("sp", aq[:, c_, 0:nqt], self.AqT[c_ * 128:(c_ + 1) * 128, q0:q0 + nqt], aq, True)
                    kb.dma("sp", dq[:, c_, 0:nqt], self.DqT[c_ * 128:(c_ + 1) * 128, q0:q0 + nqt], dq, True)
                amx, dmx = amix[gi % 2], dmix[gi % 2]
                for h in range(4):
                    c_ = h // 2
                    specs = []
                    for m in range(2):
                        base = (h % 2) * 64 + m * 32
                        kw = dict(tile_position=(96, 0)) if base == 96 else {}
                        specs.append((lambda kc, base=base, kw=kw: (AkT[base:base + 32, c_, kc * 128:(kc + 1) * 128], kw),
                                      aq[base:base + 32, c_, 0:nqt], lambda kc: Avs[:, kc, h * 65:(h + 1) * 65], AkT, aq, Avs))
                    Os = pair_sweep(specs, nq, kcs, 32 ** -0.5)
                    for m in range(2):
                        O = Os[m]
                        Ov = O[:, 0:260].rearrange("p (j e) -> p j e", e=65)
                        r_, a_ = rec[m], am[m]
                        kb.op("dve", lambda e: e.reciprocal(out=r_[:, 0:nq], in_=Ov[:, 0:nq, 64]), [O], [r_])
                        kb.op("dve", lambda e: e.tensor_tensor(out=a_[:, 0:nq, :], in0=Ov[:, 0:nq, 0:64],
                                                              in1=r_[:, 0:nq].unsqueeze(2).to_broadcast([128, nq, 64]), op=ALU.mult),
                              [O, r_], [a_])
                    kb.op("dve", lambda e: e.scalar_tensor_tensor(out=am[0][:, 0:nq, :], in0=am[1][:, 0:nq, :], scalar=neglam[:, 0:1],
                                                                 in1=am[0][:, 0:nq, :], op0=ALU.mult, op1=ALU.add),
                          [am[0], am[1], neglam], [am[0]])
                    kb.op("dve", lambda e: e.tensor_tensor(out=dsq[:, 0:nq, :], in0=am[0][:, 0:nq, :], in1=am[0][:, 0:nq, :], op=ALU.mult),
                          [am[0]], [dsq])
                    kb.op("dve", lambda e: e.reduce_sum(out=ssq[:, 0:nq], in_=dsq[:, 0:nq, :], axis=AX.X), [dsq], [ssq])
                    kb.op("act", lambda e: e.activation(out=ssq[:, 0:nq], in_=ssq[:, 0:nq], func=AF.Sqrt, bias=epsc[:, 0:1], scale=1.0 / 64),
                          [ssq, epsc], [ssq])
                    kb.op("dve", lambda e: e.reciprocal(out=ssq[:, 0:nq], in_=ssq[:, 0:nq]), [ssq], [ssq])
                    kb.op("dve", lambda e: e.tensor_tensor(out=am[0][:, 0:nq, :], in0=am[0][:, 0:nq, :],
                                                          in1=ssq[:, 0:nq].unsqueeze(2).to_broadcast([128, nq, 64]), op=ALU.mult),
                          [am[0], ssq], [am[0]])
                    kb.op("dve", lambda e, h=h: e.tensor_tensor(out=amx[:, 0:nq, h * 64:(h + 1) * 64], in0=am[0][:, 0:nq, :],
                                                               in1=gA[:, h * 64:(h + 1) * 64].unsqueeze(1).to_broadcast([128, nq, 64]), op=ALU.mult),
                          [am[0], gA], [amx])
                for c_ in range(2):
                    heads = [c_, c_ + 2]
                    specs = []
                    for h in heads:
                        kv = h // 2
                        specs.append((lambda kc, kv=kv: (DkT[kv * 64:(kv + 1) * 64, kc * 128:(kc + 1) * 128], {}),
                                      dq[kv * 64:(kv + 1) * 64, c_, 0:nqt], lambda kc, kv=kv: Dvs[:, kc, kv * 65:(kv + 1) * 65], DkT, dq, Dvs))
                    Os = pair_sweep(specs, nq, kcs, 64 ** -0.5)
                    for a, h in enumerate(heads):
                        O = Os[a]
                        Ov = O[:, 0:260].rearrange("p (j e) -> p j e", e=65)
                        r_ = rec[a]
                        kb.op("dve", lambda e: e.reciprocal(out=r_[:, 0:nq], in_=Ov[:, 0:nq, 64]), [O], [r_])
                        kb.op("dve", lambda e, h=h: e.tensor_tensor(out=dmx[:, 0:nq, h * 64:(h + 1) * 64], in0=Ov[:, 0:nq, 0:64],
                                                                   in1=r_[:, 0:nq].unsqueeze(2).to_broadcast([128, nq, 64]), op=ALU.mult),
                              [O, r_], [dmx])
                for j in range(nq):
                    kb.dma("pool", self.mix[q0 + j * 128:q0 + (j + 1) * 128, 0:256], amx[:, j, :], amx, False)
                    kb.dma("pool", self.mix[q0 + j * 128:q0 + (j + 1) * 128, 768:1024], dmx[:, j, :], dmx, False)
        kb.barrier()


    def phase_pool(self, l, with_ctx):
        kb, nc, I = self.kb, self.nc, self.i
        L, LT = self.L, self.LT
        with ExitStack() as cx:
            PM = kb.sb(cx, "PM", [128, 20, 128], F32)
            pwf = kb.sb(cx, "pwf", [64, 4, 64], F32)
            pw = kb.sb(cx, "pw", [64, 4, 64], BF16)
            psc = kb.sb(cx, "psc", [128, 256], F32)
            ut = [kb.sb(cx, "ut%d" % i, [128, 256], F32) for i in range(4)]
            dT = [kb.sb(cx, "dT%d" % i, [64, 512], BF16) for i in range(2)]
            yo = [kb.sb(cx, "yo%d" % i, [128, 256], BF16) for i in range(2)]
            pd = [kb.ps(cx, "pd%d" % i, [128, 512], F32) for i in range(2)]
            py = [kb.ps(cx, "pyb%d" % i, [128, 512], F32) for i in range(2)]
            kb.dma("sp", PM[:], I["poolM"][:, :, :], PM, True)
            kb.dma("sp", pwf[:], I["pool_w"][l].rearrange("g c e -> c g e"), pwf, True)
            kb.dma("sp", psc[:], _bc(I["pool_scale"][l, :]), psc, True)
            kb.op("dve", lambda e: e.tensor_copy(out=pw[:], in_=pwf[:]), [pwf], [pw])
            seqs = [(0, L // 128)]
            if with_ctx:
                seqs.append((L, self.Lc // 128))
            ui = 0
            ti = 0
            for (r0, n) in seqs:
                tiles = {}

                def get(t):
                    nonlocal ui
                    if t not in tiles:
                        u = ut[ui % 4]
                        ui += 1
                        kb.dma("sp", u[:], self.Bu[r0 + t * 128:r0 + (t + 1) * 128, :], u, True)
                        tiles[t] = u
                    return tiles[t]
                for t in range(n):
                    srcs = []
                    if t > 0:
                        srcs.append((get(t - 1), 0))
                    srcs.append((get(t), 3 if t == 0 else (4 if t == n - 1 else 1)))
                    if t < n - 1:
                        srcs.append((get(t + 1), 2))
                    tiles.pop(t - 2, None)
                    p_d, p_y = pd[ti % 2], py[ti % 2]
                    d_, y_ = dT[ti % 2], yo[ti % 2]
                    ti += 1
                    first = True
                    for g in range(4):
                        for si, (u, v) in enumerate(srcs):
                            kb.op("pe", lambda e, g=g, u=u, v=v, first=first: e.matmul(
                                p_d[0:64, g * 128:(g + 1) * 128], u[:, g * 64:(g + 1) * 64], PM[:, v * 4 + g, :],
                                start=first, stop=(si == len(srcs) - 1), skip_group_check=True),
                                [u, PM], [p_d], inc=(g == 3 and si == len(srcs) - 1))
                            first = False
                    kb.op("act", lambda e: e.copy(out=d_[:, :], in_=p_d[0:64, :]), [p_d], [d_])
                    for g in range(4):
                        kb.op("pe", lambda e, g=g: e.matmul(p_y[:, g * 64:(g + 1) * 64], d_[:, g * 128:(g + 1) * 128], pw[:, g, :],
                                                            start=True, stop=True, skip_group_check=True), [d_, pw], [p_y], inc=(g == 3))
                    kb.op("dve", lambda e: e.tensor_tensor(out=y_[:], in0=p_y[:, 0:256], in1=psc[:], op=ALU.mult), [p_y, psc], [y_])
                    kb.dma("pool", self.mix[r0 + t * 128:r0 + (t + 1) * 128, 256:512], y_[:], y_, False)
        kb.barrier()

    def phase_wout(self, l, with_ctx):
        kb, nc, I = self.kb, self.nc, self.i
        with ExitStack() as cx:
            wo = kb.sb(cx, "wout", [128, 8, D], BF16)
            stg = [kb.sb(cx, "wostg%d" % i, [128, D], F32) for i in range(2)]
            gate = kb.sb(cx, "gate5", [128, D], F32)
            gln = kb.sb(cx, "gln", [128, D], F32)
            bln = kb.sb(cx, "bln", [128, D], F32)
            ident = kb.sb(cx, "ident", [128, 128], BF16)
            epsc = kb.sb(cx, "epsc", [128, 1], F32)
            mt = [kb.sb(cx, "mt%d" % i, [128, D], BF16) for i in range(2)]
            mT = [kb.sb(cx, "mT%d" % i, [128, 8, 128], BF16) for i in range(2)]
            xsl = [kb.sb(cx, "xsl%d" % i, [128, D], F32) for i in range(3)]
            zsl = [kb.sb(cx, "zsl%d" % i, [128, D], F32) for i in range(2)]
            tmp = dict(st2=kb.sb(cx, "st2", [128, 2, 6], F32), mv2=kb.sb(cx, "mv2", [128, 2], F32),
                       rstd2=kb.sb(cx, "rstd2", [128, 1], F32), epsc=epsc)
            tp = [kb.ps(cx, "tpw%d" % i, [128, 8, 128], BF16) for i in range(2)]
            py = [kb.ps(cx, "pyw%d" % i, [128, 512], F32) for i in range(4)]
            kb.dma("sp", ident[:], I["ident"][:, :], ident, True)
            kb.op("dve", lambda e: e.memset(epsc[:], EPS), [], [epsc])
            kb.dma("sp", gln[:], _bc(I["ln_g"][l, 1, :]), gln, True)
            kb.dma("sp", bln[:], _bc(I["ln_b"][l, 1, :]), bln, True)
            for k in range(8):
                sg_ = stg[k % 2]
                kb.dma("sp", sg_[:], I["w_out"][l, k * 128:(k + 1) * 128, :], sg_, True)
                kb.op(["dve", "pool"][k % 2], lambda e, k=k, sg_=sg_: e.tensor_copy(out=wo[:, k, :], in_=sg_[:]), [sg_], [wo])
            cur_r = None
            for ti, (row0, r) in enumerate(self.tiles(with_ctx)):
                if r != cur_r:
                    kb.dma("sp", gate[:], _bc(self.modd[l, r, 5 * D:6 * D]), gate, True)
                    cur_r = r
                m_, mT_, xt, z = mt[ti % 2], mT[ti % 2], xsl[ti % 3], zsl[ti % 2]
                tp_ = tp[ti % 2]
                kb.dma("sp", m_[:], self.mix[row0:row0 + 128, :], m_, True)
                kb.dma("sp", xt[:], self.xs[row0:row0 + 128, :], xt, True)
                for k in range(8):
                    kb.op("pe", lambda e, k=k: e.transpose(tp_[:, k, :], m_[:, k * 128:(k + 1) * 128], ident[:]),
                          [m_, ident], [tp_], inc=(k == 7))
                kb.op("act", lambda e: e.copy(out=mT_[:], in_=tp_[:]), [tp_], [mT_])
                for hh in range(2):
                    p_ = py[(ti % 2) * 2 + hh]
                    for k in range(8):
                        kb.op("pe", lambda e, k=k, hh=hh, p_=p_: e.matmul(p_[:, :], mT_[:, k, :], wo[:, k, hh * 512:(hh + 1) * 512],
                                                                     start=(k == 0), stop=(k == 7)), [mT_, wo], [p_], inc=(k == 7))
                    kb.op("dve", lambda e, hh=hh, p_=p_: e.tensor_tensor(out=z[:, hh * 512:(hh + 1) * 512], in0=p_[:, :],
                                                                        in1=gate[:, hh * 512:(hh + 1) * 512], op=ALU.mult), [p_, gate], [z])
                kb.op("dve", lambda e: e.scalar_tensor_tensor(out=z[:], in0=xt[:], scalar=ALPHA, in1=z[:], op0=ALU.mult, op1=ALU.add),
                      [xt, z], [z])
                self.ln_affine_out(z, xt, gln, bln, tmp)
                kb.dma("pool", self.xs[row0:row0 + 128, :], xt[:], xt, False)
        kb.barrier()

    def phase_mlstm(self, l, with_ctx_out):
        kb, nc, I = self.kb, self.nc, self.i
        L, LT = self.L, self.LT
        NCH = LT // 128
        nlat = L // 128
        nctx = self.Lc // 128
        HM = [0, 2, 1, 3]
        with ExitStack() as cx:
            Hb = kb.sb(cx, "Hb", [128, NCH, 256], F32)
            Hbc = [Buf(Hb.t, "Hb%d" % i) for i in range(NCH)]
            CM = kb.sb(cx, "CM", [128, 9, 128], F32)
            bmask = kb.sb(cx, "bmask", [128, 130], F32)
            ones = kb.sb(cx, "ones", [128, 1], F32)
            epsc = kb.sb(cx, "epsc", [128, 1], F32)
            gC = kb.sb(cx, "gC", [128, 256], F32)
            kb.dma("sp", CM[:], I["cmat"][:, :, :], CM, True)
            kb.dma("sp", bmask[:], I["blkmask"][:, :], bmask, True)
            kb.dma("sp", gC[:], _bc(I["ml_norm_g"][l, :]), gC, True)
            kb.op("dve", lambda e: e.memset(ones[:], 1.0), [], [ones])
            kb.op("dve", lambda e: e.memset(epsc[:], EPS), [], [epsc])
            fwd_order = [nlat + i for i in range(nctx)] + list(range(nlat))
            bwd_order = [nlat + i for i in reversed(range(nctx))] + list(reversed(range(nlat)))
            orders = [fwd_order, bwd_order]
            step_of = [{ch: t for t, ch in enumerate(o)} for o in orders]

            def mk(dr):
                n = "f" if dr == 0 else "b"
                R = dict(
                    Cst=[kb.sb(cx, "Cst%s%d" % (n, i), [128, 130], F32) for i in range(2)],
                    Cbf=[kb.sb(cx, "Cbf%s%d" % (n, i), [128, 130], BF16) for i in range(2)],
                    qT=[kb.sb(cx, "qT%s%d" % (n, i), [128, 2, 128], BF16) for i in range(2)],
                    kT=[kb.sb(cx, "kT%s%d" % (n, i), [128, 2, 128], BF16) for i in range(2)],
                    va=[kb.sb(cx, "va%s%d" % (n, i), [128, 260], BF16) for i in range(2)],
                    kt=[kb.sb(cx, "kt%s%d" % (n, i), [128, 256], F32) for i in range(2)],
                    cg=[kb.sb(cx, "cg%s%d" % (n, i), [128, 16], F32) for i in range(2)],
                    so=[kb.sb(cx, "so%s%d" % (n, i), [128, 256], F32) for i in range(2)],
                    outb=[kb.sb(cx, "outb%s%d" % (n, i), [128, 256], BF16) for i in range(2)],
                    lfrep=kb.sb(cx, "lfrep" + n, [128, 4, 128], F32), linm=kb.sb(cx, "linm" + n, [128, 4, 128], F32),
                    lfrep2=kb.sb(cx, "lfrep2" + n, [128, 256], F32), DT=kb.sb(cx, "DT" + n, [128, 512], F32),
                    AT=kb.sb(cx, "AT" + n, [128, 512], BF16), ew=kb.sb(cx, "ew" + n, [128, 8], F32),
                    INs=kb.sb(cx, "INs" + n, [128, 4, 65], F32), tot=kb.sb(cx, "tot" + n, [128, 4, 65], F32),
                    den=kb.sb(cx, "den" + n, [128, 4], F32), K2=kb.sb(cx, "K2" + n, [128, 256], BF16),
                    dec=kb.sb(cx, "dec" + n, [128, 2], F32), tmpu=kb.sb(cx, "tmpu" + n, [128, 130], F32),
                    hs=kb.sb(cx, "hs" + n, [128, 256], F32), sq=kb.sb(cx, "sq" + n, [128, 256], F32),
                    ssq=kb.sb(cx, "ssq" + n, [128, 4], F32),
                    P0=kb.ps(cx, "P0" + n, [128, 512], F32), P1=kb.ps(cx, "P1" + n, [128, 512], F32),
                    P2=kb.ps(cx, "P2" + n, [128, 512], F32), P3=kb.ps(cx, "P3" + n, [128, 512], F32),
                )
                return R

            def run_dir(dr, R):
                order = orders[dr]
                Cst, Cbf = R["Cst"], R["Cbf"]
                lfrep, linm, lfrep2, DT, AT, ew = R["lfrep"], R["linm"], R["lfrep2"], R["DT"], R["AT"], R["ew"]
                INs, tot, den, K2, dec, tmpu, hs, sq, ssq = R["INs"], R["tot"], R["den"], R["K2"], R["dec"], R["tmpu"], R["hs"], R["sq"], R["ssq"]
                P0, P1, P2, P3 = R["P0"], R["P1"], R["P2"], R["P3"]
                for pr in range(2):
                    kb.op("dve", lambda e, pr=pr: e.memset(Cst[pr][:], 0.0), [], [Cst[pr]])
                    kb.op("dve", lambda e, pr=pr: e.memset(Cbf[pr][:], 0.0), [], [Cbf[pr]])
                Tm, nTm, Ts, nm = (0, 1, 2, 3) if dr == 0 else (4, 5, 6, 7)
                li0, lf0 = (0, 4) if dr == 0 else (8, 12)
                for t, ch in enumerate(order):
                    b_ = t % 2
                    r0 = ch * 128
                    q_, k_, v_, kt_, cg_, so_ = R["qT"][b_], R["kT"][b_], R["va"][b_], R["kt"][b_], R["cg"][b_], R["so"][b_]
                    need_out = ch < nlat or with_ctx_out
                    second = step_of[1 - dr][ch] < t
                    for c_ in range(2):
                        kb.dma("sp", q_[:, c_, :], self.CqT[c_ * 128:(c_ + 1) * 128, r0:r0 + 128], q_, True)
                        kb.dma("sp", k_[:, c_, :], self.CkT[c_ * 128:(c_ + 1) * 128, r0:r0 + 128], k_, True)
                    kb.dma("sp", v_[:], self.Cv[r0:r0 + 128, :, :].rearrange("p h e -> p (h e)"), v_, True)
                    kb.dma("sp", kt_[:], self.Ck[r0:r0 + 128, :], kt_, True)
                    kb.dma("sp", cg_[:], self.Cg[r0:r0 + 128, :], cg_, True)
                    if need_out and second:
                        kb.dma("sp", so_[:], self.Co[r0:r0 + 128, :], so_, True)
                    lf = cg_[:, lf0:lf0 + 4]
                    li = cg_[:, li0:li0 + 4]
                    yield
                    kb.op("dve", lambda e: e.tensor_copy(out=lfrep[:], in_=lf.unsqueeze(2).to_broadcast([128, 4, 128])), [cg_], [lfrep])
                    kb.op("dve", lambda e: e.tensor_copy(out=lfrep2[:].rearrange("p (h e) -> p h e", e=64),
                                                        in_=lf.unsqueeze(2).to_broadcast([128, 4, 64])), [cg_], [lfrep2])
                    kb.op("dve", lambda e: e.tensor_tensor(out=linm[:], in0=CM[:, nm, :].unsqueeze(1).to_broadcast([128, 4, 128]),
                                                          in1=li.unsqueeze(2).to_broadcast([128, 4, 128]), op=ALU.add), [CM, cg_], [linm])
                    kb.op("pe", lambda e: e.matmul(P3[:, 0:4], CM[:, Tm, :], lf, start=True, stop=True, skip_group_check=True), [CM, cg_], [P3], inc=False)
                    kb.op("pe", lambda e: e.matmul(P3[:, 4:8], CM[:, Ts, :], lf, start=False, stop=False, skip_group_check=True), [CM, cg_], [P3], inc=False)
                    kb.op("pe", lambda e: e.matmul(P3[:, 4:8], CM[:, 8, :], li, start=False, stop=True, skip_group_check=True), [CM, cg_], [P3])
                    for bi, h in enumerate(HM):
                        pb = (h % 2) * 64
                        STx = P1 if pb == 0 else P2
                        kb.op("pe", lambda e, h=h, pb=pb, STx=STx, bi=bi: e.matmul(
                            STx[:, (bi % 2) * 128:(bi % 2 + 1) * 128], k_[pb:pb + 64, h // 2, :], q_[pb:pb + 64, h // 2, :],
                            start=True, stop=True, skip_group_check=True), [k_, q_], [STx], inc=(bi % 2 == 1))
                    yield
                    for bi, h in enumerate(HM):
                        o_ = P0[:, bi * 128:(bi + 1) * 128]
                        kb.op("pe", lambda e, h=h, o_=o_: e.matmul(o_, lfrep[:, h, :], CM[:, Tm, :], start=(bi == 0), stop=False, skip_group_check=True),
                              [lfrep, CM], [P0], inc=False)
                        kb.op("pe", lambda e, h=h, o_=o_: e.matmul(o_, CM[:, nTm, :], lfrep[:, h, :], start=False, stop=False, skip_group_check=True),
                              [lfrep, CM], [P0], inc=False)
                        kb.op("pe", lambda e, h=h, o_=o_: e.matmul(o_, CM[:, 8, :], linm[:, h, :], start=False, stop=True, skip_group_check=True),
                              [linm, CM], [P0], inc=(bi == 3))
                    kb.op("act", lambda e: e.activation(out=ew[:], in_=P3[:, 0:8], func=AF.Exp), [P3], [ew])
                    yield
                    kb.op("act", lambda e: e.activation(out=DT[:], in_=P0[:, :], func=AF.Exp), [P0], [DT])
                    kb.op("dve", lambda e: e.tensor_tensor(out=K2[:].rearrange("p (h e) -> p h e", e=64),
                                                          in0=kt_[:].rearrange("p (h e) -> p h e", e=64),
                                                          in1=ew[:, 4:8].unsqueeze(2).to_broadcast([128, 4, 64]), op=ALU.mult), [kt_, ew], [K2])
                    for pr in range(2):
                        kb.op("pe", lambda e, pr=pr: e.matmul(P3[:, 8 + pr:9 + pr], lfrep2[:, pr * 128:(pr + 1) * 128], ones[:, 0:1],
                                                              start=True, stop=True, skip_group_check=True), [lfrep2, ones], [P3], inc=(pr == 1))
                    yield
                    kb.op("act", lambda e: e.activation(out=dec[:], in_=P3[:, 8:10], func=AF.Exp), [P3], [dec])
                    for half, STx in enumerate([P1, P2]):
                        kb.op("dve", lambda e, half=half, STx=STx: e.scalar_tensor_tensor(
                            out=AT[:, half * 256:(half + 1) * 256], in0=STx[:, 0:256], scalar=0.125, in1=DT[:, half * 256:(half + 1) * 256],
                            op0=ALU.mult, op1=ALU.mult), [STx, DT], [AT])
                    yield
                    for bi, h in enumerate(HM):
                        kb.op("pe", lambda e, h=h, bi=bi: e.matmul(P0[:, h * 65:(h + 1) * 65], AT[:, bi * 128:(bi + 1) * 128], v_[:, h * 65:(h + 1) * 65],
                                                                   start=True, stop=True, skip_group_check=True), [AT, v_], [P0], inc=(bi == 3))
                    for pr in range(2):
                        kb.op("pe", lambda e, pr=pr: e.matmul(P1[:, pr * 130:(pr + 1) * 130], q_[:, pr, :], Cbf[pr][:, :],
                                                              start=True, stop=True, skip_group_check=True), [q_, Cbf[pr]], [P1], inc=(pr == 1))
                    kb.op("pe", lambda e: e.matmul(P2[:, 0:130], K2[:, 0:128], v_[:, 0:130], start=True, stop=True), [K2, v_], [P2])
                    kb.op("pe", lambda e: e.matmul(P3[:, 0:130], K2[:, 128:256], v_[:, 130:260], start=True, stop=True), [K2, v_], [P3])
                    yield
                    INv = P1[:, 0:260].rearrange("p (h e) -> p h e", e=65)
                    NDv = P0[:, 0:260].rearrange("p (h e) -> p h e", e=65)
                    kb.op("dve", lambda e: e.scalar_tensor_tensor(out=INs[:], in0=INv, scalar=0.125,
                                                                 in1=ew[:, 0:4].unsqueeze(2).to_broadcast([128, 4, 65]), op0=ALU.mult, op1=ALU.mult),
                          [P1, ew], [INs])
                    kb.op("dve", lambda e: e.tensor_tensor(out=tot[:], in0=NDv, in1=INs[:], op=ALU.add), [P0, INs], [tot])
                    for pr, Pu in enumerate([P2, P3]):
                        kb.op("dve", lambda e, pr=pr, Pu=Pu: e.tensor_tensor(out=tmpu[:], in0=Pu[:, 0:130], in1=bmask[:], op=ALU.mult),
                              [Pu, bmask], [tmpu])
                        kb.op("dve", lambda e, pr=pr: e.scalar_tensor_tensor(out=Cst[pr][:], in0=Cst[pr][:], scalar=dec[:, pr:pr + 1], in1=tmpu[:],
                                                                            op0=ALU.mult, op1=ALU.add), [Cst[pr], dec, tmpu], [Cst[pr]])
                        kb.op("act", lambda e, pr=pr: e.copy(out=Cbf[pr][:], in_=Cst[pr][:]), [Cst[pr]], [Cbf[pr]])
                    yield
                    kb.op("dve", lambda e: e.tensor_scalar_mul(out=den[:], in0=tot[:, :, 64], scalar1=-1.0), [tot], [den])
                    kb.op("dve", lambda e: e.scalar_tensor_tensor(out=den[:], in0=tot[:, :, 64], scalar=1.0, in1=den[:], op0=ALU.max, op1=ALU.max),
                          [tot, den], [den])
                    kb.op("dve", lambda e: e.reciprocal(out=den[:], in_=den[:]), [den], [den])
                    dbc = den[:].unsqueeze(2).to_broadcast([128, 4, 64])
                    Hc = Hbc[ch]
                    if not need_out:
                        continue
                    if not second:
                        hv = Hb[:, ch, :].rearrange("p (h e) -> p h e", e=64)
                        kb.op("dve", lambda e: e.tensor_tensor(out=hv, in0=tot[:, :, 0:64], in1=dbc, op=ALU.mult), [tot, den], [Hc])
                    else:
                        hsv = hs[:].rearrange("p (h e) -> p h e", e=64)
                        kb.op("dve", lambda e: e.tensor_tensor(out=hsv, in0=tot[:, :, 0:64], in1=dbc, op=ALU.mult), [tot, den], [hs])
                        kb.op("dve", lambda e: e.tensor_tensor(out=hs[:], in0=hs[:], in1=Hb[:, ch, :], op=ALU.add), [hs, Hc], [hs])
                        kb.op("dve", lambda e: e.tensor_tensor(out=sq[:], in0=hs[:], in1=hs[:], op=ALU.mult), [hs], [sq])
                        kb.op("dve", lambda e: e.reduce_sum(out=ssq[:], in_=sq[:].rearrange("p (h e) -> p h e", e=64), axis=AX.X), [sq], [ssq])
                        yield
                        kb.op("act", lambda e: e.activation(out=ssq[:], in_=ssq[:], func=AF.Sqrt, bias=epsc[:, 0:1], scale=1.0 / 64),
                              [ssq, epsc], [ssq])
                        yield
                        kb.op("dve", lambda e: e.reciprocal(out=ssq[:], in_=ssq[:]), [ssq], [ssq])
                        kb.op("dve", lambda e: e.tensor_tensor(out=hsv, in0=hsv, in1=ssq[:].unsqueeze(2).to_broadcast([128, 4, 64]), op=ALU.mult),
                              [hs, ssq], [hs])
                        kb.op("dve", lambda e: e.tensor_tensor(out=hs[:], in0=hs[:], in1=gC[:], op=ALU.mult), [hs, gC], [hs])
                        ob_ = R["outb"][b_]
                        kb.op("dve", lambda e: e.tensor_tensor(out=ob_[:], in0=hs[:], in1=so_[:], op=ALU.mult), [hs, so_], [ob_])
                        kb.dma("pool", self.mix[r0:r0 + 128, 512:768], ob_[:], ob_, False)

            gens = [run_dir(0, mk(0)), run_dir(1, mk(1))]
            alive = list(gens)
            while alive:
                for g in list(alive):
                    try:
                        next(g)
                    except StopIteration:
                        alive.remove(g)
        kb.barrier()

    def finish(self):
        self.kb.es.close()
        return self.nc


def host_consts(L=8192, Lc=256):
    LT = L + Lc
    c = {}
    c["ident"] = np.eye(128, dtype=np.float32).astype(ml_dtypes.bfloat16)
    t = np.arange(L)
    row = (t // 64).astype(np.float32)
    col = (t % 64).astype(np.float32)
    def tables(dim):
        axis_dim = dim // 2
        inv = (10000.0 ** (-np.arange(0, axis_dim, 2, dtype=np.float32) / axis_dim)).astype(np.float32)
        ang = np.concatenate([row[:, None] * inv, col[:, None] * inv], axis=-1).astype(np.float32)
        cos = np.cos(ang).astype(np.float32)
        sin = np.sin(ang).astype(np.float32)
        half = dim // 2
        C = np.ones((128, LT), np.float32)
        S = np.zeros((128, LT), np.float32)
        for p in range(128):
            d = p % dim
            C[p, :L] = cos[:, d % half]
            S[p, :L] = (-1.0 if d < half else 1.0) * sin[:, d % half]
        return C, S
    c["ropeA_c"], c["ropeA_s"] = tables(32)
    c["ropeD_c"], c["ropeD_s"] = tables(64)
    p = np.arange(128)
    c["blk64"] = (p[:, None] // 64 == p[None, :] // 64).astype(np.float32)
    PM = np.zeros((128, 20, 128), np.float32)
    for g, w in enumerate((2, 4, 8, 16)):
        h = w // 2
        for v in range(5):
            M = np.zeros((128, 128), np.float32)
            for t_ in range(128):
                lo, hi = t_ - h, t_ + h
                cnt = float(w)
                if v == 3:
                    lo = max(lo, 0); cnt = float(hi - lo)
                if v == 4:
                    hi = min(hi, 128); cnt = float(hi - lo)
                for tp_ in range(lo, hi):
                    if v in (1, 3, 4):
                        src = tp_
                    elif v == 0:
                        src = tp_ + 128
                    else:
                        src = tp_ - 128
                    if v == 3 and tp_ >= 128:
                        continue
                    if v == 4 and tp_ < 0:
                        continue
                    if 0 <= src < 128:
                        M[src, t_] += 1.0 / cnt
                if v in (1, 3, 4):
                    M[t_, t_] -= 1.0
            PM[:, v * 4 + g, :] = M
    c["poolM"] = PM
    s_ = np.arange(128)[:, None]
    j_ = np.arange(128)[None, :]
    CM = np.zeros((128, 9, 128), np.float32)
    CM[:, 0] = (s_ <= j_); CM[:, 1] = -(s_ <= j_).astype(np.float32); CM[:, 2] = (s_ > j_)
    CM[:, 3] = np.where(s_ <= j_, 0.0, -30000.0)
    CM[:, 4] = (s_ >= j_); CM[:, 5] = -(s_ >= j_).astype(np.float32); CM[:, 6] = (s_ < j_)
    CM[:, 7] = np.where(s_ >= j_, 0.0, -30000.0)
    CM[:, 8] = np.eye(128)
    c["cmat"] = CM
    bm = np.zeros((128, 130), np.float32)
    bm[0:64, 0:65] = 1.0
    bm[64:128, 65:130] = 1.0
    c["blkmask"] = bm
    return c


def build_program(L=8192, Lc=256, NL=DEPTH, dbg=()):
    P = Prog(L, Lc, NL, dbg=dbg)
    P.phase_mod()
    for l in range(NL):
        last = l == NL - 1
        if l == 0:
            src_lat, src_ctx = P.i["x"], P.i["ctx"]
        else:
            src_lat, src_ctx = P.xs[0:L, :], P.xs[L:L + Lc, :]
        P.phase_ffn(l, 0, src_lat, src_ctx, P.xs[0:L, :], P.xs[L:L + Lc, :], True)
        P.phase_proj(l)
        P.phase_attn(l, not last)
        P.phase_pool(l, not last)
        P.phase_mlstm(l, not last)
        P.phase_wout(l, not last)
        P.phase_ffn(l, 2, P.xs[0:L, :], P.xs[L:L + Lc, :], P.out if last else P.xs[0:L, :], P.xs[L:L + Lc, :], not last)
    P.finish()
    return P


def make_in_maps(inputs, L, Lc, ncores):
    hc = host_consts(L, Lc)
    shared = {}
    for k in ["w_ada", "b_ada", "ln_g", "ln_b", "ffn1_wi", "ffn1_wo", "ffn2_wi", "ffn2_wo", "w_in", "w_out",
              "diff_norm_g", "pool_w", "pool_scale", "ml_norm_g", "gqa_qnorm_g", "gqa_knorm_g"]:
        shared[k] = np.ascontiguousarray(np.asarray(inputs[k], dtype=np.float32))
    shared["diff_lambda"] = np.ascontiguousarray(np.asarray(inputs["diff_lambda"], dtype=np.float32).reshape(DEPTH, 128))
    shared["ml_gate_b"] = np.ascontiguousarray(np.asarray(inputs["ml_gate_b"], dtype=np.float32).reshape(DEPTH, 16))
    shared.update(hc)
    x = np.asarray(inputs["x"], dtype=np.float32)
    ctx = np.asarray(inputs["ctx"], dtype=np.float32)
    c = np.asarray(inputs["c"], dtype=np.float32)
    c_ctx = np.asarray(inputs["c_ctx"], dtype=np.float32)
    maps = []
    for b in range(ncores):
        cv = np.zeros((128, 2, 8), np.float32)
        cv[:, 0, :] = c[b].reshape(8, 128).T
        cv[:, 1, :] = c_ctx.reshape(8, 128).T
        m = dict(shared)
        m["x"] = np.ascontiguousarray(x[b, :L])
        m["ctx"] = np.ascontiguousarray(ctx[b, :Lc])
        m["cvec"] = cv
        maps.append(m)
    return maps


def kernel(**inputs):
    L, Lc, B = 8192, 256, 8
    P = build_program(L, Lc)
    maps = make_in_maps(inputs, L, Lc, B)
    res = run_bass_kernel_spmd(P.nc, maps, core_ids=list(range(B)))
    out = np.stack([np.asarray(res.results[b]["out"]) for b in range(B)], axis=0)
    return out.astype(np.float32)
```

```python
import numpy as np
import ml_dtypes
from contextlib import ExitStack
import concourse.bass as bass
import concourse.mybir as mybir
from concourse.bass_utils import run_bass_kernel_spmd

F32 = mybir.dt.float32
BF16 = mybir.dt.bfloat16
AF = mybir.ActivationFunctionType
ALU = mybir.AluOpType
AX = mybir.AxisListType

D = 1024
DFF = 2816
NFC = DFF // 128
NMOD = 9
EPS = 1e-6
DEPTH = 2
ALPHA = (2.0 * DEPTH) ** 0.25
INW = 2576
PIPE_FFN = False
HALF_PIPE_FFN = True


class Sem:
    def __init__(self, h):
        self.h = h
        self.count = 0


class Buf:
    def __init__(self, t, name):
        self.t = t
        self.name = name
        self.w = {}
        self.r = {}
        self.dsem = None
        self.psum = False

    def __getitem__(self, k):
        return self.t[k]


class KB:
    def __init__(self, nc):
        self.nc = nc
        self.es = ExitStack()
        self.eng = {}
        for name, h in [("pe", nc.tensor), ("act", nc.scalar), ("dve", nc.vector),
                        ("pool", nc.gpsimd), ("sp", nc.sync)]:
            sem = Sem(self.es.enter_context(nc.semaphore("s_" + name)))
            self.eng[name] = (h, sem)
        self.known = {name: {} for name in self.eng}
        self.bar = Sem(self.es.enter_context(nc.semaphore("s_bar")))
        self.dfree = [Sem(self.es.enter_context(nc.semaphore("s_d%d" % i))) for i in range(72)]
        self.dall = list(self.dfree)
        self.dused = []
        self.n_ins = 0
        self.uid = 0

    def sb(self, ctx, name, shape, dt):
        self.uid += 1
        name = "%s_%d" % (name, self.uid)
        t = ctx.enter_context(self.nc.sbuf_tensor(name, list(shape), dt))
        return Buf(t, name)

    def ps(self, ctx, name, shape, dt):
        self.uid += 1
        name = "%s_%d" % (name, self.uid)
        t = ctx.enter_context(self.nc.psum_tensor(name, list(shape), dt))
        b = Buf(t, name)
        b.psum = True
        return b

    def _deps(self, reads, writes):
        d = {}
        for b in reads:
            for sm, v in b.w.items():
                if d.get(sm, 0) < v:
                    d[sm] = v
        for b in writes:
            for sm, v in b.w.items():
                if d.get(sm, 0) < v:
                    d[sm] = v
            for sm, v in b.r.items():
                if d.get(sm, 0) < v:
                    d[sm] = v
        return d

    def _wait(self, ename, d):
        h, _ = self.eng[ename]
        kn = self.known[ename]
        for sm, v in d.items():
            if kn.get(sm, 0) >= v:
                continue
            assert v <= sm.count, "dependency on a not-yet-issued increment"
            h.wait_ge(sm.h, v)
            kn[sm] = v
            self.n_ins += 1

    def op(self, ename, fn, reads=(), writes=(), inc=True):
        h, sem = self.eng[ename]
        d = self._deps(reads, writes)
        for b in reads:
            if b.psum:
                for sm, v in b.r.items():
                    if sm is not sem and d.get(sm, 0) < v:
                        d[sm] = v
        if ename == "pe":
            d.pop(sem, None)
        self._wait(ename, d)
        ins = fn(h)
        self.n_ins += 1
        if inc:
            ins.then_inc(sem.h, 1)
            sem.count += 1
            tick = sem.count
        else:
            tick = sem.count + 1
        for b in writes:
            b.w = {sem: tick}
            b.r = {}
        for b in reads:
            if b.r.get(sem, 0) < tick:
                b.r[sem] = tick
        return ins

    def dma(self, q, out, in_, sb, load, **kw):
        h, _ = self.eng[q]
        if sb.dsem is None:
            sb.dsem = self.dfree.pop()
            self.dused.append(sb)
        ds = sb.dsem
        if load:
            d = self._deps((), (sb,))
            if not sb.r and set(sb.w.keys()) == {ds}:
                d.pop(ds, None)
        else:
            d = self._deps((sb,), ())
        self._wait(q, d)
        ins = h.dma_start(out=out, in_=in_, **kw)
        self.n_ins += 1
        ins.then_inc(ds.h, 16)
        ds.count += 16
        tick = ds.count
        if load:
            sb.w = {ds: tick}
            sb.r = {}
        else:
            sb.r[ds] = tick
        return ins

    def barrier(self):
        h, _ = self.eng["sp"]
        allsems = [s for (_, s) in self.eng.values()] + self.dall
        kn = self.known["sp"]
        for sm in allsems:
            if kn.get(sm, 0) < sm.count:
                h.wait_ge(sm.h, sm.count)
                kn[sm] = sm.count
        h.sem_inc(self.bar.h, 1)
        self.bar.count += 1
        for name, (eh, _) in self.eng.items():
            if name != "sp":
                eh.wait_ge(self.bar.h, self.bar.count)
            self.known[name] = {sm: sm.count for sm in allsems}
        for b in self.dused:
            self.dfree.append(b.dsem)
            b.dsem = None
        self.dused = []


def _bc(ap, n=128):
    return ap.partition_broadcast(n)


class Prog:
    def __init__(self, L, Lc, NL, dbg=()):
        self.L, self.Lc, self.NL = L, Lc, NL
        self.LT = L + Lc
        self.dbg = set(dbg)
        nc = bass.Bass("TRN2", target_bir_lowering=False)
        self.nc = nc
        self.kb = KB(nc)
        LT = self.LT

        def din(name, shape, dt=F32):
            return nc.dram_tensor(name, list(shape), dt, kind="ExternalInput").ap()

        def dsc(name, shape, dt=F32):
            kind = "ExternalOutput" if name in self.dbg else "Internal"
            return nc.dram_tensor(name, list(shape), dt, kind=kind).ap()

        self.i = dict(
            x=din("x", [L, D]), ctx=din("ctx", [Lc, D]), cvec=din("cvec", [128, 2, 8]),
            w_ada=din("w_ada", [NL, D, NMOD * D]), b_ada=din("b_ada", [NL, NMOD * D]),
            ln_g=din("ln_g", [NL, 3, D]), ln_b=din("ln_b", [NL, 3, D]),
            ffn1_wi=din("ffn1_wi", [NL, D, 2 * DFF]), ffn1_wo=din("ffn1_wo", [NL, DFF, D]),
            ffn2_wi=din("ffn2_wi", [NL, D, 2 * DFF]), ffn2_wo=din("ffn2_wo", [NL, DFF, D]),
            ident=din("ident", [128, 128], BF16),
        )
        self.i.update(
            w_in=din("w_in", [NL, D, INW]), w_out=din("w_out", [NL, D, D]),
            diff_lambda=din("diff_lambda", [NL, 128]), diff_norm_g=din("diff_norm_g", [NL, 256]),
            pool_w=din("pool_w", [NL, 4, 64, 64]), pool_scale=din("pool_scale", [NL, 256]),
            ml_gate_b=din("ml_gate_b", [NL, 16]), ml_norm_g=din("ml_norm_g", [NL, 256]),
            gqa_qnorm_g=din("gqa_qnorm_g", [NL, 64]), gqa_knorm_g=din("gqa_knorm_g", [NL, 64]),
            ropeA_c=din("ropeA_c", [128, LT]), ropeA_s=din("ropeA_s", [128, LT]),
            ropeD_c=din("ropeD_c", [128, LT]), ropeD_s=din("ropeD_s", [128, LT]),
            blk64=din("blk64", [128, 128]), poolM=din("poolM", [128, 20, 128]),
            cmat=din("cmat", [128, 9, 128]), blkmask=din("blkmask", [128, 130]),
        )
        self.out = nc.dram_tensor("out", [L, D], F32, kind="ExternalOutput").ap()
        self.xs = dsc("xs", [LT, D])
        self.modd = dsc("modd", [NL, 2, NMOD * D])
        self.AqT = dsc("AqT", [256, LT], BF16); self.AkT = dsc("AkT", [256, LT], BF16)
        self.Av = dsc("Av", [LT, 4, 65], BF16)
        self.DqT = dsc("DqT", [256, LT], BF16); self.DkT = dsc("DkT", [128, LT], BF16)
        self.Dv = dsc("Dv", [LT, 2, 65], BF16)
        self.CqT = dsc("CqT", [256, LT], BF16); self.CkT = dsc("CkT", [256, LT], BF16)
        self.Cv = dsc("Cv", [LT, 4, 65], BF16)
        self.Ck = dsc("Ck", [LT, 256]); self.Co = dsc("Co", [LT, 256]); self.Cg = dsc("Cg", [LT, 16])
        self.Bu = dsc("Bu", [LT, 256])
        self.mix = dsc("mix", [LT, D], BF16)

    def phase_mod(self):
        kb, nc, I = self.kb, self.nc, self.i
        NL = self.NL
        CG = 3072
        with ExitStack() as cx:
            cv = kb.sb(cx, "cv", [128, 2, 8], F32)
            sv = kb.sb(cx, "sv", [128, 2, 8], F32)
            wbuf = [kb.sb(cx, "wada%d" % i, [128, CG], F32) for i in range(3)]
            bb = kb.sb(cx, "bada", [2, CG], F32)
            ob = [kb.sb(cx, "modo%d" % i, [2, CG], F32) for i in range(2)]
            pss = [kb.ps(cx, "modps%d" % i, [128, 512], F32) for i in range(6)]
            kb.dma("sp", cv[:], I["cvec"][:, :, :], cv, True)
            kb.op("act", lambda e: e.activation(out=sv[:], in_=cv[:], func=AF.Sigmoid), [cv], [sv])
            kb.op("dve", lambda e: e.tensor_tensor(out=sv[:], in0=sv[:], in1=cv[:], op=ALU.mult), [sv, cv], [sv])
            it = 0
            for l in range(NL):
                for cg in range(NMOD * D // CG):
                    for k in range(8):
                        wb = wbuf[it % 3]
                        it += 1
                        kb.dma("sp", wb[:], I["w_ada"][l, k * 128:(k + 1) * 128, cg * CG:(cg + 1) * CG], wb, True)
                        for j in range(6):
                            kb.op("pe", lambda e, j=j, k=k, wb=wb: e.matmul(
                                pss[j][0:2, :], sv[:, :, k], wb[:, j * 512:(j + 1) * 512],
                                start=(k == 0), stop=(k == 7)), [sv, wb], [pss[j]], inc=(j == 5))
                    o = ob[(l * 3 + cg) % 2]
                    kb.dma("sp", bb[:], _bc(I["b_ada"][l, cg * CG:(cg + 1) * CG], 2), bb, True)
                    for j in range(6):
                        kb.op("dve", lambda e, j=j, o=o: e.tensor_tensor(
                            out=o[:, j * 512:(j + 1) * 512], in0=pss[j][0:2, :], in1=bb[:, j * 512:(j + 1) * 512],
                            op=ALU.add), [pss[j], bb], [o])
                    kb.dma("pool", self.modd[l, :, cg * CG:(cg + 1) * CG], o[:], o, False)
        kb.barrier()

    def load_ffn_consts(self, cx, l, s, si):
        kb, I = self.kb, self.i
        c = {}
        c["sc1"] = kb.sb(cx, "sc1", [128, 2, 8], F32)
        c["sh"] = kb.sb(cx, "sh", [128, 2, 8], F32)
        for r in range(2):
            src1 = self.modd[l, r, (3 * s + 1) * D:(3 * s + 2) * D].rearrange("(k p) -> p k", p=128)
            src0 = self.modd[l, r, (3 * s) * D:(3 * s + 1) * D].rearrange("(k p) -> p k", p=128)
            kb.dma("sp", c["sc1"][:, r, :], src1, c["sc1"], True, allow_slow_non_contiguous=True)
            kb.dma("sp", c["sh"][:, r, :], src0, c["sh"], True, allow_slow_non_contiguous=True)
        kb.op("dve", lambda e: e.tensor_scalar_add(out=c["sc1"][:], in0=c["sc1"][:], scalar1=1.0), [c["sc1"]], [c["sc1"]])
        return c

    def ln_part(self, xt, xn, tmp):
        kb = self.kb
        st, mv, rstd = tmp["st"], tmp["mv"], tmp["rstd"]
        for hh in range(2):
            kb.op("dve", lambda e, hh=hh: e.bn_stats(out=st[:, hh, :], in_=xt[:, hh * 512:(hh + 1) * 512]), [xt], [st])
        kb.op("dve", lambda e: e.bn_aggr(out=mv[:], in_=st[:]), [st], [mv])
        kb.op("act", lambda e: e.activation(out=rstd[:], in_=mv[:, 1:2], func=AF.Sqrt, bias=tmp["epsc"][:, 0:1], scale=1.0),
              [mv, tmp["epsc"]], [rstd])
        kb.op("dve", lambda e: e.reciprocal(out=rstd[:], in_=rstd[:]), [rstd], [rstd])
        kb.op("dve", lambda e: e.tensor_scalar(out=xn[:], in0=xt[:], scalar1=mv[:, 0:1], scalar2=rstd[:, 0:1],
                                              op0=ALU.subtract, op1=ALU.mult), [xt, mv, rstd], [xn])

    def tr_part(self, xn, hT, col0, r, c, tp, ident):
        kb = self.kb
        for k in range(8):
            kb.op("pe", lambda e, k=k: e.transpose(tp[:, k, :], xn[:, k * 128:(k + 1) * 128], ident[:]),
                  [xn, ident], [tp], inc=(k == 7))
        for k in range(8):
            kb.op("act", lambda e, k=k: e.activation(out=hT[:, k, col0:col0 + 128], in_=tp[:, k, :], func=AF.Identity,
                                                    scale=c["sc1"][:, r, k:k + 1], bias=c["sh"][:, r, k:k + 1]),
                  [tp, c["sc1"], c["sh"]], [hT])

    def ln_to_hT(self, xt, hT, col0, r, c, tmp):
        self.ln_part(xt, tmp["xn"], tmp)
        self.tr_part(tmp["xn"], hT, col0, r, c, tmp["tp"], tmp["ident"])

    def ln_affine_out(self, z, dst, g, b, tmp):
        kb = self.kb
        st, mv, rstd = tmp["st2"], tmp["mv2"], tmp["rstd2"]
        for hh in range(2):
            kb.op("dve", lambda e, hh=hh: e.bn_stats(out=st[:, hh, :], in_=z[:, hh * 512:(hh + 1) * 512]), [z], [st])
        kb.op("dve", lambda e: e.bn_aggr(out=mv[:], in_=st[:]), [st], [mv])
        kb.op("act", lambda e: e.activation(out=rstd[:], in_=mv[:, 1:2], func=AF.Sqrt, bias=tmp["epsc"][:, 0:1], scale=1.0),
              [mv, tmp["epsc"]], [rstd])
        kb.op("dve", lambda e: e.reciprocal(out=rstd[:], in_=rstd[:]), [rstd], [rstd])
        kb.op("dve", lambda e: e.tensor_scalar(out=z[:], in0=z[:], scalar1=mv[:, 0:1], scalar2=rstd[:, 0:1],
                                              op0=ALU.subtract, op1=ALU.mult), [z, mv, rstd], [z])
        kb.op("dve", lambda e: e.tensor_tensor(out=z[:], in0=z[:], in1=g[:], op=ALU.mult), [z, g], [z])
        kb.op("dve", lambda e: e.tensor_tensor(out=dst[:], in0=z[:], in1=b[:], op=ALU.add), [z, b], [dst])

    def tiles(self, with_ctx):
        t = [(i * 128, 0) for i in range(self.L // 128)]
        if with_ctx:
            t += [(self.L + i * 128, 1) for i in range(self.Lc // 128)]
        return t

    def phase_ffn(self, l, s, src_lat, src_ctx, dst_lat, dst_ctx, with_ctx):
        kb, nc, I = self.kb, self.nc, self.i
        L = self.L
        si = 0 if s == 0 else 2
        wi_d = I["ffn1_wi" if s == 0 else "ffn2_wi"]
        wo_d = I["ffn1_wo" if s == 0 else "ffn2_wo"]
        NT = 256

        def srcap(row0):
            return src_lat[row0:row0 + 128, :] if row0 < L else src_ctx[row0 - L:row0 - L + 128, :]

        def dstap(row0):
            return dst_lat[row0:row0 + 128, :] if row0 < L else dst_ctx[row0 - L:row0 - L + 128, :]

        with ExitStack() as cx:
            wi = kb.sb(cx, "wi", [128, 8, 2 * DFF], BF16)
            wo = kb.sb(cx, "wo", [128, NFC, D], BF16)
            c = self.load_ffn_consts(cx, l, s, si)
            gate = kb.sb(cx, "gate", [128, D], F32)
            gln = kb.sb(cx, "gln", [128, D], F32)
            bln = kb.sb(cx, "bln", [128, D], F32)
            xsl = [kb.sb(cx, "xsl%d" % i, [128, D], F32) for i in range(4)]
            zsl = [kb.sb(cx, "zsl%d" % i, [128, D], F32) for i in range(2)]
            hTs = [kb.sb(cx, "hT%d" % i, [128, 8, NT], BF16) for i in range(2)]
            xnb = [kb.sb(cx, "xnb%d" % i, [128, D], BF16) for i in range(NT // 128)]
            aT = kb.sb(cx, "aT", [128, NFC, NT], BF16)
            sg = [kb.sb(cx, "sg%d" % i, [128, NT], F32) for i in range(2)]
            tmp = dict(
                st=kb.sb(cx, "st", [128, 2, 6], F32), mv=kb.sb(cx, "mv", [128, 2], F32),
                rstd=kb.sb(cx, "rstd", [128, 1], F32),
                st2=kb.sb(cx, "st2", [128, 2, 6], F32), mv2=kb.sb(cx, "mv2", [128, 2], F32),
                rstd2=kb.sb(cx, "rstd2", [128, 1], F32),
                ident=kb.sb(cx, "ident", [128, 128], BF16),
                epsc=kb.sb(cx, "epsc", [128, 1], F32),
            )
            tps = [kb.ps(cx, "tp%d" % i, [128, 8, 128], BF16) for i in range(2)]
            pg = [kb.ps(cx, "pg%d" % i, [128, 512], F32) for i in range(2)]
            pu = [kb.ps(cx, "pu%d" % i, [128, 512], F32) for i in range(2)]
            py = [kb.ps(cx, "py%d" % i, [128, 512], F32) for i in range(2)]
            kb.dma("sp", tmp["ident"][:], I["ident"][:, :], tmp["ident"], True)
            kb.op("dve", lambda e: e.memset(tmp["epsc"][:], EPS), [], [tmp["epsc"]])
            kb.dma("sp", gln[:], _bc(I["ln_g"][l, si, :]), gln, True)
            kb.dma("sp", bln[:], _bc(I["ln_b"][l, si, :]), bln, True)
            stage = xsl + zsl
            pieces = []
            for k in range(8):
                for c0 in range(0, 2 * DFF, 1024):
                    w = min(1024, 2 * DFF - c0)
                    pieces.append((wi_d[l, k * 128:(k + 1) * 128, c0:c0 + w], wi, (k, c0, w)))
            for fc in range(NFC):
                pieces.append((wo_d[l, fc * 128:(fc + 1) * 128, :], wo, (fc, 0, 1024)))
            cast_eng = ["dve", "act", "pool"]
            for n, (src, dstb, (a, c0, w)) in enumerate(pieces):
                sbuf = stage[n % len(stage)]
                kb.dma("sp", sbuf[:, 0:w], src, sbuf, True)
                en = cast_eng[n % 3]
                if en == "act":
                    kb.op("act", lambda e, sbuf=sbuf, dstb=dstb, a=a, c0=c0, w=w: e.copy(
                        out=dstb[:, a, c0:c0 + w], in_=sbuf[:, 0:w]), [sbuf], [dstb])
                else:
                    kb.op(en, lambda e, sbuf=sbuf, dstb=dstb, a=a, c0=c0, w=w: e.tensor_copy(
                        out=dstb[:, a, c0:c0 + w], in_=sbuf[:, 0:w]), [sbuf], [dstb])
            tl = self.tiles(with_ctx)
            groups = [tl[i:i + NT // 128] for i in range(0, len(tl), NT // 128)]
            cur_r = None
            xi = 0
            xts_of = {}

            def prep_ln(gi):
                nonlocal xi
                xts = []
                for j, (row0, _) in enumerate(groups[gi]):
                    xt = xsl[xi % 4]
                    xi += 1
                    xts.append(xt)
                    kb.dma("sp", xt[:], srcap(row0), xt, True)
                    self.ln_part(xt, xnb[j], tmp)
                xts_of[gi] = xts

            def prep_tr(gi):
                r_ = groups[gi][0][1]
                for j in range(len(groups[gi])):
                    self.tr_part(xnb[j], hTs[gi % 2], j * 128, r_, c, tps[j % 2], tmp["ident"])

            PIPE = PIPE_FFN
            if PIPE:
                prep_ln(0)
                prep_tr(0)
            for gi, grp in enumerate(groups):
                r = grp[0][1]
                assert all(t[1] == r for t in grp)
                hT = hTs[gi % 2]
                nt = len(grp) * 128
                if not PIPE:
                    if HALF_PIPE_FFN:
                        if gi == 0:
                            prep_ln(0)
                        prep_tr(gi)
                    else:
                        prep_ln(gi)
                        prep_tr(gi)
                xts = xts_of.pop(gi)
                if PIPE and gi + 1 < len(groups):
                    prep_ln(gi + 1)
                for fc in range(NFC):
                    g_ps, u_ps = pg[fc % 2], pu[fc % 2]
                    for k in range(8):
                        kb.op("pe", lambda e, k=k, fc=fc, g_ps=g_ps: e.matmul(
                            g_ps[:, 0:nt], wi[:, k, fc * 128:(fc + 1) * 128], hT[:, k, 0:nt],
                            start=(k == 0), stop=(k == 7)), [wi, hT], [g_ps], inc=False)
                    for k in range(8):
                        kb.op("pe", lambda e, k=k, fc=fc, u_ps=u_ps: e.matmul(
                            u_ps[:, 0:nt], wi[:, k, DFF + fc * 128:DFF + (fc + 1) * 128], hT[:, k, 0:nt],
                            start=(k == 0), stop=(k == 7)), [wi, hT], [u_ps], inc=(k == 7))
                    sgb = sg[fc % 2]
                    kb.op("act", lambda e, g_ps=g_ps, sgb=sgb: e.activation(out=sgb[:, 0:nt], in_=g_ps[:, 0:nt], func=AF.Silu),
                          [g_ps], [sgb])
                    kb.op("dve", lambda e, u_ps=u_ps, sgb=sgb, fc=fc: e.tensor_tensor(
                        out=aT[:, fc, 0:nt], in0=u_ps[:, 0:nt], in1=sgb[:, 0:nt], op=ALU.mult), [u_ps, sgb], [aT])
                if (not PIPE) and HALF_PIPE_FFN and gi + 1 < len(groups):
                    prep_ln(gi + 1)
                if PIPE and gi + 1 < len(groups):
                    prep_tr(gi + 1)
                if r != cur_r:
                    kb.dma("sp", gate[:], _bc(self.modd[l, r, (3 * s + 2) * D:(3 * s + 3) * D]), gate, True)
                    kb.op("dve", lambda e: e.tensor_scalar_mul(out=gate[:], in0=gate[:], scalar1=0.5), [gate], [gate])
                    cur_r = r
                for j, (row0, _) in enumerate(grp):
                    xt = xts[j]
                    z = zsl[j % 2]
                    for hh in range(2):
                        for fc in range(NFC):
                            kb.op("pe", lambda e, fc=fc, hh=hh, j=j: e.matmul(
                                py[hh][:, :], aT[:, fc, j * 128:(j + 1) * 128], wo[:, fc, hh * 512:(hh + 1) * 512],
                                start=(fc == 0), stop=(fc == NFC - 1)), [aT, wo], [py[hh]], inc=(fc == NFC - 1))
                    for hh in range(2):
                        kb.op("dve", lambda e, hh=hh, z=z: e.tensor_tensor(
                            out=z[:, hh * 512:(hh + 1) * 512], in0=py[hh][:, :], in1=gate[:, hh * 512:(hh + 1) * 512],
                            op=ALU.mult), [py[hh], gate], [z])
                    kb.op("dve", lambda e, z=z, xt=xt: e.scalar_tensor_tensor(
                        out=z[:], in0=xt[:], scalar=ALPHA, in1=z[:], op0=ALU.mult, op1=ALU.add), [xt, z], [z])
                    self.ln_affine_out(z, xt, gln, bln, tmp)
                    kb.dma("pool", dstap(row0), xt[:], xt, False)
        kb.barrier()


    def phase_proj(self, l):
        kb, nc, I = self.kb, self.nc, self.i
        L, LT = self.L, self.LT
        NF = 2304
        NTK = 1424
        with ExitStack() as cx:
            WF = kb.sb(cx, "WF", [128, 8, NF], BF16)
            WT = kb.sb(cx, "WT", [128, 8, NTK], BF16)
            c = self.load_ffn_consts(cx, l, 1, 1)
            stg = [kb.sb(cx, "wstg%d" % i, [128, INW], F32) for i in range(2)]
            ident = kb.sb(cx, "ident", [128, 128], BF16)
            epsc = kb.sb(cx, "epsc", [128, 1], F32)
            blk = kb.sb(cx, "blk64", [128, 128], F32)
            gq = kb.sb(cx, "gq", [128, 4], F32)
            gb = kb.sb(cx, "gateb", [128, 16], F32)

            def mk(n):
                R = dict(
                    xsl=[kb.sb(cx, "xsl%s%d" % (n, i), [128, D], F32) for i in range(2)],
                    hT=kb.sb(cx, "hT" + n, [128, 8, 512], BF16),
                    tmp=dict(st=kb.sb(cx, "st" + n, [128, 2, 6], F32), mv=kb.sb(cx, "mv" + n, [128, 2], F32),
                             rstd=kb.sb(cx, "rstd" + n, [128, 1], F32), xn=kb.sb(cx, "xn" + n, [128, D], BF16),
                             ident=ident, epsc=epsc, tp=kb.ps(cx, "tp" + n, [128, 8, 128], BF16)),
                    rope=[kb.sb(cx, "rope%s%d" % (n, j), [128, 512], F32) for j in range(4)],
                    t1=kb.sb(cx, "t1" + n, [128, 512], F32), t2=kb.sb(cx, "t2" + n, [128, 512], F32),
                    sq=kb.sb(cx, "sq" + n, [128, 512], F32), rs=kb.sb(cx, "rs" + n, [128, 512], F32),
                    ob=[kb.sb(cx, "ob%s%d" % (n, i), [128, 512], BF16) for i in range(2)],
                    avo=kb.sb(cx, "avo" + n, [128, 4, 65], BF16), cvo=kb.sb(cx, "cvo" + n, [128, 4, 65], BF16),
                    dvo=kb.sb(cx, "dvo" + n, [128, 2, 65], BF16), cko=kb.sb(cx, "cko" + n, [128, 512], F32),
                    buo=kb.sb(cx, "buo" + n, [128, 256], F32), cgo=kb.sb(cx, "cgo" + n, [128, 16], F32),
                    cgt=kb.sb(cx, "cgt" + n, [128, 16], F32),
                    pf=[kb.ps(cx, "pf%s%d" % (n, i), [128, 512], F32) for i in range(2)],
                    px=kb.ps(cx, "px" + n, [128, 512], F32),
                )
                return R
            RS = [mk("a"), mk("b")]
            tmp = RS[0]["tmp"]
            kb.dma("sp", tmp["ident"][:], I["ident"][:, :], tmp["ident"], True)
            kb.op("dve", lambda e: e.memset(tmp["epsc"][:], EPS), [], [tmp["epsc"]])
            kb.dma("sp", blk[:], I["blk64"][:, :], blk, True)
            kb.dma("sp", gb[:], _bc(I["ml_gate_b"][l, :]), gb, True)
            for hf in range(2):
                for ci, (nm, sw) in enumerate([("gqa_qnorm_g", 0), ("gqa_qnorm_g", 1), ("gqa_knorm_g", 0), ("gqa_knorm_g", 1)]):
                    for q in range(2):
                        so = (q * 32 + 32 * sw) % 64
                        kb.dma("sp", gq[hf * 64 + q * 32:hf * 64 + q * 32 + 32, ci:ci + 1],
                               I[nm][l, so:so + 32].rearrange("(p o) -> p o", o=1), gq, True)
            for R_ in RS:
                for b_ in (R_["avo"], R_["cvo"], R_["dvo"]):
                    kb.op("dve", lambda e, b_=b_: e.memset(b_[:], 1.0), [], [b_])
            ci = 0
            for k in range(8):
                sg_ = stg[k % 2]
                kb.dma("sp", sg_[:], I["w_in"][l, k * 128:(k + 1) * 128, :], sg_, True)
                ops = []
                ops.append((WF[:, k, 0:512], sg_[:, 0:512]))
                for a in range(2):
                    ops.append((WF[:, k, 512:1024].rearrange("p (b h d) -> p b h d", h=2, d=16)[:, :, a, :],
                                sg_[:, 0:512].rearrange("p (b h d) -> p b h d", h=2, d=16)[:, :, 1 - a, :]))
                ops.append((WF[:, k, 1024:1280].rearrange("p (c s d) -> p c s d", c=2, s=2),
                            sg_[:, 2064:2320].rearrange("p (s c d) -> p c s d", c=2, s=2)))
                ops.append((WF[:, k, 1280:1408], sg_[:, 2320:2448]))
                for a in range(2):
                    ops.append((WF[:, k, 1408:1664].rearrange("p (c s h d) -> p c s h d", c=2, s=2, h=2)[:, :, :, a, :],
                                sg_[:, 2064:2320].rearrange("p (s c h d) -> p c s h d", c=2, s=2, h=2)[:, :, :, 1 - a, :]))
                    ops.append((WF[:, k, 1664:1792].rearrange("p (c h d) -> p c h d", c=2, h=2)[:, :, a, :],
                                sg_[:, 2320:2448].rearrange("p (c h d) -> p c h d", c=2, h=2)[:, :, 1 - a, :]))
                ops.append((WF[:, k, 1792:2304], sg_[:, 1024:1536]))
                ops.append((WT[:, k, 0:256], sg_[:, 512:768]))
                ops.append((WT[:, k, 256:512], sg_[:, 1536:1792]))
                ops.append((WT[:, k, 512:768], sg_[:, 1280:1536]))
                ops.append((WT[:, k, 768:1024], sg_[:, 1792:2048]))
                ops.append((WT[:, k, 1024:1280], sg_[:, 768:1024]))
                ops.append((WT[:, k, 1280:1408], sg_[:, 2448:2576]))
                ops.append((WT[:, k, 1408:1424], sg_[:, 2048:2064]))
                for oi, (o_, i_) in enumerate(ops):
                    en = ["dve", "pool"][ci % 2]
                    ci += 1
                    dstb = WF if oi < len(ops) - 7 else WT
                    kb.op(en, lambda e, o_=o_, i_=i_: e.tensor_copy(out=o_, in_=i_), [sg_], [dstb])
            tl = self.tiles(True)
            groups = [tl[i:i + 4] for i in range(0, len(tl), 4)]

            def run_chain(cid, R):
                hT, tmpc, rp = R["hT"], R["tmp"], R["rope"]
                t1_, t2_, sq_, rs_ = R["t1"], R["t2"], R["sq"], R["rs"]
                pf, px = R["pf"], R["px"]
                avo, cvo, dvo, cko, buo, cgo, cgt = R["avo"], R["cvo"], R["dvo"], R["cko"], R["buo"], R["cgo"], R["cgt"]
                xi = 0
                obi = 0
                for gi in range(cid, len(groups), 2):
                    grp = groups[gi]
                    r = grp[0][1]
                    row0g = grp[0][0]
                    nt = len(grp) * 128
                    for j_, nm in enumerate(["ropeA_c", "ropeA_s", "ropeD_c", "ropeD_s"]):
                        kb.dma("sp", rp[j_][:, 0:nt], I[nm][:, row0g:row0g + nt], rp[j_], True)
                    for j, (row0, _) in enumerate(grp):
                        xt = R["xsl"][xi % 2]
                        xi += 1
                        kb.dma("sp", xt[:], self.xs[row0:row0 + 128, :], xt, True)
                        self.ln_part(xt, tmpc["xn"], tmpc)
                        yield
                        self.tr_part(tmpc["xn"], hT, j * 128, r, c, tmpc["tp"], ident)
                        yield

                    def fmm(cc, p_):
                        for k in range(8):
                            kb.op("pe", lambda e, k=k: e.matmul(
                                p_[:, 0:nt], WF[:, k, cc * 128:(cc + 1) * 128], hT[:, k, 0:nt],
                                start=(k == 0), stop=(k == 7)), [WF, hT], [p_], inc=(k == 7))
                        return p_

                    for (c1, c2, dstT) in [(0, 4, self.AqT), (1, 5, self.AqT), (2, 6, self.AkT), (3, 7, self.AkT)]:
                        p1 = fmm(c1, pf[0])
                        p2 = fmm(c2, pf[1])
                        yield
                        o_ = R["ob"][obi % 2]
                        obi += 1
                        kb.op("dve", lambda e: e.tensor_tensor(out=t1_[:, 0:nt], in0=p1[:, 0:nt], in1=rp[0][:, 0:nt], op=ALU.mult),
                              [p1, rp[0]], [t1_])
                        kb.op("dve", lambda e: e.tensor_tensor(out=t2_[:, 0:nt], in0=p2[:, 0:nt], in1=rp[1][:, 0:nt], op=ALU.mult),
                              [p2, rp[1]], [t2_])
                        kb.op("dve", lambda e: e.tensor_tensor(out=o_[:, 0:nt], in0=t1_[:, 0:nt], in1=t2_[:, 0:nt], op=ALU.add),
                              [t1_, t2_], [o_])
                        rr = (c1 % 2) * 128
                        kb.dma("pool", dstT[rr:rr + 128, row0g:row0g + nt], o_[:, 0:nt], o_, False)
                    for (c1, c2, dstT, rr, gc) in [(8, 11, self.DqT, 0, 0), (9, 12, self.DqT, 128, 0), (10, 13, self.DkT, 0, 2)]:
                        p1 = fmm(c1, pf[0])
                        p2 = fmm(c2, pf[1])
                        yield
                        o_ = R["ob"][obi % 2]
                        obi += 1
                        kb.op("act", lambda e: e.activation(out=sq_[:, 0:nt], in_=p1[:, 0:nt], func=AF.Square), [p1], [sq_])
                        yield
                        kb.op("pe", lambda e: e.matmul(px[:, 0:nt], blk[:, :], sq_[:, 0:nt], start=True, stop=True), [blk, sq_], [px])
                        kb.op("dve", lambda e: e.scalar_tensor_tensor(
                            out=t1_[:, 0:nt], in0=p1[:, 0:nt], scalar=gq[:, gc:gc + 1], in1=rp[2][:, 0:nt], op0=ALU.mult, op1=ALU.mult),
                            [p1, gq, rp[2]], [t1_])
                        kb.op("dve", lambda e: e.scalar_tensor_tensor(
                            out=t2_[:, 0:nt], in0=p2[:, 0:nt], scalar=gq[:, gc + 1:gc + 2], in1=rp[3][:, 0:nt], op0=ALU.mult, op1=ALU.mult),
                            [p2, gq, rp[3]], [t2_])
                        yield
                        kb.op("act", lambda e: e.activation(out=rs_[:, 0:nt], in_=px[:, 0:nt], func=AF.Sqrt,
                                                            bias=epsc[:, 0:1], scale=1.0 / 64), [px, epsc], [rs_])
                        kb.op("dve", lambda e: e.tensor_tensor(out=t1_[:, 0:nt], in0=t1_[:, 0:nt], in1=t2_[:, 0:nt], op=ALU.add),
                              [t1_, t2_], [t1_])
                        yield
                        kb.op("dve", lambda e: e.reciprocal(out=rs_[:, 0:nt], in_=rs_[:, 0:nt]), [rs_], [rs_])
                        kb.op("dve", lambda e: e.tensor_tensor(out=o_[:, 0:nt], in0=t1_[:, 0:nt], in1=rs_[:, 0:nt], op=ALU.mult),
                              [t1_, rs_], [o_])
                        kb.dma("pool", dstT[rr:rr + 128, row0g:row0g + nt], o_[:, 0:nt], o_, False)
                    for ci_, (c1, dstT, rr) in enumerate([(14, self.CqT, 0), (15, self.CqT, 128), (16, self.CkT, 0), (17, self.CkT, 128)]):
                        p1 = fmm(c1, pf[ci_ % 2])
                        yield
                        o_ = R["ob"][obi % 2]
                        obi += 1
                        kb.op("act", lambda e: e.copy(out=o_[:, 0:nt], in_=p1[:, 0:nt]), [p1], [o_])
                        kb.dma("pool", dstT[rr:rr + 128, row0g:row0g + nt], o_[:, 0:nt], o_, False)
                    for j, (row0, _) in enumerate(grp):
                        for tc, (c0, w) in enumerate([(0, 512), (512, 512), (1024, 400)]):
                            p_ = [pf[0], pf[1], px][tc]
                            for k in range(8):
                                kb.op("pe", lambda e, k=k: e.matmul(
                                    p_[:, 0:w], hT[:, k, j * 128:(j + 1) * 128], WT[:, k, c0:c0 + w],
                                    start=(k == 0), stop=(k == 7)), [hT, WT], [p_], inc=(k == 7))
                            yield
                            if tc == 0:
                                kb.op("act", lambda e: e.copy(out=avo[:, :, 0:64], in_=p_[:, 0:256].rearrange("p (h d) -> p h d", h=4)),
                                      [p_], [avo])
                                kb.op("dve", lambda e: e.tensor_copy(out=cvo[:, :, 0:64], in_=p_[:, 256:512].rearrange("p (h d) -> p h d", h=4)),
                                      [p_], [cvo])
                                kb.dma("pool", self.Av[row0:row0 + 128, :, :], avo[:], avo, False)
                                kb.dma("pool", self.Cv[row0:row0 + 128, :, :], cvo[:], cvo, False)
                            elif tc == 1:
                                kb.op("dve", lambda e: e.tensor_copy(out=cko[:, 0:256], in_=p_[:, 0:256]), [p_], [cko])
                                kb.op("act", lambda e: e.activation(out=cko[:, 256:512], in_=p_[:, 256:512], func=AF.Sigmoid), [p_], [cko])
                                kb.dma("pool", self.Ck[row0:row0 + 128, :], cko[:, 0:256], cko, False)
                                kb.dma("pool", self.Co[row0:row0 + 128, :], cko[:, 256:512], cko, False)
                            else:
                                kb.op("dve", lambda e: e.tensor_copy(out=buo[:], in_=p_[:, 0:256]), [p_], [buo])
                                kb.op("act", lambda e: e.copy(out=dvo[:, :, 0:64], in_=p_[:, 256:384].rearrange("p (h d) -> p h d", h=2)),
                                      [p_], [dvo])
                                kb.op("dve", lambda e: e.tensor_tensor(out=cgo[:], in0=p_[:, 384:400], in1=gb[:], op=ALU.add), [p_, gb], [cgo])
                                kb.op("act", lambda e: e.activation(out=cgt[:], in_=cgo[:], func=AF.Exp, scale=-1.0), [cgo], [cgt])
                                kb.op("dve", lambda e: e.tensor_scalar_add(out=cgt[:], in0=cgt[:], scalar1=1.0), [cgt], [cgt])
                                kb.op("act", lambda e: e.activation(out=cgt[:], in_=cgt[:], func=AF.Ln), [cgt], [cgt])
                                for q in (4, 12):
                                    kb.op("dve", lambda e, q=q: e.tensor_scalar_mul(out=cgo[:, q:q + 4], in0=cgt[:, q:q + 4], scalar1=-1.0),
                                          [cgt, cgo], [cgo])
                                kb.dma("pool", self.Bu[row0:row0 + 128, :], buo[:], buo, False)
                                kb.dma("pool", self.Dv[row0:row0 + 128, :, :], dvo[:], dvo, False)
                                kb.dma("pool", self.Cg[row0:row0 + 128, :], cgo[:], cgo, False)

            alive = [run_chain(0, RS[0]), run_chain(1, RS[1])]
            while alive:
                for g_ in list(alive):
                    try:
                        next(g_)
                    except StopIteration:
                        alive.remove(g_)
        kb.barrier()

    def phase_attn(self, l, with_ctx_q):
        import math
        kb, nc, I = self.kb, self.nc, self.i
        L, LT, Lc = self.L, self.LT, self.Lc
        NCH = LT // 128
        lam_init = 0.8 - 0.6 * math.exp(-0.3 * l)
        with ExitStack() as cx:
            AkT = kb.sb(cx, "AkTs", [128, 2, LT], BF16)
            Avs = kb.sb(cx, "Avs", [128, NCH, 260], BF16)
            DkT = kb.sb(cx, "DkTs", [128, LT], BF16)
            Dvs = kb.sb(cx, "Dvs", [128, NCH, 130], BF16)
            Aq = [kb.sb(cx, "Aqg%d" % i, [128, 2, 512], BF16) for i in range(2)]
            Dq = [kb.sb(cx, "Dqg%d" % i, [128, 2, 512], BF16) for i in range(2)]
            pT = [kb.sb(cx, "pT%d" % i, [128, 512], BF16) for i in range(4)]
            dl = kb.sb(cx, "dl", [128, 128], F32)
            dlp = kb.sb(cx, "dlp", [128, 2, 32], F32)
            lamt = kb.sb(cx, "lamt", [128, 2], F32)
            neglam = kb.sb(cx, "neglam", [128, 1], F32)
            gA = kb.sb(cx, "gA", [128, 256], F32)
            epsc = kb.sb(cx, "epsc", [128, 1], F32)
            rec = [kb.sb(cx, "rec%d" % i, [128, 4], F32) for i in range(2)]
            am = [kb.sb(cx, "am%d" % i, [128, 4, 64], F32) for i in range(2)]
            dsq = kb.sb(cx, "dsq", [128, 4, 64], F32)
            ssq = kb.sb(cx, "ssq", [128, 4], F32)
            amix = [kb.sb(cx, "amix%d" % i, [128, 4, 256], BF16) for i in range(2)]
            dmix = [kb.sb(cx, "dmix%d" % i, [128, 4, 256], BF16) for i in range(2)]
            psS = [kb.ps(cx, "psS%d" % i, [128, 512], F32) for i in range(4)]
            po = [kb.ps(cx, "po%d" % i, [128, 512], F32) for i in range(4)]
            for c_ in range(2):
                kb.dma("sp", AkT[:, c_, :], self.AkT[c_ * 128:(c_ + 1) * 128, :], AkT, True)
            kb.dma("sp", DkT[:], self.DkT[:, :], DkT, True)
            avv = self.Av.rearrange("(c p) h e -> p c (h e)", p=128)
            dvv = self.Dv.rearrange("(c p) h e -> p c (h e)", p=128)
            for c0 in range(0, NCH, 8):
                c1 = min(NCH, c0 + 8)
                kb.dma("sp", Avs[:, c0:c1, :], avv[:, c0:c1, :], Avs, True)
                kb.dma("sp", Dvs[:, c0:c1, :], dvv[:, c0:c1, :], Dvs, True)
            kb.dma("sp", dl[:], _bc(I["diff_lambda"][l, :]), dl, True)
            kb.dma("sp", gA[:], _bc(I["diff_norm_g"][l, :]), gA, True)
            kb.op("dve", lambda e: e.memset(epsc[:], EPS), [], [epsc])
            kb.op("dve", lambda e: e.tensor_scalar_mul(out=gA[:], in0=gA[:], scalar1=1.0 - lam_init), [gA], [gA])
            dlv = dl[:].rearrange("p (a b) -> p a b", a=4)
            kb.op("dve", lambda e: e.tensor_tensor(out=dlp[:, 0, :], in0=dlv[:, 0, :], in1=dlv[:, 1, :], op=ALU.mult), [dl], [dlp])
            kb.op("dve", lambda e: e.tensor_tensor(out=dlp[:, 1, :], in0=dlv[:, 2, :], in1=dlv[:, 3, :], op=ALU.mult), [dl], [dlp])
            kb.op("dve", lambda e: e.reduce_sum(out=lamt[:], in_=dlp[:], axis=AX.X), [dlp], [lamt])
            kb.op("act", lambda e: e.activation(out=lamt[:], in_=lamt[:], func=AF.Exp), [lamt], [lamt])
            kb.op("dve", lambda e: e.tensor_tensor(out=neglam[:], in0=lamt[:, 1:2], in1=lamt[:, 0:1], op=ALU.subtract), [lamt], [neglam])
            kb.op("dve", lambda e: e.tensor_scalar_add(out=neglam[:], in0=neglam[:], scalar1=-lam_init), [neglam], [neglam])

            cnt = dict(s=0, o=0)

            def pair_sweep(specs, nq, kcs, scale):
                Os = [po[(cnt["o"] % 2) * 2], po[(cnt["o"] % 2) * 2 + 1]]
                cnt["o"] += 1
                n = len(kcs)
                Ss = [None] * n

                def emitS(idx):
                    sl = cnt["s"] % 2
                    cnt["s"] += 1
                    Ss[idx] = []
                    for a in range(2):
                        S, P = psS[sl * 2 + a], pT[sl * 2 + a]
                        kT_fn, q_ap, v_fn, kTb, qb, vb = specs[a]
                        lhsT, kw = kT_fn(kcs[idx])
                        kb.op("pe", lambda e: e.matmul(S[:, 0:nq * 128], lhsT, q_ap, start=True, stop=True, **kw), [kTb, qb], [S])
                        Ss[idx].append((S, P))
                emitS(0)
                for idx in range(n):
                    if idx + 1 < n:
                        emitS(idx + 1)
                    for a in range(2):
                        S, P = Ss[idx][a]
                        kb.op("act", lambda e: e.activation(out=P[:, 0:nq * 128], in_=S[:, 0:nq * 128], func=AF.Exp, scale=scale), [S], [P])
                    for a in range(2):
                        S, P = Ss[idx][a]
                        v_fn, vb = specs[a][2], specs[a][5]
                        for j in range(nq):
                            kb.op("pe", lambda e, j=j: e.matmul(
                                Os[a][:, j * 65:(j + 1) * 65], P[:, j * 128:(j + 1) * 128], v_fn(kcs[idx]),
                                start=(idx == 0 and j == 0), stop=(idx == n - 1), skip_group_check=True),
                                [P, vb], [Os[a]], inc=(j == nq - 1))
                return Os

            qgroups = [(q0, 4, list(range(NCH))) for q0 in range(0, L, 512)]
            if with_ctx_q:
                qgroups.append((L, Lc // 128, list(range(L // 128, NCH))))
            for gi, (q0, nq, kcs) in enumerate(qgroups):
                aq, dq = Aq[gi % 2], Dq[gi % 2]
                nqt = nq * 128
                for c_ in range(2):
                    kb.dma("sp", aq[:, c_, 0:nqt], self.AqT[c_ * 128:(c_ + 1) * 128, q0:q0 + nqt], aq, True)
                    kb.dma("sp", dq[:, c_, 0:nqt], self.DqT[c_ * 128:(c_ + 1) * 128, q0:q0 + nqt], dq, True)
                amx, dmx = amix[gi % 2], dmix[gi % 2]
                for h in range(4):
                    c_ = h // 2
                    specs = []
                    for m in range(2):
                        base = (h % 2) * 64 + m * 32
                        kw = dict(tile_position=(96, 0)) if base == 96 else {}
                        specs.append((lambda kc, base=base, kw=kw: (AkT[base:base + 32, c_, kc * 128:(kc + 1) * 128], kw),
                                      aq[base:base + 32, c_, 0:nqt], lambda kc: Avs[:, kc, h * 65:(h + 1) * 65], AkT, aq, Avs))
                    Os = pair_sweep(specs, nq, kcs, 32 ** -0.5)
                    for m in range(2):
                        O = Os[m]
                        Ov = O[:, 0:260].rearrange("p (j e) -> p j e", e=65)
                        r_, a_ = rec[m], am[m]
                        kb.op("dve", lambda e: e.reciprocal(out=r_[:, 0:nq], in_=Ov[:, 0:nq, 64]), [O], [r_])
                        kb.op("dve", lambda e: e.tensor_tensor(out=a_[:, 0:nq, :], in0=Ov[:, 0:nq, 0:64],
                                                              in1=r_[:, 0:nq].unsqueeze(2).to_broadcast([128, nq, 64]), op=ALU.mult),
                              [O, r_], [a_])
                    kb.op("dve", lambda e: e.scalar_tensor_tensor(out=am[0][:, 0:nq, :], in0=am[1][:, 0:nq, :], scalar=neglam[:, 0:1],
                                                                 in1=am[0][:, 0:nq, :], op0=ALU.mult, op1=ALU.add),
                          [am[0], am[1], neglam], [am[0]])
                    kb.op("dve", lambda e: e.tensor_tensor(out=dsq[:, 0:nq, :], in0=am[0][:, 0:nq, :], in1=am[0][:, 0:nq, :], op=ALU.mult),
                          [am[0]], [dsq])
                    kb.op("dve", lambda e: e.reduce_sum(out=ssq[:, 0:nq], in_=dsq[:, 0:nq, :], axis=AX.X), [dsq], [ssq])
                    kb.op("act", lambda e: e.activation(out=ssq[:, 0:nq], in_=ssq[:, 0:nq], func=AF.Sqrt, bias=epsc[:, 0:1], scale=1.0 / 64),
                          [ssq, epsc], [ssq])
                    kb.op("dve", lambda e: e.reciprocal(out=ssq[:, 0:nq], in_=ssq[:, 0:nq]), [ssq], [ssq])
                    kb.op("dve", lambda e: e.tensor_tensor(out=am[0][:, 0:nq, :], in0=am[0][:, 0:nq, :],
                                                          in1=ssq[:, 0:nq].unsqueeze(2).to_broadcast([128, nq, 64]), op=ALU.mult),
                          [am[0], ssq], [am[0]])
                    kb.op("dve", lambda e, h=h: e.tensor_tensor(out=amx[:, 0:nq, h * 64:(h + 1) * 64], in0=am[0][:, 0:nq, :],
                                                               in1=gA[:, h * 64:(h + 1) * 64].unsqueeze(1).to_broadcast([128, nq, 64]), op=ALU.mult),
                          [am[0], gA], [amx])
                for c_ in range(2):
                    heads = [c_, c_ + 2]
                    specs = []
                    for h in heads:
                        kv = h // 2
                        specs.append((lambda kc, kv=kv: (DkT[kv * 64:(kv + 1) * 64, kc * 128:(kc + 1) * 128], {}),
                                      dq[kv * 64:(kv + 1) * 64, c_, 0:nqt], lambda kc, kv=kv: Dvs[:, kc, kv * 65:(kv + 1) * 65], DkT, dq, Dvs))
                    Os = pair_sweep(specs, nq, kcs, 64 ** -0.5)
                    for a, h in enumerate(heads):
                        O = Os[a]
                        Ov = O[:, 0:260].rearrange("p (j e) -> p j e", e=65)
                        r_ = rec[a]
                        kb.op("dve", lambda e: e.reciprocal(out=r_[:, 0:nq], in_=Ov[:, 0:nq, 64]), [O], [r_])
                        kb.op("dve", lambda e, h=h: e.tensor_tensor(out=dmx[:, 0:nq, h * 64:(h + 1) * 64], in0=Ov[:, 0:nq, 0:64],
                                                                   in1=r_[:, 0:nq].unsqueeze(2).to_broadcast([128, nq, 64]), op=ALU.mult),
                              [O, r_], [dmx])
                for j in range(nq):
                    kb.dma("pool", self.mix[q0 + j * 128:q0 + (j + 1) * 128, 0:256], amx[:, j, :], amx, False)
                    kb.dma("pool", self.mix[q0 + j * 128:q0 + (j + 1) * 128, 768:1024], dmx[:, j, :], dmx, False)
        kb.barrier()


    def phase_pool(self, l, with_ctx):
        kb, nc, I = self.kb, self.nc, self.i
        L, LT = self.L, self.LT
        with ExitStack() as cx:
            PM = kb.sb(cx, "PM", [128, 20, 128], F32)
            pwf = kb.sb(cx, "pwf", [64, 4, 64], F32)
            pw = kb.sb(cx, "pw", [64, 4, 64], BF16)
            psc = kb.sb(cx, "psc", [128, 256], F32)
            ut = [kb.sb(cx, "ut%d" % i, [128, 256], F32) for i in range(4)]
            dT = [kb.sb(cx, "dT%d" % i, [64, 512], BF16) for i in range(2)]
            yo = [kb.sb(cx, "yo%d" % i, [128, 256], BF16) for i in range(2)]
            pd = [kb.ps(cx, "pd%d" % i, [128, 512], F32) for i in range(2)]
            py = [kb.ps(cx, "pyb%d" % i, [128, 512], F32) for i in range(2)]
            kb.dma("sp", PM[:], I["poolM"][:, :, :], PM, True)
            kb.dma("sp", pwf[:], I["pool_w"][l].rearrange("g c e -> c g e"), pwf, True)
            kb.dma("sp", psc[:], _bc(I["pool_scale"][l, :]), psc, True)
            kb.op("dve", lambda e: e.tensor_copy(out=pw[:], in_=pwf[:]), [pwf], [pw])
            seqs = [(0, L // 128)]
            if with_ctx:
                seqs.append((L, self.Lc // 128))
            ui = 0
            ti = 0
            for (r0, n) in seqs:
                tiles = {}

                def get(t):
                    nonlocal ui
                    if t not in tiles:
                        u = ut[ui % 4]
                        ui += 1
                        kb.dma("sp", u[:], self.Bu[r0 + t * 128:r0 + (t + 1) * 128, :], u, True)
                        tiles[t] = u
                    return tiles[t]
                for t in range(n):
                    srcs = []
                    if t > 0:
                        srcs.append((get(t - 1), 0))
                    srcs.append((get(t), 3 if t == 0 else (4 if t == n - 1 else 1)))
                    if t < n - 1:
                        srcs.append((get(t + 1), 2))
                    tiles.pop(t - 2, None)
                    p_d, p_y = pd[ti % 2], py[ti % 2]
                    d_, y_ = dT[ti % 2], yo[ti % 2]
                    ti += 1
                    first = True
                    for g in range(4):
                        for si, (u, v) in enumerate(srcs):
                            kb.op("pe", lambda e, g=g, u=u, v=v, first=first: e.matmul(
                                p_d[0:64, g * 128:(g + 1) * 128], u[:, g * 64:(g + 1) * 64], PM[:, v * 4 + g, :],
                                start=first, stop=(si == len(srcs) - 1), skip_group_check=True),
                                [u, PM], [p_d], inc=(g == 3 and si == len(srcs) - 1))
                            first = False
                    kb.op("act", lambda e: e.copy(out=d_[:, :], in_=p_d[0:64, :]), [p_d], [d_])
                    for g in range(4):
                        kb.op("pe", lambda e, g=g: e.matmul(p_y[:, g * 64:(g + 1) * 64], d_[:, g * 128:(g + 1) * 128], pw[:, g, :],
                                                            start=True, stop=True, skip_group_check=True), [d_, pw], [p_y], inc=(g == 3))
                    kb.op("dve", lambda e: e.tensor_tensor(out=y_[:], in0=p_y[:, 0:256], in1=psc[:], op=ALU.mult), [p_y, psc], [y_])
                    kb.dma("pool", self.mix[r0 + t * 128:r0 + (t + 1) * 128, 256:512], y_[:], y_, False)
        kb.barrier()

    def phase_wout(self, l, with_ctx):
        kb, nc, I = self.kb, self.nc, self.i
        with ExitStack() as cx:
            wo = kb.sb(cx, "wout", [128, 8, D], BF16)
            stg = [kb.sb(cx, "wostg%d" % i, [128, D], F32) for i in range(2)]
            gate = kb.sb(cx, "gate5", [128, D], F32)
            gln = kb.sb(cx, "gln", [128, D], F32)
            bln = kb.sb(cx, "bln", [128, D], F32)
            ident = kb.sb(cx, "ident", [128, 128], BF16)
            epsc = kb.sb(cx, "epsc", [128, 1], F32)
            mt = [kb.sb(cx, "mt%d" % i, [128, D], BF16) for i in range(2)]
            mT = [kb.sb(cx, "mT%d" % i, [128, 8, 128], BF16) for i in range(2)]
            xsl = [kb.sb(cx, "xsl%d" % i, [128, D], F32) for i in range(3)]
            zsl = [kb.sb(cx, "zsl%d" % i, [128, D], F32) for i in range(2)]
            tmp = dict(st2=kb.sb(cx, "st2", [128, 2, 6], F32), mv2=kb.sb(cx, "mv2", [128, 2], F32),
                       rstd2=kb.sb(cx, "rstd2", [128, 1], F32), epsc=epsc)
            tp = [kb.ps(cx, "tpw%d" % i, [128, 8, 128], BF16) for i in range(2)]
            py = [kb.ps(cx, "pyw%d" % i, [128, 512], F32) for i in range(4)]
            kb.dma("sp", ident[:], I["ident"][:, :], ident, True)
            kb.op("dve", lambda e: e.memset(epsc[:], EPS), [], [epsc])
            kb.dma("sp", gln[:], _bc(I["ln_g"][l, 1, :]), gln, True)
            kb.dma("sp", bln[:], _bc(I["ln_b"][l, 1, :]), bln, True)
            for k in range(8):
                sg_ = stg[k % 2]
                kb.dma("sp", sg_[:], I["w_out"][l, k * 128:(k + 1) * 128, :], sg_, True)
                kb.op(["dve", "pool"][k % 2], lambda e, k=k, sg_=sg_: e.tensor_copy(out=wo[:, k, :], in_=sg_[:]), [sg_], [wo])
            cur_r = None
            for ti, (row0, r) in enumerate(self.tiles(with_ctx)):
                if r != cur_r:
                    kb.dma("sp", gate[:], _bc(self.modd[l, r, 5 * D:6 * D]), gate, True)
                    cur_r = r
                m_, mT_, xt, z = mt[ti % 2], mT[ti % 2], xsl[ti % 3], zsl[ti % 2]
                tp_ = tp[ti % 2]
                kb.dma("sp", m_[:], self.mix[row0:row0 + 128, :], m_, True)
                kb.dma("sp", xt[:], self.xs[row0:row0 + 128, :], xt, True)
                for k in range(8):
                    kb.op("pe", lambda e, k=k: e.transpose(tp_[:, k, :], m_[:, k * 128:(k + 1) * 128], ident[:]),
                          [m_, ident], [tp_], inc=(k == 7))
                kb.op("act", lambda e: e.copy(out=mT_[:], in_=tp_[:]), [tp_], [mT_])
                for hh in range(2):
                    p_ = py[(ti % 2) * 2 + hh]
                    for k in range(8):
                        kb.op("pe", lambda e, k=k, hh=hh, p_=p_: e.matmul(p_[:, :], mT_[:, k, :], wo[:, k, hh * 512:(hh + 1) * 512],
                                                                     start=(k == 0), stop=(k == 7)), [mT_, wo], [p_], inc=(k == 7))
                    kb.op("dve", lambda e, hh=hh, p_=p_: e.tensor_tensor(out=z[:, hh * 512:(hh + 1) * 512], in0=p_[:, :],
                                                                        in1=gate[:, hh * 512:(hh + 1) * 512], op=ALU.mult), [p_, gate], [z])
                kb.op("dve", lambda e: e.scalar_tensor_tensor(out=z[:], in0=xt[:], scalar=ALPHA, in1=z[:], op0=ALU.mult, op1=ALU.add),
                      [xt, z], [z])
                self.ln_affine_out(z, xt, gln, bln, tmp)
                kb.dma("pool", self.xs[row0:row0 + 128, :], xt[:], xt, False)
        kb.barrier()

    def phase_mlstm(self, l, with_ctx_out):
        kb, nc, I = self.kb, self.nc, self.i
        L, LT = self.L, self.LT
        NCH = LT // 128
        nlat = L // 128
        nctx = self.Lc // 128
        HM = [0, 2, 1, 3]
        with ExitStack() as cx:
            Hb = kb.sb(cx, "Hb", [128, NCH, 256], F32)
            Hbc = [Buf(Hb.t, "Hb%d" % i) for i in range(NCH)]
            CM = kb.sb(cx, "CM", [128, 9, 128], F32)
            bmask = kb.sb(cx, "bmask", [128, 130], F32)
            ones = kb.sb(cx, "ones", [128, 1], F32)
            epsc = kb.sb(cx, "epsc", [128, 1], F32)
            gC = kb.sb(cx, "gC", [128, 256], F32)
            kb.dma("sp", CM[:], I["cmat"][:, :, :], CM, True)
            kb.dma("sp", bmask[:], I["blkmask"][:, :], bmask, True)
            kb.dma("sp", gC[:], _bc(I["ml_norm_g"][l, :]), gC, True)
            kb.op("dve", lambda e: e.memset(ones[:], 1.0), [], [ones])
            kb.op("dve", lambda e: e.memset(epsc[:], EPS), [], [epsc])
            fwd_order = [nlat + i for i in range(nctx)] + list(range(nlat))
            bwd_order = [nlat + i for i in reversed(range(nctx))] + list(reversed(range(nlat)))
            orders = [fwd_order, bwd_order]
            step_of = [{ch: t for t, ch in enumerate(o)} for o in orders]

            def mk(dr):
                n = "f" if dr == 0 else "b"
                R = dict(
                    Cst=[kb.sb(cx, "Cst%s%d" % (n, i), [128, 130], F32) for i in range(2)],
                    Cbf=[kb.sb(cx, "Cbf%s%d" % (n, i), [128, 130], BF16) for i in range(2)],
                    qT=[kb.sb(cx, "qT%s%d" % (n, i), [128, 2, 128], BF16) for i in range(2)],
                    kT=[kb.sb(cx, "kT%s%d" % (n, i), [128, 2, 128], BF16) for i in range(2)],
                    va=[kb.sb(cx, "va%s%d" % (n, i), [128, 260], BF16) for i in range(2)],
                    kt=[kb.sb(cx, "kt%s%d" % (n, i), [128, 256], F32) for i in range(2)],
                    cg=[kb.sb(cx, "cg%s%d" % (n, i), [128, 16], F32) for i in range(2)],
                    so=[kb.sb(cx, "so%s%d" % (n, i), [128, 256], F32) for i in range(2)],
                    outb=[kb.sb(cx, "outb%s%d" % (n, i), [128, 256], BF16) for i in range(2)],
                    lfrep=kb.sb(cx, "lfrep" + n, [128, 4, 128], F32), linm=kb.sb(cx, "linm" + n, [128, 4, 128], F32),
                    lfrep2=kb.sb(cx, "lfrep2" + n, [128, 256], F32), DT=kb.sb(cx, "DT" + n, [128, 512], F32),
                    AT=kb.sb(cx, "AT" + n, [128, 512], BF16), ew=kb.sb(cx, "ew" + n, [128, 8], F32),
                    INs=kb.sb(cx, "INs" + n, [128, 4, 65], F32), tot=kb.sb(cx, "tot" + n, [128, 4, 65], F32),
                    den=kb.sb(cx, "den" + n, [128, 4], F32), K2=kb.sb(cx, "K2" + n, [128, 256], BF16),
                    dec=kb.sb(cx, "dec" + n, [128, 2], F32), tmpu=kb.sb(cx, "tmpu" + n, [128, 130], F32),
                    hs=kb.sb(cx, "hs" + n, [128, 256], F32), sq=kb.sb(cx, "sq" + n, [128, 256], F32),
                    ssq=kb.sb(cx, "ssq" + n, [128, 4], F32),
                    P0=kb.ps(cx, "P0" + n, [128, 512], F32), P1=kb.ps(cx, "P1" + n, [128, 512], F32),
                    P2=kb.ps(cx, "P2" + n, [128, 512], F32), P3=kb.ps(cx, "P3" + n, [128, 512], F32),
                )
                return R

            def run_dir(dr, R):
                order = orders[dr]
                Cst, Cbf = R["Cst"], R["Cbf"]
                lfrep, linm, lfrep2, DT, AT, ew = R["lfrep"], R["linm"], R["lfrep2"], R["DT"], R["AT"], R["ew"]
                INs, tot, den, K2, dec, tmpu, hs, sq, ssq = R["INs"], R["tot"], R["den"], R["K2"], R["dec"], R["tmpu"], R["hs"], R["sq"], R["ssq"]
                P0, P1, P2, P3 = R["P0"], R["P1"], R["P2"], R["P3"]
                for pr in range(2):
                    kb.op("dve", lambda e, pr=pr: e.memset(Cst[pr][:], 0.0), [], [Cst[pr]])
                    kb.op("dve", lambda e, pr=pr: e.memset(Cbf[pr][:], 0.0), [], [Cbf[pr]])
                Tm, nTm, Ts, nm = (0, 1, 2, 3) if dr == 0 else (4, 5, 6, 7)
                li0, lf0 = (0, 4) if dr == 0 else (8, 12)
                for t, ch in enumerate(order):
                    b_ = t % 2
                    r0 = ch * 128
                    q_, k_, v_, kt_, cg_, so_ = R["qT"][b_], R["kT"][b_], R["va"][b_], R["kt"][b_], R["cg"][b_], R["so"][b_]
                    need_out = ch < nlat or with_ctx_out
                    second = step_of[1 - dr][ch] < t
                    for c_ in range(2):
                        kb.dma("sp", q_[:, c_, :], self.CqT[c_ * 128:(c_ + 1) * 128, r0:r0 + 128], q_, True)
                        kb.dma("sp", k_[:, c_, :], self.CkT[c_ * 128:(c_ + 1) * 128, r0:r0 + 128], k_, True)
                    kb.dma("sp", v_[:], self.Cv[r0:r0 + 128, :, :].rearrange("p h e -> p (h e)"), v_, True)
                    kb.dma("sp", kt_[:], self.Ck[r0:r0 + 128, :], kt_, True)
                    kb.dma("sp", cg_[:], self.Cg[r0:r0 + 128, :], cg_, True)
                    if need_out and second:
                        kb.dma("sp", so_[:], self.Co[r0:r0 + 128, :], so_, True)
                    lf = cg_[:, lf0:lf0 + 4]
                    li = cg_[:, li0:li0 + 4]
                    yield
                    kb.op("dve", lambda e: e.tensor_copy(out=lfrep[:], in_=lf.unsqueeze(2).to_broadcast([128, 4, 128])), [cg_], [lfrep])
                    kb.op("dve", lambda e: e.tensor_copy(out=lfrep2[:].rearrange("p (h e) -> p h e", e=64),
                                                        in_=lf.unsqueeze(2).to_broadcast([128, 4, 64])), [cg_], [lfrep2])
                    kb.op("dve", lambda e: e.tensor_tensor(out=linm[:], in0=CM[:, nm, :].unsqueeze(1).to_broadcast([128, 4, 128]),
                                                          in1=li.unsqueeze(2).to_broadcast([128, 4, 128]), op=ALU.add), [CM, cg_], [linm])
                    kb.op("pe", lambda e: e.matmul(P3[:, 0:4], CM[:, Tm, :], lf, start=True, stop=True, skip_group_check=True), [CM, cg_], [P3], inc=False)
                    kb.op("pe", lambda e: e.matmul(P3[:, 4:8], CM[:, Ts, :], lf, start=False, stop=False, skip_group_check=True), [CM, cg_], [P3], inc=False)
                    kb.op("pe", lambda e: e.matmul(P3[:, 4:8], CM[:, 8, :], li, start=False, stop=True, skip_group_check=True), [CM, cg_], [P3])
                    for bi, h in enumerate(HM):
                        pb = (h % 2) * 64
                        STx = P1 if pb == 0 else P2
                        kb.op("pe", lambda e, h=h, pb=pb, STx=STx, bi=bi: e.matmul(
                            STx[:, (bi % 2) * 128:(bi % 2 + 1) * 128], k_[pb:pb + 64, h // 2, :], q_[pb:pb + 64, h // 2, :],
                            start=True, stop=True, skip_group_check=True), [k_, q_], [STx], inc=(bi % 2 == 1))
                    yield
                    for bi, h in enumerate(HM):
                        o_ = P0[:, bi * 128:(bi + 1) * 128]
                        kb.op("pe", lambda e, h=h, o_=o_: e.matmul(o_, lfrep[:, h, :], CM[:, Tm, :], start=(bi == 0), stop=False, skip_group_check=True),
                              [lfrep, CM], [P0], inc=False)
                        kb.op("pe", lambda e, h=h, o_=o_: e.matmul(o_, CM[:, nTm, :], lfrep[:, h, :], start=False, stop=False, skip_group_check=True),
                              [lfrep, CM], [P0], inc=False)
                        kb.op("pe", lambda e, h=h, o_=o_: e.matmul(o_, CM[:, 8, :], linm[:, h, :], start=False, stop=True, skip_group_check=True),
                              [linm, CM], [P0], inc=(bi == 3))
                    kb.op("act", lambda e: e.activation(out=ew[:], in_=P3[:, 0:8], func=AF.Exp), [P3], [ew])
                    yield
                    kb.op("act", lambda e: e.activation(out=DT[:], in_=P0[:, :], func=AF.Exp), [P0], [DT])
                    kb.op("dve", lambda e: e.tensor_tensor(out=K2[:].rearrange("p (h e) -> p h e", e=64),
                                                          in0=kt_[:].rearrange("p (h e) -> p h e", e=64),
                                                          in1=ew[:, 4:8].unsqueeze(2).to_broadcast([128, 4, 64]), op=ALU.mult), [kt_, ew], [K2])
                    for pr in range(2):
                        kb.op("pe", lambda e, pr=pr: e.matmul(P3[:, 8 + pr:9 + pr], lfrep2[:, pr * 128:(pr + 1) * 128], ones[:, 0:1],
                                                              start=True, stop=True, skip_group_check=True), [lfrep2, ones], [P3], inc=(pr == 1))
                    yield
                    kb.op("act", lambda e: e.activation(out=dec[:], in_=P3[:, 8:10], func=AF.Exp), [P3], [dec])
                    for half, STx in enumerate([P1, P2]):
                        kb.op("dve", lambda e, half=half, STx=STx: e.scalar_tensor_tensor(
                            out=AT[:, half * 256:(half + 1) * 256], in0=STx[:, 0:256], scalar=0.125, in1=DT[:, half * 256:(half + 1) * 256],
                            op0=ALU.mult, op1=ALU.mult), [STx, DT], [AT])
                    yield
                    for bi, h in enumerate(HM):
                        kb.op("pe", lambda e, h=h, bi=bi: e.matmul(P0[:, h * 65:(h + 1) * 65], AT[:, bi * 128:(bi + 1) * 128], v_[:, h * 65:(h + 1) * 65],
                                                                   start=True, stop=True, skip_group_check=True), [AT, v_], [P0], inc=(bi == 3))
                    for pr in range(2):
                        kb.op("pe", lambda e, pr=pr: e.matmul(P1[:, pr * 130:(pr + 1) * 130], q_[:, pr, :], Cbf[pr][:, :],
                                                              start=True, stop=True, skip_group_check=True), [q_, Cbf[pr]], [P1], inc=(pr == 1))
                    kb.op("pe", lambda e: e.matmul(P2[:, 0:130], K2[:, 0:128], v_[:, 0:130], start=True, stop=True), [K2, v_], [P2])
                    kb.op("pe", lambda e: e.matmul(P3[:, 0:130], K2[:, 128:256], v_[:, 130:260], start=True, stop=True), [K2, v_], [P3])
                    yield
                    INv = P1[:, 0:260].rearrange("p (h e) -> p h e", e=65)
                    NDv = P0[:, 0:260].rearrange("p (h e) -> p h e", e=65)
                    kb.op("dve", lambda e: e.scalar_tensor_tensor(out=INs[:], in0=INv, scalar=0.125,
                                                                 in1=ew[:, 0:4].unsqueeze(2).to_broadcast([128, 4, 65]), op0=ALU.mult, op1=ALU.mult),
                          [P1, ew], [INs])
                    kb.op("dve", lambda e: e.tensor_tensor(out=tot[:], in0=NDv, in1=INs[:], op=ALU.add), [P0, INs], [tot])
                    for pr, Pu in enumerate([P2, P3]):
                        kb.op("dve", lambda e, pr=pr, Pu=Pu: e.tensor_tensor(out=tmpu[:], in0=Pu[:, 0:130], in1=bmask[:], op=ALU.mult),
                              [Pu, bmask], [tmpu])
                        kb.op("dve", lambda e, pr=pr: e.scalar_tensor_tensor(out=Cst[pr][:], in0=Cst[pr][:], scalar=dec[:, pr:pr + 1], in1=tmpu[:],
                                                                            op0=ALU.mult, op1=ALU.add), [Cst[pr], dec, tmpu], [Cst[pr]])
                        kb.op("act", lambda e, pr=pr: e.copy(out=Cbf[pr][:], in_=Cst[pr][:]), [Cst[pr]], [Cbf[pr]])
                    yield
                    kb.op("dve", lambda e: e.tensor_scalar_mul(out=den[:], in0=tot[:, :, 64], scalar1=-1.0), [tot], [den])
                    kb.op("dve", lambda e: e.scalar_tensor_tensor(out=den[:], in0=tot[:, :, 64], scalar=1.0, in1=den[:], op0=ALU.max, op1=ALU.max),
                          [tot, den], [den])
                    kb.op("dve", lambda e: e.reciprocal(out=den[:], in_=den[:]), [den], [den])
                    dbc = den[:].unsqueeze(2).to_broadcast([128, 4, 64])
                    Hc = Hbc[ch]
                    if not need_out:
                        continue
                    if not second:
                        hv = Hb[:, ch, :].rearrange("p (h e) -> p h e", e=64)
                        kb.op("dve", lambda e: e.tensor_tensor(out=hv, in0=tot[:, :, 0:64], in1=dbc, op=ALU.mult), [tot, den], [Hc])
                    else:
                        hsv = hs[:].rearrange("p (h e) -> p h e", e=64)
                        kb.op("dve", lambda e: e.tensor_tensor(out=hsv, in0=tot[:, :, 0:64], in1=dbc, op=ALU.mult), [tot, den], [hs])
                        kb.op("dve", lambda e: e.tensor_tensor(out=hs[:], in0=hs[:], in1=Hb[:, ch, :], op=ALU.add), [hs, Hc], [hs])
                        kb.op("dve", lambda e: e.tensor_tensor(out=sq[:], in0=hs[:], in1=hs[:], op=ALU.mult), [hs], [sq])
                        kb.op("dve", lambda e: e.reduce_sum(out=ssq[:], in_=sq[:].rearrange("p (h e) -> p h e", e=64), axis=AX.X), [sq], [ssq])
                        yield
                        kb.op("act", lambda e: e.activation(out=ssq[:], in_=ssq[:], func=AF.Sqrt, bias=epsc[:, 0:1], scale=1.0 / 64),
                              [ssq, epsc], [ssq])
                        yield
                        kb.op("dve", lambda e: e.reciprocal(out=ssq[:], in_=ssq[:]), [ssq], [ssq])
                        kb.op("dve", lambda e: e.tensor_tensor(out=hsv, in0=hsv, in1=ssq[:].unsqueeze(2).to_broadcast([128, 4, 64]), op=ALU.mult),
                              [hs, ssq], [hs])
                        kb.op("dve", lambda e: e.tensor_tensor(out=hs[:], in0=hs[:], in1=gC[:], op=ALU.mult), [hs, gC], [hs])
                        ob_ = R["outb"][b_]
                        kb.op("dve", lambda e: e.tensor_tensor(out=ob_[:], in0=hs[:], in1=so_[:], op=ALU.mult), [hs, so_], [ob_])
                        kb.dma("pool", self.mix[r0:r0 + 128, 512:768], ob_[:], ob_, False)

            gens = [run_dir(0, mk(0)), run_dir(1, mk(1))]
            alive = list(gens)
            while alive:
                for g in list(alive):
                    try:
                        next(g)
                    except StopIteration:
                        alive.remove(g)
        kb.barrier()

    def finish(self):
        self.kb.es.close()
        return self.nc


def host_consts(L=8192, Lc=256):
    LT = L + Lc
    c = {}
    c["ident"] = np.eye(128, dtype=np.float32).astype(ml_dtypes.bfloat16)
    t = np.arange(L)
    row = (t // 64).astype(np.float32)
    col = (t % 64).astype(np.float32)
    def tables(dim):
        axis_dim = dim // 2
        inv = (10000.0 ** (-np.arange(0, axis_dim, 2, dtype=np.float32) / axis_dim)).astype(np.float32)
        ang = np.concatenate([row[:, None] * inv, col[:, None] * inv], axis=-1).astype(np.float32)
        cos = np.cos(ang).astype(np.float32)
        sin = np.sin(ang).astype(np.float32)
        half = dim // 2
        C = np.ones((128, LT), np.float32)
        S = np.zeros((128, LT), np.float32)
        for p in range(128):
            d = p % dim
            C[p, :L] = cos[:, d % half]
            S[p, :L] = (-1.0 if d < half else 1.0) * sin[:, d % half]
        return C, S
    c["ropeA_c"], c["ropeA_s"] = tables(32)
    c["ropeD_c"], c["ropeD_s"] = tables(64)
    p = np.arange(128)
    c["blk64"] = (p[:, None] // 64 == p[None, :] // 64).astype(np.float32)
    PM = np.zeros((128, 20, 128), np.float32)
    for g, w in enumerate((2, 4, 8, 16)):
        h = w // 2
        for v in range(5):
            M = np.zeros((128, 128), np.float32)
            for t_ in range(128):
                lo, hi = t_ - h, t_ + h
                cnt = float(w)
                if v == 3:
                    lo = max(lo, 0); cnt = float(hi - lo)
                if v == 4:
                    hi = min(hi, 128); cnt = float(hi - lo)
                for tp_ in range(lo, hi):
                    if v in (1, 3, 4):
                        src = tp_
                    elif v == 0:
                        src = tp_ + 128
                    else:
                        src = tp_ - 128
                    if v == 3 and tp_ >= 128:
                        continue
                    if v == 4 and tp_ < 0:
                        continue
                    if 0 <= src < 128:
                        M[src, t_] += 1.0 / cnt
                if v in (1, 3, 4):
                    M[t_, t_] -= 1.0
            PM[:, v * 4 + g, :] = M
    c["poolM"] = PM
    s_ = np.arange(128)[:, None]
    j_ = np.arange(128)[None, :]
    CM = np.zeros((128, 9, 128), np.float32)
    CM[:, 0] = (s_ <= j_); CM[:, 1] = -(s_ <= j_).astype(np.float32); CM[:, 2] = (s_ > j_)
    CM[:, 3] = np.where(s_ <= j_, 0.0, -30000.0)
    CM[:, 4] = (s_ >= j_); CM[:, 5] = -(s_ >= j_).astype(np.float32); CM[:, 6] = (s_ < j_)
    CM[:, 7] = np.where(s_ >= j_, 0.0, -30000.0)
    CM[:, 8] = np.eye(128)
    c["cmat"] = CM
    bm = np.zeros((128, 130), np.float32)
    bm[0:64, 0:65] = 1.0
    bm[64:128, 65:130] = 1.0
    c["blkmask"] = bm
    return c


def build_program(L=8192, Lc=256, NL=DEPTH, dbg=()):
    P = Prog(L, Lc, NL, dbg=dbg)
    P.phase_mod()
    for l in range(NL):
        last = l == NL - 1
        if l == 0:
            src_lat, src_ctx = P.i["x"], P.i["ctx"]
        else:
            src_lat, src_ctx = P.xs[0:L, :], P.xs[L:L + Lc, :]
        P.phase_ffn(l, 0, src_lat, src_ctx, P.xs[0:L, :], P.xs[L:L + Lc, :], True)
        P.phase_proj(l)
        P.phase_attn(l, not last)
        P.phase_pool(l, not last)
        P.phase_mlstm(l, not last)
        P.phase_wout(l, not last)
        P.phase_ffn(l, 2, P.xs[0:L, :], P.xs[L:L + Lc, :], P.out if last else P.xs[0:L, :], P.xs[L:L + Lc, :], not last)
    P.finish()
    return P


def make_in_maps(inputs, L, Lc, ncores):
    hc = host_consts(L, Lc)
    shared = {}
    for k in ["w_ada", "b_ada", "ln_g", "ln_b", "ffn1_wi", "ffn1_wo", "ffn2_wi", "ffn2_wo", "w_in", "w_out",
              "diff_norm_g", "pool_w", "pool_scale", "ml_norm_g", "gqa_qnorm_g", "gqa_knorm_g"]:
        shared[k] = np.ascontiguousarray(np.asarray(inputs[k], dtype=np.float32))
    shared["diff_lambda"] = np.ascontiguousarray(np.asarray(inputs["diff_lambda"], dtype=np.float32).reshape(DEPTH, 128))
    shared["ml_gate_b"] = np.ascontiguousarray(np.asarray(inputs["ml_gate_b"], dtype=np.float32).reshape(DEPTH, 16))
    shared.update(hc)
    x = np.asarray(inputs["x"], dtype=np.float32)
    ctx = np.asarray(inputs["ctx"], dtype=np.float32)
    c = np.asarray(inputs["c"], dtype=np.float32)
    c_ctx = np.asarray(inputs["c_ctx"], dtype=np.float32)
    maps = []
    for b in range(ncores):
        cv = np.zeros((128, 2, 8), np.float32)
        cv[:, 0, :] = c[b].reshape(8, 128).T
        cv[:, 1, :] = c_ctx.reshape(8, 128).T
        m = dict(shared)
        m["x"] = np.ascontiguousarray(x[b, :L])
        m["ctx"] = np.ascontiguousarray(ctx[b, :Lc])
        m["cvec"] = cv
        maps.append(m)
    return maps


def kernel(**inputs):
    L, Lc, B = 8192, 256, 8
    P = build_program(L, Lc)
    maps = make_in_maps(inputs, L, Lc, B)
    res = run_bass_kernel_spmd(P.nc, maps, core_ids=list(range(B)))
    out = np.stack([np.asarray(res.results[b]["out"]) for b in range(B)], axis=0)
    return out.astype(np.float32)
```

```python
import numpy as np
import ml_dtypes
from contextlib import ExitStack
import concourse.bass as bass
import concourse.mybir as mybir
from concourse.bass_utils import run_bass_kernel_spmd

F32 = mybir.dt.float32
BF16 = mybir.dt.bfloat16
AF = mybir.ActivationFunctionType
ALU = mybir.AluOpType
AX = mybir.AxisListType

D = 1024
DFF = 2816
NFC = DFF // 128
NMOD = 9
EPS = 1e-6
DEPTH = 2
ALPHA = (2.0 * DEPTH) ** 0.25
INW = 2576
PIPE_FFN = False
HALF_PIPE_FFN = True


class Sem:
    def __init__(self, h):
        self.h = h
        self.count = 0


class Buf:
    def __init__(self, t, name):
        self.t = t
        self.name = name
        self.w = {}
        self.r = {}
        self.dsem = None
        self.psum = False

    def __getitem__(self, k):
        return self.t[k]


class KB:
    def __init__(self, nc):
        self.nc = nc
        self.es = ExitStack()
        self.eng = {}
        for name, h in [("pe", nc.tensor), ("act", nc.scalar), ("dve", nc.vector),
                        ("pool", nc.gpsimd), ("sp", nc.sync)]:
            sem = Sem(self.es.enter_context(nc.semaphore("s_" + name)))
            self.eng[name] = (h, sem)
        self.known = {name: {} for name in self.eng}
        self.bar = Sem(self.es.enter_context(nc.semaphore("s_bar")))
        self.dfree = [Sem(self.es.enter_context(nc.semaphore("s_d%d" % i))) for i in range(72)]
        self.dall = list(self.dfree)
        self.dused = []
        self.n_ins = 0
        self.uid = 0

    def sb(self, ctx, name, shape, dt):
        self.uid += 1
        name = "%s_%d" % (name, self.uid)
        t = ctx.enter_context(self.nc.sbuf_tensor(name, list(shape), dt))
        return Buf(t, name)

    def ps(self, ctx, name, shape, dt):
        self.uid += 1
        name = "%s_%d" % (name, self.uid)
        t = ctx.enter_context(self.nc.psum_tensor(name, list(shape), dt))
        b = Buf(t, name)
        b.psum = True
        return b

    def _deps(self, reads, writes):
        d = {}
        for b in reads:
            for sm, v in b.w.items():
                if d.get(sm, 0) < v:
                    d[sm] = v
        for b in writes:
            for sm, v in b.w.items():
                if d.get(sm, 0) < v:
                    d[sm] = v
            for sm, v in b.r.items():
                if d.get(sm, 0) < v:
                    d[sm] = v
        return d

    def _wait(self, ename, d):
        h, _ = self.eng[ename]
        kn = self.known[ename]
        for sm, v in d.items():
            if kn.get(sm, 0) >= v:
                continue
            assert v <= sm.count, "dependency on a not-yet-issued increment"
            h.wait_ge(sm.h, v)
            kn[sm] = v
            self.n_ins += 1

    def op(self, ename, fn, reads=(), writes=(), inc=True):
        h, sem = self.eng[ename]
        d = self._deps(reads, writes)
        for b in reads:
            if b.psum:
                for sm, v in b.r.items():
                    if sm is not sem and d.get(sm, 0) < v:
                        d[sm] = v
        if ename == "pe":
            d.pop(sem, None)
        self._wait(ename, d)
        ins = fn(h)
        self.n_ins += 1
        if inc:
            ins.then_inc(sem.h, 1)
            sem.count += 1
            tick = sem.count
        else:
            tick = sem.count + 1
        for b in writes:
            b.w = {sem: tick}
            b.r = {}
        for b in reads:
            if b.r.get(sem, 0) < tick:
                b.r[sem] = tick
        return ins

    def dma(self, q, out, in_, sb, load, **kw):
        h, _ = self.eng[q]
        if sb.dsem is None:
            sb.dsem = self.dfree.pop()
            self.dused.append(sb)
        ds = sb.dsem
        if load:
            d = self._deps((), (sb,))
            if not sb.r and set(sb.w.keys()) == {ds}:
                d.pop(ds, None)
        else:
            d = self._deps((sb,), ())
        self._wait(q, d)
        ins = h.dma_start(out=out, in_=in_, **kw)
        self.n_ins += 1
        ins.then_inc(ds.h, 16)
        ds.count += 16
        tick = ds.count
        if load:
            sb.w = {ds: tick}
            sb.r = {}
        else:
            sb.r[ds] = tick
        return ins

    def barrier(self):
        h, _ = self.eng["sp"]
        allsems = [s for (_, s) in self.eng.values()] + self.dall
        kn = self.known["sp"]
        for sm in allsems:
            if kn.get(sm, 0) < sm.count:
                h.wait_ge(sm.h, sm.count)
                kn[sm] = sm.count
        h.sem_inc(self.bar.h, 1)
        self.bar.count += 1
        for name, (eh, _) in self.eng.items():
            if name != "sp":
                eh.wait_ge(self.bar.h, self.bar.count)
            self.known[name] = {sm: sm.count for sm in allsems}
        for b in self.dused:
            self.dfree.append(b.dsem)
            b.dsem = None
        self.dused = []


def _bc(ap, n=128):
    return ap.partition_broadcast(n)


class Prog:
    def __init__(self, L, Lc, NL, dbg=()):
        self.L, self.Lc, self.NL = L, Lc, NL
        self.LT = L + Lc
        self.dbg = set(dbg)
        nc = bass.Bass("TRN2", target_bir_lowering=False)
        self.nc = nc
        self.kb = KB(nc)
        LT = self.LT

        def din(name, shape, dt=F32):
            return nc.dram_tensor(name, list(shape), dt, kind="ExternalInput").ap()

        def dsc(name, shape, dt=F32):
            kind = "ExternalOutput" if name in self.dbg else "Internal"
            return nc.dram_tensor(name, list(shape), dt, kind=kind).ap()

        self.i = dict(
            x=din("x", [L, D]), ctx=din("ctx", [Lc, D]), cvec=din("cvec", [128, 2, 8]),
            w_ada=din("w_ada", [NL, D, NMOD * D]), b_ada=din("b_ada", [NL, NMOD * D]),
            ln_g=din("ln_g", [NL, 3, D]), ln_b=din("ln_b", [NL, 3, D]),
            ffn1_wi=din("ffn1_wi", [NL, D, 2 * DFF]), ffn1_wo=din("ffn1_wo", [NL, DFF, D]),
            ffn2_wi=din("ffn2_wi", [NL, D, 2 * DFF]), ffn2_wo=din("ffn2_wo", [NL, DFF, D]),
            ident=din("ident", [128, 128], BF16),
        )
        self.i.update(
            w_in=din("w_in", [NL, D, INW]), w_out=din("w_out", [NL, D, D]),
            diff_lambda=din("diff_lambda", [NL, 128]), diff_norm_g=din("diff_norm_g", [NL, 256]),
            pool_w=din("pool_w", [NL, 4, 64, 64]), pool_scale=din("pool_scale", [NL, 256]),
            ml_gate_b=din("ml_gate_b", [NL, 16]), ml_norm_g=din("ml_norm_g", [NL, 256]),
            gqa_qnorm_g=din("gqa_qnorm_g", [NL, 64]), gqa_knorm_g=din("gqa_knorm_g", [NL, 64]),
            ropeA_c=din("ropeA_c", [128, LT]), ropeA_s=din("ropeA_s", [128, LT]),
            ropeD_c=din("ropeD_c", [128, LT]), ropeD_s=din("ropeD_s", [128, LT]),
            blk64=din("blk64", [128, 128]), poolM=din("poolM", [128, 20, 128]),
            cmat=din("cmat", [128, 9, 128]), blkmask=din("blkmask", [128, 130]),
        )
        self.out = nc.dram_tensor("out", [L, D], F32, kind="ExternalOutput").ap()
        self.xs = dsc("xs", [LT, D])
        self.modd = dsc("modd", [NL, 2, NMOD * D])
        self.AqT = dsc("AqT", [256, LT], BF16); self.AkT = dsc("AkT", [256, LT], BF16)
        self.Av = dsc("Av", [LT, 4, 65], BF16)
        self.DqT = dsc("DqT", [256, LT], BF16); self.DkT = dsc("DkT", [128, LT], BF16)
        self.Dv = dsc("Dv", [LT, 2, 65], BF16)
        self.CqT = dsc("CqT", [256, LT], BF16); self.CkT = dsc("CkT", [256, LT], BF16)
        self.Cv = dsc("Cv", [LT, 4, 65], BF16)
        self.Ck = dsc("Ck", [LT, 256]); self.Co = dsc("Co", [LT, 256]); self.Cg = dsc("Cg", [LT, 16])
        self.Bu = dsc("Bu", [LT, 256])
        self.mix = dsc("mix", [LT, D], BF16)

    def phase_mod(self):
        kb, nc, I = self.kb, self.nc, self.i
        NL = self.NL
        CG = 3072
        with ExitStack() as cx:
            cv = kb.sb(cx, "cv", [128, 2, 8], F32)
            sv = kb.sb(cx, "sv", [128, 2, 8], F32)
            wbuf = [kb.sb(cx, "wada%d" % i, [128, CG], F32) for i in range(3)]
            bb = kb.sb(cx, "bada", [2, CG], F32)
            ob = [kb.sb(cx, "modo%d" % i, [2, CG], F32) for i in range(2)]
            pss = [kb.ps(cx, "modps%d" % i, [128, 512], F32) for i in range(6)]
            kb.dma("sp", cv[:], I["cvec"][:, :, :], cv, True)
            kb.op("act", lambda e: e.activation(out=sv[:], in_=cv[:], func=AF.Sigmoid), [cv], [sv])
            kb.op("dve", lambda e: e.tensor_tensor(out=sv[:], in0=sv[:], in1=cv[:], op=ALU.mult), [sv, cv], [sv])
            it = 0
            for l in range(NL):
                for cg in range(NMOD * D // CG):
                    for k in range(8):
                        wb = wbuf[it % 3]
                        it += 1
                        kb.dma("sp", wb[:], I["w_ada"][l, k * 128:(k + 1) * 128, cg * CG:(cg + 1) * CG], wb, True)
                        for j in range(6):
                            kb.op("pe", lambda e, j=j, k=k, wb=wb: e.matmul(
                                pss[j][0:2, :], sv[:, :, k], wb[:, j * 512:(j + 1) * 512],
                                start=(k == 0), stop=(k == 7)), [sv, wb], [pss[j]], inc=(j == 5))
                    o = ob[(l * 3 + cg) % 2]
                    kb.dma("sp", bb[:], _bc(I["b_ada"][l, cg * CG:(cg + 1) * CG], 2), bb, True)
                    for j in range(6):
                        kb.op("dve", lambda e, j=j, o=o: e.tensor_tensor(
                            out=o[:, j * 512:(j + 1) * 512], in0=pss[j][0:2, :], in1=bb[:, j * 512:(j + 1) * 512],
                            op=ALU.add), [pss[j], bb], [o])
                    kb.dma("pool", self.modd[l, :, cg * CG:(cg + 1) * CG], o[:], o, False)
        kb.barrier()

    def load_ffn_consts(self, cx, l, s, si):
        kb, I = self.kb, self.i
        c = {}
        c["sc1"] = kb.sb(cx, "sc1", [128, 2, 8], F32)
        c["sh"] = kb.sb(cx, "sh", [128, 2, 8], F32)
        for r in range(2):
            src1 = self.modd[l, r, (3 * s + 1) * D:(3 * s + 2) * D].rearrange("(k p) -> p k", p=128)
            src0 = self.modd[l, r, (3 * s) * D:(3 * s + 1) * D].rearrange("(k p) -> p k", p=128)
            kb.dma("sp", c["sc1"][:, r, :], src1, c["sc1"], True, allow_slow_non_contiguous=True)
            kb.dma("sp", c["sh"][:, r, :], src0, c["sh"], True, allow_slow_non_contiguous=True)
        kb.op("dve", lambda e: e.tensor_scalar_add(out=c["sc1"][:], in0=c["sc1"][:], scalar1=1.0), [c["sc1"]], [c["sc1"]])
        return c

    def ln_part(self, xt, xn, tmp):
        kb = self.kb
        st, mv, rstd = tmp["st"], tmp["mv"], tmp["rstd"]
        for hh in range(2):
            kb.op("dve", lambda e, hh=hh: e.bn_stats(out=st[:, hh, :], in_=xt[:, hh * 512:(hh + 1) * 512]), [xt], [st])
        kb.op("dve", lambda e: e.bn_aggr(out=mv[:], in_=st[:]), [st], [mv])
        kb.op("act", lambda e: e.activation(out=rstd[:], in_=mv[:, 1:2], func=AF.Sqrt, bias=tmp["epsc"][:, 0:1], scale=1.0),
              [mv, tmp["epsc"]], [rstd])
        kb.op("dve", lambda e: e.reciprocal(out=rstd[:], in_=rstd[:]), [rstd], [rstd])
        kb.op("dve", lambda e: e.tensor_scalar(out=xn[:], in0=xt[:], scalar1=mv[:, 0:1], scalar2=rstd[:, 0:1],
                                              op0=ALU.subtract, op1=ALU.mult), [xt, mv, rstd], [xn])

    def tr_part(self, xn, hT, col0, r, c, tp, ident):
        kb = self.kb
        for k in range(8):
            kb.op("pe", lambda e, k=k: e.transpose(tp[:, k, :], xn[:, k * 128:(k + 1) * 128], ident[:]),
                  [xn, ident], [tp], inc=(k == 7))
        for k in range(8):
            kb.op("act", lambda e, k=k: e.activation(out=hT[:, k, col0:col0 + 128], in_=tp[:, k, :], func=AF.Identity,
                                                    scale=c["sc1"][:, r, k:k + 1], bias=c["sh"][:, r, k:k + 1]),
                  [tp, c["sc1"], c["sh"]], [hT])

    def ln_to_hT(self, xt, hT, col0, r, c, tmp):
        self.ln_part(xt, tmp["xn"], tmp)
        self.tr_part(tmp["xn"], hT, col0, r, c, tmp["tp"], tmp["ident"])

    def ln_affine_out(self, z, dst, g, b, tmp):
        kb = self.kb
        st, mv, rstd = tmp["st2"], tmp["mv2"], tmp["rstd2"]
        for hh in range(2):
            kb.op("dve", lambda e, hh=hh: e.bn_stats(out=st[:, hh, :], in_=z[:, hh * 512:(hh + 1) * 512]), [z], [st])
        kb.op("dve", lambda e: e.bn_aggr(out=mv[:], in_=st[:]), [st], [mv])
        kb.op("act", lambda e: e.activation(out=rstd[:], in_=mv[:, 1:2], func=AF.Sqrt, bias=tmp["epsc"][:, 0:1], scale=1.0),
              [mv, tmp["epsc"]], [rstd])
        kb.op("dve", lambda e: e.reciprocal(out=rstd[:], in_=rstd[:]), [rstd], [rstd])
        kb.op("dve", lambda e: e.tensor_scalar(out=z[:], in0=z[:], scalar1=mv[:, 0:1], scalar2=rstd[:, 0:1],
                                              op0=ALU.subtract, op1=ALU.mult), [z, mv, rstd], [z])
        kb.op("dve", lambda e: e.tensor_tensor(out=z[:], in0=z[:], in1=g[:], op=ALU.mult), [z, g], [z])
        kb.op("dve", lambda e: e.tensor_tensor(out=dst[:], in0=z[:], in1=b[:], op=ALU.add), [z, b], [dst])

    def tiles(self, with_ctx):
        t = [(i * 128, 0) for i in range(self.L // 128)]
        if with_ctx:
            t += [(self.L + i * 128, 1) for i in range(self.Lc // 128)]
        return t

    def ffn_weight_pieces(self, l, s, wi, wo):
        wi_d = self.i["ffn1_wi" if s == 0 else "ffn2_wi"]
        wo_d = self.i["ffn1_wo" if s == 0 else "ffn2_wo"]
        pieces = []
        for k in range(8):
            for c0 in range(0, 2 * DFF, 1024):
                w = min(1024, 2 * DFF - c0)
                pieces.append((wi_d[l, k * 128:(k + 1) * 128, c0:c0 + w], wi, (k, c0, w)))
        for fc in range(NFC):
            pieces.append((wo_d[l, fc * 128:(fc + 1) * 128, :], wo, (fc, 0, 1024)))
        return pieces

    def phase_ffn(self, l, s, src_lat, src_ctx, dst_lat, dst_ctx, with_ctx, pre=None):
        kb, nc, I = self.kb, self.nc, self.i
        L = self.L
        si = 0 if s == 0 else 2
        wi_d = I["ffn1_wi" if s == 0 else "ffn2_wi"]
        wo_d = I["ffn1_wo" if s == 0 else "ffn2_wo"]
        NT = 256

        def srcap(row0):
            return src_lat[row0:row0 + 128, :] if row0 < L else src_ctx[row0 - L:row0 - L + 128, :]

        def dstap(row0):
            return dst_lat[row0:row0 + 128, :] if row0 < L else dst_ctx[row0 - L:row0 - L + 128, :]

        with ExitStack() as cx:
            if pre is None:
                wi = kb.sb(cx, "wi", [128, 8, 2 * DFF], BF16)
                wo = kb.sb(cx, "wo", [128, NFC, D], BF16)
            else:
                wi, wo = pre
            c = self.load_ffn_consts(cx, l, s, si)
            gate = kb.sb(cx, "gate", [128, D], F32)
            gln = kb.sb(cx, "gln", [128, D], F32)
            bln = kb.sb(cx, "bln", [128, D], F32)
            xsl = [kb.sb(cx, "xsl%d" % i, [128, D], F32) for i in range(4)]
            zsl = [kb.sb(cx, "zsl%d" % i, [128, D], F32) for i in range(2)]
            hTs = [kb.sb(cx, "hT%d" % i, [128, 8, NT], BF16) for i in range(2)]
            xnb = [kb.sb(cx, "xnb%d" % i, [128, D], BF16) for i in range(NT // 128)]
            aT = kb.sb(cx, "aT", [128, NFC, NT], BF16)
            sg = [kb.sb(cx, "sg%d" % i, [128, NT], F32) for i in range(2)]
            tmp = dict(
                st=kb.sb(cx, "st", [128, 2, 6], F32), mv=kb.sb(cx, "mv", [128, 2], F32),
                rstd=kb.sb(cx, "rstd", [128, 1], F32),
                st2=kb.sb(cx, "st2", [128, 2, 6], F32), mv2=kb.sb(cx, "mv2", [128, 2], F32),
                rstd2=kb.sb(cx, "rstd2", [128, 1], F32),
                ident=kb.sb(cx, "ident", [128, 128], BF16),
                epsc=kb.sb(cx, "epsc", [128, 1], F32),
            )
            tps = [kb.ps(cx, "tp%d" % i, [128, 8, 128], BF16) for i in range(2)]
            pg = [kb.ps(cx, "pg%d" % i, [128, 512], F32) for i in range(2)]
            pu = [kb.ps(cx, "pu%d" % i, [128, 512], F32) for i in range(2)]
            py = [kb.ps(cx, "py%d" % i, [128, 512], F32) for i in range(2)]
            kb.dma("sp", tmp["ident"][:], I["ident"][:, :], tmp["ident"], True)
            kb.op("dve", lambda e: e.memset(tmp["epsc"][:], EPS), [], [tmp["epsc"]])
            kb.dma("sp", gln[:], _bc(I["ln_g"][l, si, :]), gln, True)
            kb.dma("sp", bln[:], _bc(I["ln_b"][l, si, :]), bln, True)
            stage = xsl + zsl
            pieces = self.ffn_weight_pieces(l, s, wi, wo) if pre is None else []
            cast_eng = ["dve", "act", "pool"]
            for n, (src, dstb, (a, c0, w)) in enumerate(pieces):
                sbuf = stage[n % len(stage)]
                kb.dma("sp", sbuf[:, 0:w], src, sbuf, True)
                en = cast_eng[n % 3]
                if en == "act":
                    kb.op("act", lambda e, sbuf=sbuf, dstb=dstb, a=a, c0=c0, w=w: e.copy(
                        out=dstb[:, a, c0:c0 + w], in_=sbuf[:, 0:w]), [sbuf], [dstb])
                else:
                    kb.op(en, lambda e, sbuf=sbuf, dstb=dstb, a=a, c0=c0, w=w: e.tensor_copy(
                        out=dstb[:, a, c0:c0 + w], in_=sbuf[:, 0:w]), [sbuf], [dstb])
            tl = self.tiles(with_ctx)
            groups = [tl[i:i + NT // 128] for i in range(0, len(tl), NT // 128)]
            cur_r = None
            xi = 0
            xts_of = {}

            def prep_ln(gi):
                nonlocal xi
                xts = []
                for j, (row0, _) in enumerate(groups[gi]):
                    xt = xsl[xi % 4]
                    xi += 1
                    xts.append(xt)
                    kb.dma("sp", xt[:], srcap(row0), xt, True)
                    self.ln_part(xt, xnb[j], tmp)
                xts_of[gi] = xts

            def prep_tr(gi):
                r_ = groups[gi][0][1]
                for j in range(len(groups[gi])):
                    self.tr_part(xnb[j], hTs[gi % 2], j * 128, r_, c, tps[j % 2], tmp["ident"])

            PIPE = PIPE_FFN
            if PIPE:
                prep_ln(0)
                prep_tr(0)
            for gi, grp in enumerate(groups):
                r = grp[0][1]
                assert all(t[1] == r for t in grp)
                hT = hTs[gi % 2]
                nt = len(grp) * 128
                if not PIPE:
                    if HALF_PIPE_FFN:
                        if gi == 0:
                            prep_ln(0)
                        prep_tr(gi)
                    else:
                        prep_ln(gi)
                        prep_tr(gi)
                xts = xts_of.pop(gi)
                if PIPE and gi + 1 < len(groups):
                    prep_ln(gi + 1)
                for fc in range(NFC):
                    g_ps, u_ps = pg[fc % 2], pu[fc % 2]
                    for k in range(8):
                        kb.op("pe", lambda e, k=k, fc=fc, g_ps=g_ps: e.matmul(
                            g_ps[:, 0:nt], wi[:, k, fc * 128:(fc + 1) * 128], hT[:, k, 0:nt],
                            start=(k == 0), stop=(k == 7)), [wi, hT], [g_ps], inc=False)
                    for k in range(8):
                        kb.op("pe", lambda e, k=k, fc=fc, u_ps=u_ps: e.matmul(
                            u_ps[:, 0:nt], wi[:, k, DFF + fc * 128:DFF + (fc + 1) * 128], hT[:, k, 0:nt],
                            start=(k == 0), stop=(k == 7)), [wi, hT], [u_ps], inc=(k == 7))
                    sgb = sg[fc % 2]
                    kb.op("act", lambda e, g_ps=g_ps, sgb=sgb: e.activation(out=sgb[:, 0:nt], in_=g_ps[:, 0:nt], func=AF.Silu),
                          [g_ps], [sgb])
                    kb.op("dve", lambda e, u_ps=u_ps, sgb=sgb, fc=fc: e.tensor_tensor(
                        out=aT[:, fc, 0:nt], in0=u_ps[:, 0:nt], in1=sgb[:, 0:nt], op=ALU.mult), [u_ps, sgb], [aT])
                if (not PIPE) and HALF_PIPE_FFN and gi + 1 < len(groups):
                    prep_ln(gi + 1)
                if PIPE and gi + 1 < len(groups):
                    prep_tr(gi + 1)
                if r != cur_r:
                    kb.dma("sp", gate[:], _bc(self.modd[l, r, (3 * s + 2) * D:(3 * s + 3) * D]), gate, True)
                    kb.op("dve", lambda e: e.tensor_scalar_mul(out=gate[:], in0=gate[:], scalar1=0.5), [gate], [gate])
                    cur_r = r
                for j, (row0, _) in enumerate(grp):
                    xt = xts[j]
                    z = zsl[j % 2]
                    for hh in range(2):
                        for fc in range(NFC):
                            kb.op("pe", lambda e, fc=fc, hh=hh, j=j: e.matmul(
                                py[hh][:, :], aT[:, fc, j * 128:(j + 1) * 128], wo[:, fc, hh * 512:(hh + 1) * 512],
                                start=(fc == 0), stop=(fc == NFC - 1)), [aT, wo], [py[hh]], inc=(fc == NFC - 1))
                    for hh in range(2):
                        kb.op("dve", lambda e, hh=hh, z=z: e.tensor_tensor(
                            out=z[:, hh * 512:(hh + 1) * 512], in0=py[hh][:, :], in1=gate[:, hh * 512:(hh + 1) * 512],
                            op=ALU.mult), [py[hh], gate], [z])
                    kb.op("dve", lambda e, z=z, xt=xt: e.scalar_tensor_tensor(
                        out=z[:], in0=xt[:], scalar=ALPHA, in1=z[:], op0=ALU.mult, op1=ALU.add), [xt, z], [z])
                    self.ln_affine_out(z, xt, gln, bln, tmp)
                    kb.dma("pool", dstap(row0), xt[:], xt, False)
        kb.barrier()


    def phase_proj(self, l):
        kb, nc, I = self.kb, self.nc, self.i
        L, LT = self.L, self.LT
        NF = 2304
        NTK = 1424
        with ExitStack() as cx:
            WF = kb.sb(cx, "WF", [128, 8, NF], BF16)
            WT = kb.sb(cx, "WT", [128, 8, NTK], BF16)
            c = self.load_ffn_consts(cx, l, 1, 1)
            stg = [kb.sb(cx, "wstg%d" % i, [128, INW], F32) for i in range(2)]
            ident = kb.sb(cx, "ident", [128, 128], BF16)
            epsc = kb.sb(cx, "epsc", [128, 1], F32)
            blk = kb.sb(cx, "blk64", [128, 128], F32)
            gq = kb.sb(cx, "gq", [128, 4], F32)
            gb = kb.sb(cx, "gateb", [128, 16], F32)

            def mk(n):
                R = dict(
                    xsl=[kb.sb(cx, "xsl%s%d" % (n, i), [128, D], F32) for i in range(2)],
                    hT=kb.sb(cx, "hT" + n, [128, 8, 512], BF16),
                    tmp=dict(st=kb.sb(cx, "st" + n, [128, 2, 6], F32), mv=kb.sb(cx, "mv" + n, [128, 2], F32),
                             rstd=kb.sb(cx, "rstd" + n, [128, 1], F32), xn=kb.sb(cx, "xn" + n, [128, D], BF16),
                             ident=ident, epsc=epsc, tp=kb.ps(cx, "tp" + n, [128, 8, 128], BF16)),
                    rope=[kb.sb(cx, "rope%s%d" % (n, j), [128, 512], F32) for j in range(4)],
                    t1=kb.sb(cx, "t1" + n, [128, 512], F32), t2=kb.sb(cx, "t2" + n, [128, 512], F32),
                    sq=kb.sb(cx, "sq" + n, [128, 512], F32), rs=kb.sb(cx, "rs" + n, [128, 512], F32),
                    ob=[kb.sb(cx, "ob%s%d" % (n, i), [128, 512], BF16) for i in range(2)],
                    avo=kb.sb(cx, "avo" + n, [128, 4, 65], BF16), cvo=kb.sb(cx, "cvo" + n, [128, 4, 65], BF16),
                    dvo=kb.sb(cx, "dvo" + n, [128, 2, 65], BF16), cko=kb.sb(cx, "cko" + n, [128, 512], F32),
                    buo=kb.sb(cx, "buo" + n, [128, 256], F32), cgo=kb.sb(cx, "cgo" + n, [128, 16], F32),
                    cgt=kb.sb(cx, "cgt" + n, [128, 16], F32),
                    pf=[kb.ps(cx, "pf%s%d" % (n, i), [128, 512], F32) for i in range(2)],
                    px=kb.ps(cx, "px" + n, [128, 512], F32),
                )
                return R
            RS = [mk("a"), mk("b")]
            tmp = RS[0]["tmp"]
            kb.dma("sp", tmp["ident"][:], I["ident"][:, :], tmp["ident"], True)
            kb.op("dve", lambda e: e.memset(tmp["epsc"][:], EPS), [], [tmp["epsc"]])
            kb.dma("sp", blk[:], I["blk64"][:, :], blk, True)
            kb.dma("sp", gb[:], _bc(I["ml_gate_b"][l, :]), gb, True)
            for hf in range(2):
                for ci, (nm, sw) in enumerate([("gqa_qnorm_g", 0), ("gqa_qnorm_g", 1), ("gqa_knorm_g", 0), ("gqa_knorm_g", 1)]):
                    for q in range(2):
                        so = (q * 32 + 32 * sw) % 64
                        kb.dma("sp", gq[hf * 64 + q * 32:hf * 64 + q * 32 + 32, ci:ci + 1],
                               I[nm][l, so:so + 32].rearrange("(p o) -> p o", o=1), gq, True)
            for R_ in RS:
                for b_ in (R_["avo"], R_["cvo"], R_["dvo"]):
                    kb.op("dve", lambda e, b_=b_: e.memset(b_[:], 1.0), [], [b_])
            ci = 0
            for k in range(8):
                sg_ = stg[k % 2]
                kb.dma("sp", sg_[:], I["w_in"][l, k * 128:(k + 1) * 128, :], sg_, True)
                ops = []
                ops.append((WF[:, k, 0:512], sg_[:, 0:512]))
                for a in range(2):
                    ops.append((WF[:, k, 512:1024].rearrange("p (b h d) -> p b h d", h=2, d=16)[:, :, a, :],
                                sg_[:, 0:512].rearrange("p (b h d) -> p b h d", h=2, d=16)[:, :, 1 - a, :]))
                ops.append((WF[:, k, 1024:1280].rearrange("p (c s d) -> p c s d", c=2, s=2),
                            sg_[:, 2064:2320].rearrange("p (s c d) -> p c s d", c=2, s=2)))
                ops.append((WF[:, k, 1280:1408], sg_[:, 2320:2448]))
                for a in range(2):
                    ops.append((WF[:, k, 1408:1664].rearrange("p (c s h d) -> p c s h d", c=2, s=2, h=2)[:, :, :, a, :],
                                sg_[:, 2064:2320].rearrange("p (s c h d) -> p c s h d", c=2, s=2, h=2)[:, :, :, 1 - a, :]))
                    ops.append((WF[:, k, 1664:1792].rearrange("p (c h d) -> p c h d", c=2, h=2)[:, :, a, :],
                                sg_[:, 2320:2448].rearrange("p (c h d) -> p c h d", c=2, h=2)[:, :, 1 - a, :]))
                ops.append((WF[:, k, 1792:2304], sg_[:, 1024:1536]))
                ops.append((WT[:, k, 0:256], sg_[:, 512:768]))
                ops.append((WT[:, k, 256:512], sg_[:, 1536:1792]))
                ops.append((WT[:, k, 512:768], sg_[:, 1280:1536]))
                ops.append((WT[:, k, 768:1024], sg_[:, 1792:2048]))
                ops.append((WT[:, k, 1024:1280], sg_[:, 768:1024]))
                ops.append((WT[:, k, 1280:1408], sg_[:, 2448:2576]))
                ops.append((WT[:, k, 1408:1424], sg_[:, 2048:2064]))
                for oi, (o_, i_) in enumerate(ops):
                    en = ["dve", "pool"][ci % 2]
                    ci += 1
                    dstb = WF if oi < len(ops) - 7 else WT
                    kb.op(en, lambda e, o_=o_, i_=i_: e.tensor_copy(out=o_, in_=i_), [sg_], [dstb])
            tl = self.tiles(True)
            groups = [tl[i:i + 4] for i in range(0, len(tl), 4)]

            def run_chain(cid, R):
                hT, tmpc, rp = R["hT"], R["tmp"], R["rope"]
                t1_, t2_, sq_, rs_ = R["t1"], R["t2"], R["sq"], R["rs"]
                pf, px = R["pf"], R["px"]
                avo, cvo, dvo, cko, buo, cgo, cgt = R["avo"], R["cvo"], R["dvo"], R["cko"], R["buo"], R["cgo"], R["cgt"]
                xi = 0
                obi = 0
                for gi in range(cid, len(groups), 2):
                    grp = groups[gi]
                    r = grp[0][1]
                    row0g = grp[0][0]
                    nt = len(grp) * 128
                    for j_, nm in enumerate(["ropeA_c", "ropeA_s", "ropeD_c", "ropeD_s"]):
                        kb.dma("sp", rp[j_][:, 0:nt], I[nm][:, row0g:row0g + nt], rp[j_], True)
                    for j, (row0, _) in enumerate(grp):
                        xt = R["xsl"][xi % 2]
                        xi += 1
                        kb.dma("sp", xt[:], self.xs[row0:row0 + 128, :], xt, True)
                        self.ln_part(xt, tmpc["xn"], tmpc)
                        yield
                        self.tr_part(tmpc["xn"], hT, j * 128, r, c, tmpc["tp"], ident)
                        yield

                    def fmm(cc, p_):
                        for k in range(8):
                            kb.op("pe", lambda e, k=k: e.matmul(
                                p_[:, 0:nt], WF[:, k, cc * 128:(cc + 1) * 128], hT[:, k, 0:nt],
                                start=(k == 0), stop=(k == 7)), [WF, hT], [p_], inc=(k == 7))
                        return p_

                    for (c1, c2, dstT) in [(0, 4, self.AqT), (1, 5, self.AqT), (2, 6, self.AkT), (3, 7, self.AkT)]:
                        p1 = fmm(c1, pf[0])
                        p2 = fmm(c2, pf[1])
                        yield
                        o_ = R["ob"][obi % 2]
                        obi += 1
                        kb.op("dve", lambda e: e.tensor_tensor(out=t1_[:, 0:nt], in0=p1[:, 0:nt], in1=rp[0][:, 0:nt], op=ALU.mult),
                              [p1, rp[0]], [t1_])
                        kb.op("dve", lambda e: e.tensor_tensor(out=t2_[:, 0:nt], in0=p2[:, 0:nt], in1=rp[1][:, 0:nt], op=ALU.mult),
                              [p2, rp[1]], [t2_])
                        kb.op("dve", lambda e: e.tensor_tensor(out=o_[:, 0:nt], in0=t1_[:, 0:nt], in1=t2_[:, 0:nt], op=ALU.add),
                              [t1_, t2_], [o_])
                        rr = (c1 % 2) * 128
                        kb.dma("pool", dstT[rr:rr + 128, row0g:row0g + nt], o_[:, 0:nt], o_, False)
                    for (c1, c2, dstT, rr, gc) in [(8, 11, self.DqT, 0, 0), (9, 12, self.DqT, 128, 0), (10, 13, self.DkT, 0, 2)]:
                        p1 = fmm(c1, pf[0])
                        p2 = fmm(c2, pf[1])
                        yield
                        o_ = R["ob"][obi % 2]
                        obi += 1
                        kb.op("act", lambda e: e.activation(out=sq_[:, 0:nt], in_=p1[:, 0:nt], func=AF.Square), [p1], [sq_])
                        yield
                        kb.op("pe", lambda e: e.matmul(px[:, 0:nt], blk[:, :], sq_[:, 0:nt], start=True, stop=True), [blk, sq_], [px])
                        kb.op("dve", lambda e: e.scalar_tensor_tensor(
                            out=t1_[:, 0:nt], in0=p1[:, 0:nt], scalar=gq[:, gc:gc + 1], in1=rp[2][:, 0:nt], op0=ALU.mult, op1=ALU.mult),
                            [p1, gq, rp[2]], [t1_])
                        kb.op("dve", lambda e: e.scalar_tensor_tensor(
                            out=t2_[:, 0:nt], in0=p2[:, 0:nt], scalar=gq[:, gc + 1:gc + 2], in1=rp[3][:, 0:nt], op0=ALU.mult, op1=ALU.mult),
                            [p2, gq, rp[3]], [t2_])
                        yield
                        kb.op("act", lambda e: e.activation(out=rs_[:, 0:nt], in_=px[:, 0:nt], func=AF.Sqrt,
                                                            bias=epsc[:, 0:1], scale=1.0 / 64), [px, epsc], [rs_])
                        kb.op("dve", lambda e: e.tensor_tensor(out=t1_[:, 0:nt], in0=t1_[:, 0:nt], in1=t2_[:, 0:nt], op=ALU.add),
                              [t1_, t2_], [t1_])
                        yield
                        kb.op("dve", lambda e: e.reciprocal(out=rs_[:, 0:nt], in_=rs_[:, 0:nt]), [rs_], [rs_])
                        kb.op("dve", lambda e: e.tensor_tensor(out=o_[:, 0:nt], in0=t1_[:, 0:nt], in1=rs_[:, 0:nt], op=ALU.mult),
                              [t1_, rs_], [o_])
                        kb.dma("pool", dstT[rr:rr + 128, row0g:row0g + nt], o_[:, 0:nt], o_, False)
                    for ci_, (c1, dstT, rr) in enumerate([(14, self.CqT, 0), (15, self.CqT, 128), (16, self.CkT, 0), (17, self.CkT, 128)]):
                        p1 = fmm(c1, pf[ci_ % 2])
                        yield
                        o_ = R["ob"][obi % 2]
                        obi += 1
                        kb.op("act", lambda e: e.copy(out=o_[:, 0:nt], in_=p1[:, 0:nt]), [p1], [o_])
                        kb.dma("pool", dstT[rr:rr + 128, row0g:row0g + nt], o_[:, 0:nt], o_, False)
                    for j, (row0, _) in enumerate(grp):
                        for tc, (c0, w) in enumerate([(0, 512), (512, 512), (1024, 400)]):
                            p_ = [pf[0], pf[1], px][tc]
                            for k in range(8):
                                kb.op("pe", lambda e, k=k: e.matmul(
                                    p_[:, 0:w], hT[:, k, j * 128:(j + 1) * 128], WT[:, k, c0:c0 + w],
                                    start=(k == 0), stop=(k == 7)), [hT, WT], [p_], inc=(k == 7))
                            yield
                            if tc == 0:
                                kb.op("act", lambda e: e.copy(out=avo[:, :, 0:64], in_=p_[:, 0:256].rearrange("p (h d) -> p h d", h=4)),
                                      [p_], [avo])
                                kb.op("dve", lambda e: e.tensor_copy(out=cvo[:, :, 0:64], in_=p_[:, 256:512].rearrange("p (h d) -> p h d", h=4)),
                                      [p_], [cvo])
                                kb.dma("pool", self.Av[row0:row0 + 128, :, :], avo[:], avo, False)
                                kb.dma("pool", self.Cv[row0:row0 + 128, :, :], cvo[:], cvo, False)
                            elif tc == 1:
                                kb.op("dve", lambda e: e.tensor_copy(out=cko[:, 0:256], in_=p_[:, 0:256]), [p_], [cko])
                                kb.op("act", lambda e: e.activation(out=cko[:, 256:512], in_=p_[:, 256:512], func=AF.Sigmoid), [p_], [cko])
                                kb.dma("pool", self.Ck[row0:row0 + 128, :], cko[:, 0:256], cko, False)
                                kb.dma("pool", self.Co[row0:row0 + 128, :], cko[:, 256:512], cko, False)
                            else:
                                kb.op("dve", lambda e: e.tensor_copy(out=buo[:], in_=p_[:, 0:256]), [p_], [buo])
                                kb.op("act", lambda e: e.copy(out=dvo[:, :, 0:64], in_=p_[:, 256:384].rearrange("p (h d) -> p h d", h=2)),
                                      [p_], [dvo])
                                kb.op("dve", lambda e: e.tensor_tensor(out=cgo[:], in0=p_[:, 384:400], in1=gb[:], op=ALU.add), [p_, gb], [cgo])
                                kb.op("act", lambda e: e.activation(out=cgt[:], in_=cgo[:], func=AF.Exp, scale=-1.0), [cgo], [cgt])
                                kb.op("dve", lambda e: e.tensor_scalar_add(out=cgt[:], in0=cgt[:], scalar1=1.0), [cgt], [cgt])
                                kb.op("act", lambda e: e.activation(out=cgt[:], in_=cgt[:], func=AF.Ln), [cgt], [cgt])
                                for q in (4, 12):
                                    kb.op("dve", lambda e, q=q: e.tensor_scalar_mul(out=cgo[:, q:q + 4], in0=cgt[:, q:q + 4], scalar1=-1.0),
                                          [cgt, cgo], [cgo])
                                kb.dma("pool", self.Bu[row0:row0 + 128, :], buo[:], buo, False)
                                kb.dma("pool", self.Dv[row0:row0 + 128, :, :], dvo[:], dvo, False)
                                kb.dma("pool", self.Cg[row0:row0 + 128, :], cgo[:], cgo, False)

            alive = [run_chain(0, RS[0]), run_chain(1, RS[1])]
            while alive:
                for g_ in list(alive):
                    try:
                        next(g_)
                    except StopIteration:
                        alive.remove(g_)
        kb.barrier()

    def phase_attn(self, l, with_ctx_q):
        import math
        kb, nc, I = self.kb, self.nc, self.i
        L, LT, Lc = self.L, self.LT, self.Lc
        NCH = LT // 128
        lam_init = 0.8 - 0.6 * math.exp(-0.3 * l)
        with ExitStack() as cx:
            AkT = kb.sb(cx, "AkTs", [128, 2, LT], BF16)
            Avs = kb.sb(cx, "Avs", [128, NCH, 260], BF16)
            DkT = kb.sb(cx, "DkTs", [128, LT], BF16)
            Dvs = kb.sb(cx, "Dvs", [128, NCH, 130], BF16)
            Aq = [kb.sb(cx, "Aqg%d" % i, [128, 2, 512], BF16) for i in range(2)]
            Dq = [kb.sb(cx, "Dqg%d" % i, [128, 2, 512], BF16) for i in range(2)]
            pT = [kb.sb(cx, "pT%d" % i, [128, 512], BF16) for i in range(4)]
            dl = kb.sb(cx, "dl", [128, 128], F32)
            dlp = kb.sb(cx, "dlp", [128, 2, 32], F32)
            lamt = kb.sb(cx, "lamt", [128, 2], F32)
            neglam = kb.sb(cx, "neglam", [128, 1], F32)
            gA = kb.sb(cx, "gA", [128, 256], F32)
            epsc = kb.sb(cx, "epsc", [128, 1], F32)
            rec = [kb.sb(cx, "rec%d" % i, [128, 4], F32) for i in range(2)]
            am = [kb.sb(cx, "am%d" % i, [128, 4, 64], F32) for i in range(2)]
            dsq = kb.sb(cx, "dsq", [128, 4, 64], F32)
            ssq = kb.sb(cx, "ssq", [128, 4], F32)
            amix = [kb.sb(cx, "amix%d" % i, [128, 4, 256], BF16) for i in range(2)]
            dmix = [kb.sb(cx, "dmix%d" % i, [128, 4, 256], BF16) for i in range(2)]
            psS = [kb.ps(cx, "psS%d" % i, [128, 512], F32) for i in range(4)]
            po = [kb.ps(cx, "po%d" % i, [128, 512], F32) for i in range(4)]
            for c_ in range(2):
                kb.dma("sp", AkT[:, c_, :], self.AkT[c_ * 128:(c_ + 1) * 128, :], AkT, True)
            kb.dma("sp", DkT[:], self.DkT[:, :], DkT, True)
            avv = self.Av.rearrange("(c p) h e -> p c (h e)", p=128)
            dvv = self.Dv.rearrange("(c p) h e -> p c (h e)", p=128)
            for c0 in range(0, NCH, 8):
                c1 = min(NCH, c0 + 8)
                kb.dma("sp", Avs[:, c0:c1, :], avv[:, c0:c1, :], Avs, True)
                kb.dma("sp", Dvs[:, c0:c1, :], dvv[:, c0:c1, :], Dvs, True)
            kb.dma("sp", dl[:], _bc(I["diff_lambda"][l, :]), dl, True)
            kb.dma("sp", gA[:], _bc(I["diff_norm_g"][l, :]), gA, True)
            kb.op("dve", lambda e: e.memset(epsc[:], EPS), [], [epsc])
            kb.op("dve", lambda e: e.tensor_scalar_mul(out=gA[:], in0=gA[:], scalar1=1.0 - lam_init), [gA], [gA])
            dlv = dl[:].rearrange("p (a b) -> p a b", a=4)
            kb.op("dve", lambda e: e.tensor_tensor(out=dlp[:, 0, :], in0=dlv[:, 0, :], in1=dlv[:, 1, :], op=ALU.mult), [dl], [dlp])
            kb.op("dve", lambda e: e.tensor_tensor(out=dlp[:, 1, :], in0=dlv[:, 2, :], in1=dlv[:, 3, :], op=ALU.mult), [dl], [dlp])
            kb.op("dve", lambda e: e.reduce_sum(out=lamt[:], in_=dlp[:], axis=AX.X), [dlp], [lamt])
            kb.op("act", lambda e: e.activation(out=lamt[:], in_=lamt[:], func=AF.Exp), [lamt], [lamt])
            kb.op("dve", lambda e: e.tensor_tensor(out=neglam[:], in0=lamt[:, 1:2], in1=lamt[:, 0:1], op=ALU.subtract), [lamt], [neglam])
            kb.op("dve", lambda e: e.tensor_scalar_add(out=neglam[:], in0=neglam[:], scalar1=-lam_init), [neglam], [neglam])

            cnt = dict(s=0, o=0)

            def pair_sweep(specs, nq, kcs, scale):
                Os = [po[(cnt["o"] % 2) * 2], po[(cnt["o"] % 2) * 2 + 1]]
                cnt["o"] += 1
                n = len(kcs)
                Ss = [None] * n

                def emitS(idx):
                    sl = cnt["s"] % 2
                    cnt["s"] += 1
                    Ss[idx] = []
                    for a in range(2):
                        S, P = psS[sl * 2 + a], pT[sl * 2 + a]
                        kT_fn, q_ap, v_fn, kTb, qb, vb = specs[a]
                        lhsT, kw = kT_fn(kcs[idx])
                        kb.op("pe", lambda e: e.matmul(S[:, 0:nq * 128], lhsT, q_ap, start=True, stop=True, **kw), [kTb, qb], [S])
                        Ss[idx].append((S, P))
                emitS(0)
                for idx in range(n):
                    if idx + 1 < n:
                        emitS(idx + 1)
                    for a in range(2):
                        S, P = Ss[idx][a]
                        kb.op("act", lambda e: e.activation(out=P[:, 0:nq * 128], in_=S[:, 0:nq * 128], func=AF.Exp, scale=scale), [S], [P])
                    for a in range(2):
                        S, P = Ss[idx][a]
                        v_fn, vb = specs[a][2], specs[a][5]
                        for j in range(nq):
                            kb.op("pe", lambda e, j=j: e.matmul(
                                Os[a][:, j * 65:(j + 1) * 65], P[:, j * 128:(j + 1) * 128], v_fn(kcs[idx]),
                                start=(idx == 0 and j == 0), stop=(idx == n - 1), skip_group_check=True),
                                [P, vb], [Os[a]], inc=(j == nq - 1))
                return Os

            qgroups = [(q0, 4, list(range(NCH))) for q0 in range(0, L, 512)]
            if with_ctx_q:
                qgroups.append((L, Lc // 128, list(range(L // 128, NCH))))
            for gi, (q0, nq, kcs) in enumerate(qgroups):
                aq, dq = Aq[gi % 2], Dq[gi % 2]
                nqt = nq * 128
                for c_ in range(2):
                    kb.dma("sp", aq[:, c_, 0:nqt], self.AqT[c_ * 128:(c_ + 1) * 128, q0:q0 + nqt], aq, True)
                    kb.dma("sp", dq[:, c_, 0:nqt], self.DqT[c_ * 128:(c_ + 1) * 128, q0:q0 + nqt], dq, True)
                amx, dmx = amix[gi % 2], dmix[gi % 2]
                for h in range(4):
                    c_ = h // 2
                    specs = []
                    for m in range(2):
                        base = (h % 2) * 64 + m * 32
                        kw = dict(tile_position=(96, 0)) if base == 96 else {}
                        specs.append((lambda kc, base=base, kw=kw: (AkT[base:base + 32, c_, kc * 128:(kc + 1) * 128], kw),
                                      aq[base:base + 32, c_, 0:nqt], lambda kc: Avs[:, kc, h * 65:(h + 1) * 65], AkT, aq, Avs))
                    Os = pair_sweep(specs, nq, kcs, 32 ** -0.5)
                    for m in range(2):
                        O = Os[m]
                        Ov = O[:, 0:260].rearrange("p (j e) -> p j e", e=65)
                        r_, a_ = rec[m], am[m]
                        kb.op("dve", lambda e: e.reciprocal(out=r_[:, 0:nq], in_=Ov[:, 0:nq, 64]), [O], [r_])
                        kb.op("dve", lambda e: e.tensor_tensor(out=a_[:, 0:nq, :], in0=Ov[:, 0:nq, 0:64],
                                                              in1=r_[:, 0:nq].unsqueeze(2).to_broadcast([128, nq, 64]), op=ALU.mult),
                              [O, r_], [a_])
                    kb.op("dve", lambda e: e.scalar_tensor_tensor(out=am[0][:, 0:nq, :], in0=am[1][:, 0:nq, :], scalar=neglam[:, 0:1],
                                                                 in1=am[0][:, 0:nq, :], op0=ALU.mult, op1=ALU.add),
                          [am[0], am[1], neglam], [am[0]])
                    kb.op("dve", lambda e: e.tensor_tensor(out=dsq[:, 0:nq, :], in0=am[0][:, 0:nq, :], in1=am[0][:, 0:nq, :], op=ALU.mult),
                          [am[0]], [dsq])
                    kb.op("dve", lambda e: e.reduce_sum(out=ssq[:, 0:nq], in_=dsq[:, 0:nq, :], axis=AX.X), [dsq], [ssq])
                    kb.op("act", lambda e: e.activation(out=ssq[:, 0:nq], in_=ssq[:, 0:nq], func=AF.Sqrt, bias=epsc[:, 0:1], scale=1.0 / 64),
                          [ssq, epsc], [ssq])
                    kb.op("dve", lambda e: e.reciprocal(out=ssq[:, 0:nq], in_=ssq[:, 0:nq]), [ssq], [ssq])
                    kb.op("dve", lambda e: e.tensor_tensor(out=am[0][:, 0:nq, :], in0=am[0][:, 0:nq, :],
                                                          in1=ssq[:, 0:nq].unsqueeze(2).to_broadcast([128, nq, 64]), op=ALU.mult),
                          [am[0], ssq], [am[0]])
                    kb.op("dve", lambda e, h=h: e.tensor_tensor(out=amx[:, 0:nq, h * 64:(h + 1) * 64], in0=am[0][:, 0:nq, :],
                                                               in1=gA[:, h * 64:(h + 1) * 64].unsqueeze(1).to_broadcast([128, nq, 64]), op=ALU.mult),
                          [am[0], gA], [amx])
                for c_ in range(2):
                    heads = [c_, c_ + 2]
                    specs = []
                    for h in heads:
                        kv = h // 2
                        specs.append((lambda kc, kv=kv: (DkT[kv * 64:(kv + 1) * 64, kc * 128:(kc + 1) * 128], {}),
                                      dq[kv * 64:(kv + 1) * 64, c_, 0:nqt], lambda kc, kv=kv: Dvs[:, kc, kv * 65:(kv + 1) * 65], DkT, dq, Dvs))
                    Os = pair_sweep(specs, nq, kcs, 64 ** -0.5)
                    for a, h in enumerate(heads):
                        O = Os[a]
                        Ov = O[:, 0:260].rearrange("p (j e) -> p j e", e=65)
                        r_ = rec[a]
                        kb.op("dve", lambda e: e.reciprocal(out=r_[:, 0:nq], in_=Ov[:, 0:nq, 64]), [O], [r_])
                        kb.op("dve", lambda e, h=h: e.tensor_tensor(out=dmx[:, 0:nq, h * 64:(h + 1) * 64], in0=Ov[:, 0:nq, 0:64],
                                                                   in1=r_[:, 0:nq].unsqueeze(2).to_broadcast([128, nq, 64]), op=ALU.mult),
                              [O, r_], [dmx])
                for j in range(nq):
                    kb.dma("pool", self.mix[q0 + j * 128:q0 + (j + 1) * 128, 0:256], amx[:, j, :], amx, False)
                    kb.dma("pool", self.mix[q0 + j * 128:q0 + (j + 1) * 128, 768:1024], dmx[:, j, :], dmx, False)
        kb.barrier()


    def phase_pool(self, l, with_ctx):
        kb, nc, I = self.kb, self.nc, self.i
        L, LT = self.L, self.LT
        with ExitStack() as cx:
            PM = kb.sb(cx, "PM", [128, 20, 128], F32)
            pwf = kb.sb(cx, "pwf", [64, 4, 64], F32)
            pw = kb.sb(cx, "pw", [64, 4, 64], BF16)
            psc = kb.sb(cx, "psc", [128, 256], F32)
            ut = [kb.sb(cx, "ut%d" % i, [128, 256], F32) for i in range(4)]
            dT = [kb.sb(cx, "dT%d" % i, [64, 512], BF16) for i in range(2)]
            yo = [kb.sb(cx, "yo%d" % i, [128, 256], BF16) for i in range(2)]
            pd = [kb.ps(cx, "pd%d" % i, [128, 512], F32) for i in range(2)]
            py = [kb.ps(cx, "pyb%d" % i, [128, 512], F32) for i in range(2)]
            kb.dma("sp", PM[:], I["poolM"][:, :, :], PM, True)
            kb.dma("sp", pwf[:], I["pool_w"][l].rearrange("g c e -> c g e"), pwf, True)
            kb.dma("sp", psc[:], _bc(I["pool_scale"][l, :]), psc, True)
            kb.op("dve", lambda e: e.tensor_copy(out=pw[:], in_=pwf[:]), [pwf], [pw])
            seqs = [(0, L // 128)]
            if with_ctx:
                seqs.append((L, self.Lc // 128))
            ui = 0
            ti = 0
            for (r0, n) in seqs:
                tiles = {}

                def get(t):
                    nonlocal ui
                    if t not in tiles:
                        u = ut[ui % 4]
                        ui += 1
                        kb.dma("sp", u[:], self.Bu[r0 + t * 128:r0 + (t + 1) * 128, :], u, True)
                        tiles[t] = u
                    return tiles[t]
                for t in range(n):
                    srcs = []
                    if t > 0:
                        srcs.append((get(t - 1), 0))
                    srcs.append((get(t), 3 if t == 0 else (4 if t == n - 1 else 1)))
                    if t < n - 1:
                        srcs.append((get(t + 1), 2))
                    tiles.pop(t - 2, None)
                    p_d, p_y = pd[ti % 2], py[ti % 2]
                    d_, y_ = dT[ti % 2], yo[ti % 2]
                    ti += 1
                    first = True
                    for g in range(4):
                        for si, (u, v) in enumerate(srcs):
                            kb.op("pe", lambda e, g=g, u=u, v=v, first=first: e.matmul(
                                p_d[0:64, g * 128:(g + 1) * 128], u[:, g * 64:(g + 1) * 64], PM[:, v * 4 + g, :],
                                start=first, stop=(si == len(srcs) - 1), skip_group_check=True),
                                [u, PM], [p_d], inc=(g == 3 and si == len(srcs) - 1))
                            first = False
                    kb.op("act", lambda e: e.copy(out=d_[:, :], in_=p_d[0:64, :]), [p_d], [d_])
                    for g in range(4):
                        kb.op("pe", lambda e, g=g: e.matmul(p_y[:, g * 64:(g + 1) * 64], d_[:, g * 128:(g + 1) * 128], pw[:, g, :],
                                                            start=True, stop=True, skip_group_check=True), [d_, pw], [p_y], inc=(g == 3))
                    kb.op("dve", lambda e: e.tensor_tensor(out=y_[:], in0=p_y[:, 0:256], in1=psc[:], op=ALU.mult), [p_y, psc], [y_])
                    kb.dma("pool", self.mix[r0 + t * 128:r0 + (t + 1) * 128, 256:512], y_[:], y_, False)
        kb.barrier()

    def phase_wout(self, l, with_ctx, prefetch=None):
        kb, nc, I = self.kb, self.nc, self.i
        with ExitStack() as cx:
            wo = kb.sb(cx, "wout", [128, 8, D], BF16)
            stg = [kb.sb(cx, "wostg%d" % i, [128, D], F32) for i in range(2)]
            gate = kb.sb(cx, "gate5", [128, D], F32)
            gln = kb.sb(cx, "gln", [128, D], F32)
            bln = kb.sb(cx, "bln", [128, D], F32)
            ident = kb.sb(cx, "ident", [128, 128], BF16)
            epsc = kb.sb(cx, "epsc", [128, 1], F32)
            mt = [kb.sb(cx, "mt%d" % i, [128, D], BF16) for i in range(2)]
            mT = [kb.sb(cx, "mT%d" % i, [128, 8, 128], BF16) for i in range(2)]
            xsl = [kb.sb(cx, "xsl%d" % i, [128, D], F32) for i in range(3)]
            zsl = [kb.sb(cx, "zsl%d" % i, [128, D], F32) for i in range(2)]
            tmp = dict(st2=kb.sb(cx, "st2", [128, 2, 6], F32), mv2=kb.sb(cx, "mv2", [128, 2], F32),
                       rstd2=kb.sb(cx, "rstd2", [128, 1], F32), epsc=epsc)
            tp = [kb.ps(cx, "tpw%d" % i, [128, 8, 128], BF16) for i in range(2)]
            py = [kb.ps(cx, "pyw%d" % i, [128, 512], F32) for i in range(4)]
            kb.dma("sp", ident[:], I["ident"][:, :], ident, True)
            kb.op("dve", lambda e: e.memset(epsc[:], EPS), [], [epsc])
            kb.dma("sp", gln[:], _bc(I["ln_g"][l, 1, :]), gln, True)
            kb.dma("sp", bln[:], _bc(I["ln_b"][l, 1, :]), bln, True)
            for k in range(8):
                sg_ = stg[k % 2]
                kb.dma("sp", sg_[:], I["w_out"][l, k * 128:(k + 1) * 128, :], sg_, True)
                kb.op(["dve", "pool"][k % 2], lambda e, k=k, sg_=sg_: e.tensor_copy(out=wo[:, k, :], in_=sg_[:]), [sg_], [wo])
            cur_r = None
            pf_pieces = self.ffn_weight_pieces(l, 2, prefetch[0], prefetch[1]) if prefetch is not None else []
            pf_n = 0
            ntile = len(self.tiles(with_ctx))

            def prefetch_some(upto):
                nonlocal pf_n
                while pf_n < min(upto, len(pf_pieces)):
                    src, dstb, (a, c0, w) = pf_pieces[pf_n]
                    sg_ = stg[pf_n % 2]
                    kb.dma("sp", sg_[:, 0:w], src, sg_, True)
                    kb.op("act", lambda e: e.copy(out=dstb[:, a, c0:c0 + w], in_=sg_[:, 0:w]), [sg_], [dstb])
                    pf_n += 1
            for ti, (row0, r) in enumerate(self.tiles(with_ctx)):
                prefetch_some(((ti + 1) * len(pf_pieces) + ntile - 1) // ntile)
                if r != cur_r:
                    kb.dma("sp", gate[:], _bc(self.modd[l, r, 5 * D:6 * D]), gate, True)
                    cur_r = r
                m_, mT_, xt, z = mt[ti % 2], mT[ti % 2], xsl[ti % 3], zsl[ti % 2]
                tp_ = tp[ti % 2]
                kb.dma("sp", m_[:], self.mix[row0:row0 + 128, :], m_, True)
                kb.dma("sp", xt[:], self.xs[row0:row0 + 128, :], xt, True)
                for k in range(8):
                    kb.op("pe", lambda e, k=k: e.transpose(tp_[:, k, :], m_[:, k * 128:(k + 1) * 128], ident[:]),
                          [m_, ident], [tp_], inc=(k == 7))
                kb.op("act", lambda e: e.copy(out=mT_[:], in_=tp_[:]), [tp_], [mT_])
                for hh in range(2):
                    p_ = py[(ti % 2) * 2 + hh]
                    for k in range(8):
                        kb.op("pe", lambda e, k=k, hh=hh, p_=p_: e.matmul(p_[:, :], mT_[:, k, :], wo[:, k, hh * 512:(hh + 1) * 512],
                                                                     start=(k == 0), stop=(k == 7)), [mT_, wo], [p_], inc=(k == 7))
                    kb.op("dve", lambda e, hh=hh, p_=p_: e.tensor_tensor(out=z[:, hh * 512:(hh + 1) * 512], in0=p_[:, :],
                                                                        in1=gate[:, hh * 512:(hh + 1) * 512], op=ALU.mult), [p_, gate], [z])
                kb.op("dve", lambda e: e.scalar_tensor_tensor(out=z[:], in0=xt[:], scalar=ALPHA, in1=z[:], op0=ALU.mult, op1=ALU.add),
                      [xt, z], [z])
                self.ln_affine_out(z, xt, gln, bln, tmp)
                kb.dma("pool", self.xs[row0:row0 + 128, :], xt[:], xt, False)
            prefetch_some(len(pf_pieces))
        kb.barrier()

    def phase_mlstm(self, l, with_ctx_out):
        kb, nc, I = self.kb, self.nc, self.i
        L, LT = self.L, self.LT
        NCH = LT // 128
        nlat = L // 128
        nctx = self.Lc // 128
        HM = [0, 2, 1, 3]
        with ExitStack() as cx:
            Hb = kb.sb(cx, "Hb", [128, NCH, 256], F32)
            Hbc = [Buf(Hb.t, "Hb%d" % i) for i in range(NCH)]
            CM = kb.sb(cx, "CM", [128, 9, 128], F32)
            bmask = kb.sb(cx, "bmask", [128, 130], F32)
            ones = kb.sb(cx, "ones", [128, 1], F32)
            epsc = kb.sb(cx, "epsc", [128, 1], F32)
            gC = kb.sb(cx, "gC", [128, 256], F32)
            kb.dma("sp", CM[:], I["cmat"][:, :, :], CM, True)
            kb.dma("sp", bmask[:], I["blkmask"][:, :], bmask, True)
            kb.dma("sp", gC[:], _bc(I["ml_norm_g"][l, :]), gC, True)
            kb.op("dve", lambda e: e.memset(ones[:], 1.0), [], [ones])
            kb.op("dve", lambda e: e.memset(epsc[:], EPS), [], [epsc])
            fwd_order = [nlat + i for i in range(nctx)] + list(range(nlat))
            bwd_order = [nlat + i for i in reversed(range(nctx))] + list(reversed(range(nlat)))
            orders = [fwd_order, bwd_order]
            step_of = [{ch: t for t, ch in enumerate(o)} for o in orders]

            def mk(dr):
                n = "f" if dr == 0 else "b"
                R = dict(
                    Cst=[kb.sb(cx, "Cst%s%d" % (n, i), [128, 130], F32) for i in range(2)],
                    Cbf=[kb.sb(cx, "Cbf%s%d" % (n, i), [128, 130], BF16) for i in range(2)],
                    qT=[kb.sb(cx, "qT%s%d" % (n, i), [128, 2, 128], BF16) for i in range(2)],
                    kT=[kb.sb(cx, "kT%s%d" % (n, i), [128, 2, 128], BF16) for i in range(2)],
                    va=[kb.sb(cx, "va%s%d" % (n, i), [128, 260], BF16) for i in range(2)],
                    kt=[kb.sb(cx, "kt%s%d" % (n, i), [128, 256], F32) for i in range(2)],
                    cg=[kb.sb(cx, "cg%s%d" % (n, i), [128, 16], F32) for i in range(2)],
                    so=[kb.sb(cx, "so%s%d" % (n, i), [128, 256], F32) for i in range(2)],
                    outb=[kb.sb(cx, "outb%s%d" % (n, i), [128, 256], BF16) for i in range(2)],
                    lfrep=kb.sb(cx, "lfrep" + n, [128, 4, 128], F32), linm=kb.sb(cx, "linm" + n, [128, 4, 128], F32),
                    lfrep2=kb.sb(cx, "lfrep2" + n, [128, 256], F32), DT=kb.sb(cx, "DT" + n, [128, 512], F32),
                    AT=kb.sb(cx, "AT" + n, [128, 512], BF16), ew=kb.sb(cx, "ew" + n, [128, 8], F32),
                    INs=kb.sb(cx, "INs" + n, [128, 4, 65], F32), tot=kb.sb(cx, "tot" + n, [128, 4, 65], F32),
                    den=kb.sb(cx, "den" + n, [128, 4], F32), K2=kb.sb(cx, "K2" + n, [128, 256], BF16),
                    dec=kb.sb(cx, "dec" + n, [128, 2], F32), tmpu=kb.sb(cx, "tmpu" + n, [128, 130], F32),
                    hs=kb.sb(cx, "hs" + n, [128, 256], F32), sq=kb.sb(cx, "sq" + n, [128, 256], F32),
                    ssq=kb.sb(cx, "ssq" + n, [128, 4], F32),
                    P0=kb.ps(cx, "P0" + n, [128, 512], F32), P1=kb.ps(cx, "P1" + n, [128, 512], F32),
                    P2=kb.ps(cx, "P2" + n, [128, 512], F32), P3=kb.ps(cx, "P3" + n, [128, 512], F32),
                )
                return R

            def run_dir(dr, R):
                order = orders[dr]
                Cst, Cbf = R["Cst"], R["Cbf"]
                lfrep, linm, lfrep2, DT, AT, ew = R["lfrep"], R["linm"], R["lfrep2"], R["DT"], R["AT"], R["ew"]
                INs, tot, den, K2, dec, tmpu, hs, sq, ssq = R["INs"], R["tot"], R["den"], R["K2"], R["dec"], R["tmpu"], R["hs"], R["sq"], R["ssq"]
                P0, P1, P2, P3 = R["P0"], R["P1"], R["P2"], R["P3"]
                for pr in range(2):
                    kb.op("dve", lambda e, pr=pr: e.memset(Cst[pr][:], 0.0), [], [Cst[pr]])
                    kb.op("dve", lambda e, pr=pr: e.memset(Cbf[pr][:], 0.0), [], [Cbf[pr]])
                Tm, nTm, Ts, nm = (0, 1, 2, 3) if dr == 0 else (4, 5, 6, 7)
                li0, lf0 = (0, 4) if dr == 0 else (8, 12)
                for t, ch in enumerate(order):
                    b_ = t % 2
                    r0 = ch * 128
                    q_, k_, v_, kt_, cg_, so_ = R["qT"][b_], R["kT"][b_], R["va"][b_], R["kt"][b_], R["cg"][b_], R["so"][b_]
                    need_out = ch < nlat or with_ctx_out
                    second = step_of[1 - dr][ch] < t
                    for c_ in range(2):
                        kb.dma("sp", q_[:, c_, :], self.CqT[c_ * 128:(c_ + 1) * 128, r0:r0 + 128], q_, True)
                        kb.dma("sp", k_[:, c_, :], self.CkT[c_ * 128:(c_ + 1) * 128, r0:r0 + 128], k_, True)
                    kb.dma("sp", v_[:], self.Cv[r0:r0 + 128, :, :].rearrange("p h e -> p (h e)"), v_, True)
                    kb.dma("sp", kt_[:], self.Ck[r0:r0 + 128, :], kt_, True)
                    kb.dma("sp", cg_[:], self.Cg[r0:r0 + 128, :], cg_, True)
                    if need_out and second:
                        kb.dma("sp", so_[:], self.Co[r0:r0 + 128, :], so_, True)
                    lf = cg_[:, lf0:lf0 + 4]
                    li = cg_[:, li0:li0 + 4]
                    yield
                    kb.op("dve", lambda e: e.tensor_copy(out=lfrep[:], in_=lf.unsqueeze(2).to_broadcast([128, 4, 128])), [cg_], [lfrep])
                    kb.op("dve", lambda e: e.tensor_copy(out=lfrep2[:].rearrange("p (h e) -> p h e", e=64),
                                                        in_=lf.unsqueeze(2).to_broadcast([128, 4, 64])), [cg_], [lfrep2])
                    kb.op("dve", lambda e: e.tensor_tensor(out=linm[:], in0=CM[:, nm, :].unsqueeze(1).to_broadcast([128, 4, 128]),
                                                          in1=li.unsqueeze(2).to_broadcast([128, 4, 128]), op=ALU.add), [CM, cg_], [linm])
                    kb.op("pe", lambda e: e.matmul(P3[:, 0:4], CM[:, Tm, :], lf, start=True, stop=True, skip_group_check=True), [CM, cg_], [P3], inc=False)
                    kb.op("pe", lambda e: e.matmul(P3[:, 4:8], CM[:, Ts, :], lf, start=False, stop=False, skip_group_check=True), [CM, cg_], [P3], inc=False)
                    kb.op("pe", lambda e: e.matmul(P3[:, 4:8], CM[:, 8, :], li, start=False, stop=True, skip_group_check=True), [CM, cg_], [P3])
                    for bi, h in enumerate(HM):
                        pb = (h % 2) * 64
                        STx = P1 if pb == 0 else P2
                        kb.op("pe", lambda e, h=h, pb=pb, STx=STx, bi=bi: e.matmul(
                            STx[:, (bi % 2) * 128:(bi % 2 + 1) * 128], k_[pb:pb + 64, h // 2, :], q_[pb:pb + 64, h // 2, :],
                            start=True, stop=True, skip_group_check=True), [k_, q_], [STx], inc=(bi % 2 == 1))
                    yield
                    for bi, h in enumerate(HM):
                        o_ = P0[:, bi * 128:(bi + 1) * 128]
                        kb.op("pe", lambda e, h=h, o_=o_: e.matmul(o_, lfrep[:, h, :], CM[:, Tm, :], start=(bi == 0), stop=False, skip_group_check=True),
                              [lfrep, CM], [P0], inc=False)
                        kb.op("pe", lambda e, h=h, o_=o_: e.matmul(o_, CM[:, nTm, :], lfrep[:, h, :], start=False, stop=False, skip_group_check=True),
                              [lfrep, CM], [P0], inc=False)
                        kb.op("pe", lambda e, h=h, o_=o_: e.matmul(o_, CM[:, 8, :], linm[:, h, :], start=False, stop=True, skip_group_check=True),
                              [linm, CM], [P0], inc=(bi == 3))
                    kb.op("act", lambda e: e.activation(out=ew[:], in_=P3[:, 0:8], func=AF.Exp), [P3], [ew])
                    yield
                    kb.op("act", lambda e: e.activation(out=DT[:], in_=P0[:, :], func=AF.Exp), [P0], [DT])
                    kb.op("dve", lambda e: e.tensor_tensor(out=K2[:].rearrange("p (h e) -> p h e", e=64),
                                                          in0=kt_[:].rearrange("p (h e) -> p h e", e=64),
                                                          in1=ew[:, 4:8].unsqueeze(2).to_broadcast([128, 4, 64]), op=ALU.mult), [kt_, ew], [K2])
                    for pr in range(2):
                        kb.op("pe", lambda e, pr=pr: e.matmul(P3[:, 8 + pr:9 + pr], lfrep2[:, pr * 128:(pr + 1) * 128], ones[:, 0:1],
                                                              start=True, stop=True, skip_group_check=True), [lfrep2, ones], [P3], inc=(pr == 1))
                    yield
                    kb.op("act", lambda e: e.activation(out=dec[:], in_=P3[:, 8:10], func=AF.Exp), [P3], [dec])
                    for half, STx in enumerate([P1, P2]):
                        kb.op("dve", lambda e, half=half, STx=STx: e.scalar_tensor_tensor(
                            out=AT[:, half * 256:(half + 1) * 256], in0=STx[:, 0:256], scalar=0.125, in1=DT[:, half * 256:(half + 1) * 256],
                            op0=ALU.mult, op1=ALU.mult), [STx, DT], [AT])
                    yield
                    for bi, h in enumerate(HM):
                        kb.op("pe", lambda e, h=h, bi=bi: e.matmul(P0[:, h * 65:(h + 1) * 65], AT[:, bi * 128:(bi + 1) * 128], v_[:, h * 65:(h + 1) * 65],
                                                                   start=True, stop=True, skip_group_check=True), [AT, v_], [P0], inc=(bi == 3))
                    for pr in range(2):
                        kb.op("pe", lambda e, pr=pr: e.matmul(P1[:, pr * 130:(pr + 1) * 130], q_[:, pr, :], Cbf[pr][:, :],
                                                              start=True, stop=True, skip_group_check=True), [q_, Cbf[pr]], [P1], inc=(pr == 1))
                    kb.op("pe", lambda e: e.matmul(P2[:, 0:130], K2[:, 0:128], v_[:, 0:130], start=True, stop=True), [K2, v_], [P2])
                    kb.op("pe", lambda e: e.matmul(P3[:, 0:130], K2[:, 128:256], v_[:, 130:260], start=True, stop=True), [K2, v_], [P3])
                    yield
                    INv = P1[:, 0:260].rearrange("p (h e) -> p h e", e=65)
                    NDv = P0[:, 0:260].rearrange("p (h e) -> p h e", e=65)
                    kb.op("dve", lambda e: e.scalar_tensor_tensor(out=INs[:], in0=INv, scalar=0.125,
                                                                 in1=ew[:, 0:4].unsqueeze(2).to_broadcast([128, 4, 65]), op0=ALU.mult, op1=ALU.mult),
                          [P1, ew], [INs])
                    kb.op("dve", lambda e: e.tensor_tensor(out=tot[:], in0=NDv, in1=INs[:], op=ALU.add), [P0, INs], [tot])
                    for pr, Pu in enumerate([P2, P3]):
                        kb.op("dve", lambda e, pr=pr, Pu=Pu: e.tensor_tensor(out=tmpu[:], in0=Pu[:, 0:130], in1=bmask[:], op=ALU.mult),
                              [Pu, bmask], [tmpu])
                        kb.op("dve", lambda e, pr=pr: e.scalar_tensor_tensor(out=Cst[pr][:], in0=Cst[pr][:], scalar=dec[:, pr:pr + 1], in1=tmpu[:],
                                                                            op0=ALU.mult, op1=ALU.add), [Cst[pr], dec, tmpu], [Cst[pr]])
                        kb.op("act", lambda e, pr=pr: e.copy(out=Cbf[pr][:], in_=Cst[pr][:]), [Cst[pr]], [Cbf[pr]])
                    yield
                    kb.op("dve", lambda e: e.tensor_scalar_mul(out=den[:], in0=tot[:, :, 64], scalar1=-1.0), [tot], [den])
                    kb.op("dve", lambda e: e.scalar_tensor_tensor(out=den[:], in0=tot[:, :, 64], scalar=1.0, in1=den[:], op0=ALU.max, op1=ALU.max),
                          [tot, den], [den])
                    kb.op("dve", lambda e: e.reciprocal(out=den[:], in_=den[:]), [den], [den])
                    dbc = den[:].unsqueeze(2).to_broadcast([128, 4, 64])
                    Hc = Hbc[ch]
                    if not need_out:
                        continue
                    if not second:
                        hv = Hb[:, ch, :].rearrange("p (h e) -> p h e", e=64)
                        kb.op("dve", lambda e: e.tensor_tensor(out=hv, in0=tot[:, :, 0:64], in1=dbc, op=ALU.mult), [tot, den], [Hc])
                    else:
                        hsv = hs[:].rearrange("p (h e) -> p h e", e=64)
                        kb.op("dve", lambda e: e.tensor_tensor(out=hsv, in0=tot[:, :, 0:64], in1=dbc, op=ALU.mult), [tot, den], [hs])
                        kb.op("dve", lambda e: e.tensor_tensor(out=hs[:], in0=hs[:], in1=Hb[:, ch, :], op=ALU.add), [hs, Hc], [hs])
                        kb.op("dve", lambda e: e.tensor_tensor(out=sq[:], in0=hs[:], in1=hs[:], op=ALU.mult), [hs], [sq])
                        kb.op("dve", lambda e: e.reduce_sum(out=ssq[:], in_=sq[:].rearrange("p (h e) -> p h e", e=64), axis=AX.X), [sq], [ssq])
                        yield
                        kb.op("act", lambda e: e.activation(out=ssq[:], in_=ssq[:], func=AF.Sqrt, bias=epsc[:, 0:1], scale=1.0 / 64),
                              [ssq, epsc], [ssq])
                        yield
                        kb.op("dve", lambda e: e.reciprocal(out=ssq[:], in_=ssq[:]), [ssq], [ssq])
                        kb.op("dve", lambda e: e.tensor_tensor(out=hsv, in0=hsv, in1=ssq[:].unsqueeze(2).to_broadcast([128, 4, 64]), op=ALU.mult),
                              [hs, ssq], [hs])
                        kb.op("dve", lambda e: e.tensor_tensor(out=hs[:], in0=hs[:], in1=gC[:], op=ALU.mult), [hs, gC], [hs])
                        ob_ = R["outb"][b_]
                        kb.op("dve", lambda e: e.tensor_tensor(out=ob_[:], in0=hs[:], in1=so_[:], op=ALU.mult), [hs, so_], [ob_])
                        kb.dma("pool", self.mix[r0:r0 + 128, 512:768], ob_[:], ob_, False)

            gens = [run_dir(0, mk(0)), run_dir(1, mk(1))]
            alive = list(gens)
            while alive:
                for g in list(alive):
                    try:
                        next(g)
                    except StopIteration:
                        alive.remove(g)
        kb.barrier()

    def finish(self):
        self.kb.es.close()
        return self.nc


def host_consts(L=8192, Lc=256):
    LT = L + Lc
    c = {}
    c["ident"] = np.eye(128, dtype=np.float32).astype(ml_dtypes.bfloat16)
    t = np.arange(L)
    row = (t // 64).astype(np.float32)
    col = (t % 64).astype(np.float32)
    def tables(dim):
        axis_dim = dim // 2
        inv = (10000.0 ** (-np.arange(0, axis_dim, 2, dtype=np.float32) / axis_dim)).astype(np.float32)
        ang = np.concatenate([row[:, None] * inv, col[:, None] * inv], axis=-1).astype(np.float32)
        cos = np.cos(ang).astype(np.float32)
        sin = np.sin(ang).astype(np.float32)
        half = dim // 2
        C = np.ones((128, LT), np.float32)
        S = np.zeros((128, LT), np.float32)
        for p in range(128):
            d = p % dim
            C[p, :L] = cos[:, d % half]
            S[p, :L] = (-1.0 if d < half else 1.0) * sin[:, d % half]
        return C, S
    c["ropeA_c"], c["ropeA_s"] = tables(32)
    c["ropeD_c"], c["ropeD_s"] = tables(64)
    p = np.arange(128)
    c["blk64"] = (p[:, None] // 64 == p[None, :] // 64).astype(np.float32)
    PM = np.zeros((128, 20, 128), np.float32)
    for g, w in enumerate((2, 4, 8, 16)):
        h = w // 2
        for v in range(5):
            M = np.zeros((128, 128), np.float32)
            for t_ in range(128):
                lo, hi = t_ - h, t_ + h
                cnt = float(w)
                if v == 3:
                    lo = max(lo, 0); cnt = float(hi - lo)
                if v == 4:
                    hi = min(hi, 128); cnt = float(hi - lo)
                for tp_ in range(lo, hi):
                    if v in (1, 3, 4):
                        src = tp_
                    elif v == 0:
                        src = tp_ + 128
                    else:
                        src = tp_ - 128
                    if v == 3 and tp_ >= 128:
                        continue
                    if v == 4 and tp_ < 0:
                        continue
                    if 0 <= src < 128:
                        M[src, t_] += 1.0 / cnt
                if v in (1, 3, 4):
                    M[t_, t_] -= 1.0
            PM[:, v * 4 + g, :] = M
    c["poolM"] = PM
    s_ = np.arange(128)[:, None]
    j_ = np.arange(128)[None, :]
    CM = np.zeros((128, 9, 128), np.float32)
    CM[:, 0] = (s_ <= j_); CM[:, 1] = -(s_ <= j_).astype(np.float32); CM[:, 2] = (s_ > j_)
    CM[:, 3] = np.where(s_ <= j_, 0.0, -30000.0)
    CM[:, 4] = (s_ >= j_); CM[:, 5] = -(s_ >= j_).astype(np.float32); CM[:, 6] = (s_ < j_)
    CM[:, 7] = np.where(s_ >= j_, 0.0, -30000.0)
    CM[:, 8] = np.eye(128)
    c["cmat"] = CM
    bm = np.zeros((128, 130), np.float32)
    bm[0:64, 0:65] = 1.0
    bm[64:128, 65:130] = 1.0
    c["blkmask"] = bm
    return c


def build_program(L=8192, Lc=256, NL=DEPTH, dbg=()):
    P = Prog(L, Lc, NL, dbg=dbg)
    P.phase_mod()
    for l in range(NL):
        last = l == NL - 1
        if l == 0:
            src_lat, src_ctx = P.i["x"], P.i["ctx"]
        else:
            src_lat, src_ctx = P.xs[0:L, :], P.xs[L:L + Lc, :]
        P.phase_ffn(l, 0, src_lat, src_ctx, P.xs[0:L, :], P.xs[L:L + Lc, :], True)
        P.phase_proj(l)
        P.phase_attn(l, not last)
        P.phase_pool(l, not last)
        P.phase_mlstm(l, not last)
        with ExitStack() as cxw:
            wi2 = P.kb.sb(cxw, "wi_pre", [128, 8, 2 * DFF], BF16)
            wo2 = P.kb.sb(cxw, "wo_pre", [128, NFC, D], BF16)
            P.phase_wout(l, not last, prefetch=(wi2, wo2))
            P.phase_ffn(l, 2, P.xs[0:L, :], P.xs[L:L + Lc, :], P.out if last else P.xs[0:L, :], P.xs[L:L + Lc, :], not last,
                        pre=(wi2, wo2))
    P.finish()
    return P


def make_in_maps(inputs, L, Lc, ncores):
    hc = host_consts(L, Lc)
    shared = {}
    for k in ["w_ada", "b_ada", "ln_g", "ln_b", "ffn1_wi", "ffn1_wo", "ffn2_wi", "ffn2_wo", "w_in", "w_out",
              "diff_norm_g", "pool_w", "pool_scale", "ml_norm_g", "gqa_qnorm_g", "gqa_knorm_g"]:
        shared[k] = np.ascontiguousarray(np.asarray(inputs[k], dtype=np.float32))
    shared["diff_lambda"] = np.ascontiguousarray(np.asarray(inputs["diff_lambda"], dtype=np.float32).reshape(DEPTH, 128))
    shared["ml_gate_b"] = np.ascontiguousarray(np.asarray(inputs["ml_gate_b"], dtype=np.float32).reshape(DEPTH, 16))
    shared.update(hc)
    x = np.asarray(inputs["x"], dtype=np.float32)
    ctx = np.asarray(inputs["ctx"], dtype=np.float32)
    c = np.asarray(inputs["c"], dtype=np.float32)
    c_ctx = np.asarray(inputs["c_ctx"], dtype=np.float32)
    maps = []
    for b in range(ncores):
        cv = np.zeros((128, 2, 8), np.float32)
        cv[:, 0, :] = c[b].reshape(8, 128).T
        cv[:, 1, :] = c_ctx.reshape(8, 128).T
        m = dict(shared)
        m["x"] = np.ascontiguousarray(x[b, :L])
        m["ctx"] = np.ascontiguousarray(ctx[b, :Lc])
        m["cvec"] = cv
        maps.append(m)
    return maps


def kernel(**inputs):
    L, Lc, B = 8192, 256, 8
    P = build_program(L, Lc)
    maps = make_in_maps(inputs, L, Lc, B)
    res = run_bass_kernel_spmd(P.nc, maps, core_ids=list(range(B)))
    out = np.stack([np.asarray(res.results[b]["out"]) for b in range(B)], axis=0)
    return out.astype(np.float32)
```
